# Optimizing a Trainium2 kernel written in Bass

```python
import math
import jax, jax.numpy as jnp
from jax import lax
import numpy as np

D_MODEL = 1024
BATCH = 2
SEQ = 8192
DEPTH = 1
DEC_BATCH = 128
DEC_SEQ = 4
PAST_LEN = 2048
PAGE_SIZE = 128

D_ATTN = D_MODEL // 2
D_SSM = D_MODEL - D_ATTN
QK_DIM = 64
V_DIM = 2 * QK_DIM
N_HEADS = D_ATTN // V_DIM
N_BUCKETS = 32
MAX_DISTANCE = 128
SSM_GROUP = 16
N_SSM_GROUPS = D_SSM // SSM_GROUP
SSM_STATE = 64
Q_BLOCK = 128
EPS = 1e-6
NEG_INF = -1e30
QK_W = N_HEADS * 2 * QK_DIM
SPLITS = (QK_W, 2 * QK_W, 2 * QK_W + N_HEADS * V_DIM,
          2 * QK_W + N_HEADS * V_DIM + D_ATTN,
          2 * QK_W + N_HEADS * V_DIM + D_ATTN + D_SSM)
D_IN_PROJ = 2 * QK_W + N_HEADS * V_DIM + D_ATTN + 2 * D_SSM

kernel_name = "hymba_diffattn_s5_step"


def _rmsnorm(x, g):
    xf = x.astype(jnp.float32)
    return xf * lax.rsqrt(jnp.mean(xf * xf, axis=-1, keepdims=True) + EPS) * g.astype(jnp.float32)


def _lambda_init(layer):
    return 0.8 - 0.6 * math.exp(-0.3 * layer)


def _rel_bucket(q_pos, k_pos):
    n = jnp.maximum(q_pos[:, None] - k_pos[None, :], 0)
    max_exact = N_BUCKETS // 2
    nf = jnp.maximum(n, 1).astype(jnp.float32)
    large = max_exact + (jnp.log(nf / max_exact) / math.log(MAX_DISTANCE / max_exact)
                         * (N_BUCKETS - max_exact)).astype(jnp.int32)
    large = jnp.minimum(large, N_BUCKETS - 1)
    return jnp.where(n < max_exact, n, large)


def _diff_attend(q, k, v, q_pos, k_pos, rel_bias, lam):
    s = jnp.einsum('bqhcd,bkhcd->bhcqk', q, k) * (QK_DIM ** -0.5)
    bias = rel_bias.astype(jnp.float32)[_rel_bucket(q_pos, k_pos)]
    s = s + jnp.transpose(bias, (2, 0, 1))[None, :, None]
    mask = k_pos[None, :] <= q_pos[:, None]
    s = jnp.where(mask, s, NEG_INF)
    p = jax.nn.softmax(s, axis=-1)
    w = p[:, :, 0] - lam * p[:, :, 1]
    return jnp.einsum('bhqk,bkhe->bqhe', w, v)


def _in_proj(xn, w_in, q_g, k_g):
    b, L, _ = xn.shape
    h = xn @ w_in.astype(jnp.float32)
    q, k, v, ga, u, gs = jnp.split(h, SPLITS, axis=-1)
    q = _rmsnorm(q.reshape(b, L, N_HEADS, 2, QK_DIM), q_g)
    k = _rmsnorm(k.reshape(b, L, N_HEADS, 2, QK_DIM), k_g)
    v = v.reshape(b, L, N_HEADS, V_DIM)
    return q, k, v, ga, u, gs


def _attn_out(o, ga, subln_g, lam_init):
    b, L = o.shape[:2]
    o = _rmsnorm(o, subln_g) * (1.0 - lam_init)
    return o.reshape(b, L, D_ATTN) * jax.nn.silu(ga)


def _cplx_combine(e1, e2):
    a1r, a1i, b1r, b1i = e1
    a2r, a2i, b2r, b2i = e2
    return (a2r * a1r - a2i * a1i,
            a2r * a1i + a2i * a1r,
            a2r * b1r - a2i * b1i + b2r,
            a2r * b1i + a2i * b1r + b2i)


def _s5_branch(u, gs, h0_re, h0_im, a_re, a_im, log_dt, b_re, b_im, c_re, c_im,
               d_skip, w_glu, b_glu):
    bsz, L, _ = u.shape
    f32 = jnp.float32
    ug = u.reshape(bsz, L, N_SSM_GROUPS, SSM_GROUP)
    a_re = a_re.astype(f32); a_im = a_im.astype(f32)
    dt = jnp.exp(log_dt.astype(f32))[:, None]
    mag = jnp.exp(a_re * dt)
    abar_re = mag * jnp.cos(a_im * dt)
    abar_im = mag * jnp.sin(a_im * dt)
    nr = abar_re - 1.0
    den = a_re * a_re + a_im * a_im
    coef_re = (nr * a_re + abar_im * a_im) / den
    coef_im = (abar_im * a_re - nr * a_im) / den
    b_re = b_re.astype(f32); b_im = b_im.astype(f32)
    bbar_re = coef_re[..., None] * b_re - coef_im[..., None] * b_im
    bbar_im = coef_re[..., None] * b_im + coef_im[..., None] * b_re
    bu_re = jnp.einsum('blgc,gpc->blgp', ug, bbar_re)
    bu_im = jnp.einsum('blgc,gpc->blgp', ug, bbar_im)
    h0_re = h0_re.astype(f32); h0_im = h0_im.astype(f32)
    bu_re = bu_re.at[:, 0].add(abar_re * h0_re - abar_im * h0_im)
    bu_im = bu_im.at[:, 0].add(abar_re * h0_im + abar_im * h0_re)
    A_re = jnp.broadcast_to(abar_re, bu_re.shape)
    A_im = jnp.broadcast_to(abar_im, bu_im.shape)
    _, _, h_re, h_im = lax.associative_scan(_cplx_combine, (A_re, A_im, bu_re, bu_im), axis=1)
    y = (jnp.einsum('blgp,gcp->blgc', h_re, c_re.astype(f32))
         - jnp.einsum('blgp,gcp->blgc', h_im, c_im.astype(f32)))
    y = y.reshape(bsz, L, D_SSM) + d_skip.astype(f32) * u
    z = jax.nn.gelu(y)
    z = z * jax.nn.sigmoid(z @ w_glu.astype(f32) + b_glu.astype(f32))
    return z * jax.nn.silu(gs), h_re[:, -1], h_im[:, -1]


def setup_inputs(seed: int = 0) -> dict:
    key = jax.random.key(seed)
    ks = jax.random.split(key, 32)
    f32 = jnp.float32
    n_pages = PAST_LEN // PAGE_SIZE
    used = DEC_BATCH * n_pages
    n_pool = used + max(used // 4, 1)
    perm = jax.random.permutation(ks[0], n_pool)
    page_table = perm[:used].reshape(DEC_BATCH, n_pages).astype(jnp.int32)
    nrm = lambda k, s, sc: jax.random.normal(k, s, f32) * sc
    n_idx = jnp.arange(SSM_STATE, dtype=f32)
    a_re = -0.5 * jnp.exp(nrm(ks[1], (DEPTH, N_SSM_GROUPS, SSM_STATE), 0.01))
    a_im = math.pi * n_idx[None, None, :] + nrm(ks[2], (DEPTH, N_SSM_GROUPS, SSM_STATE), 0.01)
    log_dt = jax.random.uniform(ks[3], (DEPTH, N_SSM_GROUPS), f32, math.log(1e-3), math.log(1e-1))
    return {
        "x_prompt": nrm(ks[4], (BATCH, SEQ, D_MODEL), 1.0),
        "x_sample": nrm(ks[5], (DEC_BATCH, DEC_SEQ, D_MODEL), 1.0),
        "cache_k": nrm(ks[6], (DEPTH, n_pool, PAGE_SIZE, N_HEADS, 2 * QK_DIM), 1.0),
        "cache_v": nrm(ks[7], (DEPTH, n_pool, PAGE_SIZE, N_HEADS, V_DIM), 1.0),
        "state_ssm_re": nrm(ks[8], (DEPTH, DEC_BATCH, N_SSM_GROUPS, SSM_STATE), 0.1),
        "state_ssm_im": nrm(ks[9], (DEPTH, DEC_BATCH, N_SSM_GROUPS, SSM_STATE), 0.1),
        "page_table": page_table,
        "norm_g": 1.0 + nrm(ks[10], (DEPTH, D_MODEL), 0.01),
        "w_in": nrm(ks[11], (DEPTH, D_MODEL, D_IN_PROJ), D_MODEL ** -0.5),
        "q_norm_g": 1.0 + nrm(ks[12], (DEPTH, QK_DIM), 0.01),
        "k_norm_g": 1.0 + nrm(ks[13], (DEPTH, QK_DIM), 0.01),
        "lambda_q1": nrm(ks[14], (DEPTH, QK_DIM), 0.1),
        "lambda_k1": nrm(ks[15], (DEPTH, QK_DIM), 0.1),
        "lambda_q2": nrm(ks[16], (DEPTH, QK_DIM), 0.1),
        "lambda_k2": nrm(ks[17], (DEPTH, QK_DIM), 0.1),
        "subln_g": 1.0 + nrm(ks[18], (DEPTH, V_DIM), 0.01),
        "rel_bias": nrm(ks[19], (N_BUCKETS, N_HEADS), 0.5),
        "ssm_a_re": a_re,
        "ssm_a_im": a_im,
        "ssm_log_dt": log_dt,
        "ssm_b_re": nrm(ks[20], (DEPTH, N_SSM_GROUPS, SSM_STATE, SSM_GROUP), (2 * SSM_GROUP) ** -0.5),
        "ssm_b_im": nrm(ks[21], (DEPTH, N_SSM_GROUPS, SSM_STATE, SSM_GROUP), (2 * SSM_GROUP) ** -0.5),
        "ssm_c_re": nrm(ks[22], (DEPTH, N_SSM_GROUPS, SSM_GROUP, SSM_STATE), SSM_STATE ** -0.5),
        "ssm_c_im": nrm(ks[23], (DEPTH, N_SSM_GROUPS, SSM_GROUP, SSM_STATE), SSM_STATE ** -0.5),
        "ssm_d": nrm(ks[24], (DEPTH, D_SSM), 1.0),
        "w_glu": nrm(ks[25], (DEPTH, D_SSM, D_SSM), D_SSM ** -0.5),
        "b_glu": nrm(ks[26], (DEPTH, D_SSM), 0.01),
        "w_out": nrm(ks[27], (DEPTH, D_MODEL, D_MODEL), D_MODEL ** -0.5),
    }


def reference(x_prompt, x_sample, cache_k, cache_v, state_ssm_re, state_ssm_im, page_table,
              norm_g, w_in, q_norm_g, k_norm_g, lambda_q1, lambda_k1, lambda_q2, lambda_k2,
              subln_g, rel_bias, ssm_a_re, ssm_a_im, ssm_log_dt, ssm_b_re, ssm_b_im,
              ssm_c_re, ssm_c_im, ssm_d, w_glu, b_glu, w_out):
    f32 = jnp.float32
    n_pages = PAST_LEN // PAGE_SIZE
    n_blocks = SEQ // Q_BLOCK
    hp, hs = x_prompt, x_sample
    kp_l, vp_l, ks_l, vs_l = [], [], [], []
    srp_l, sip_l, srs_l, sis_l = [], [], [], []
    for l in range(DEPTH):
        lam_init = _lambda_init(l)
        lam = (jnp.exp(jnp.sum(lambda_q1[l].astype(f32) * lambda_k1[l].astype(f32)))
               - jnp.exp(jnp.sum(lambda_q2[l].astype(f32) * lambda_k2[l].astype(f32))) + lam_init)
        ssm_params = (ssm_a_re[l], ssm_a_im[l], ssm_log_dt[l], ssm_b_re[l], ssm_b_im[l],
                      ssm_c_re[l], ssm_c_im[l], ssm_d[l], w_glu[l], b_glu[l])

        xn = _rmsnorm(hp, norm_g[l])
        q, k, v, ga, u, gs = _in_proj(xn, w_in[l], q_norm_g[l], k_norm_g[l])
        k_pos = jnp.arange(SEQ)
        qb = jnp.moveaxis(q.reshape(BATCH, n_blocks, Q_BLOCK, N_HEADS, 2, QK_DIM), 1, 0)

        def _block(args, k=k, v=v, k_pos=k_pos, lam=lam):
            q_blk, bi = args
            q_pos = bi * Q_BLOCK + jnp.arange(Q_BLOCK)
            return _diff_attend(q_blk, k, v, q_pos, k_pos, rel_bias, lam)

        o = lax.map(_block, (qb, jnp.arange(n_blocks)))
        o = jnp.moveaxis(o, 0, 1).reshape(BATCH, SEQ, N_HEADS, V_DIM)
        o_a = _attn_out(o, ga, subln_g[l], lam_init)
        zero_h = jnp.zeros((BATCH, N_SSM_GROUPS, SSM_STATE), f32)
        o_s, hr_p, hi_p = _s5_branch(u, gs, zero_h, zero_h, *ssm_params)
        hp = (hp.astype(f32) + jnp.concatenate([o_a, o_s], axis=-1) @ w_out[l].astype(f32)).astype(x_prompt.dtype)
        kp_l.append(k.reshape(BATCH, SEQ, N_HEADS, 2 * QK_DIM).astype(cache_k.dtype))
        vp_l.append(v.astype(cache_v.dtype))
        srp_l.append(hr_p.astype(state_ssm_re.dtype)); sip_l.append(hi_p.astype(state_ssm_im.dtype))

        xn = _rmsnorm(hs, norm_g[l])
        q, k, v, ga, u, gs = _in_proj(xn, w_in[l], q_norm_g[l], k_norm_g[l])
        k_past = cache_k[l][page_table].reshape(DEC_BATCH, n_pages * PAGE_SIZE, N_HEADS, 2, QK_DIM).astype(f32)
        v_past = cache_v[l][page_table].reshape(DEC_BATCH, n_pages * PAGE_SIZE, N_HEADS, V_DIM).astype(f32)
        k_all = jnp.concatenate([k_past, k], axis=1)
        v_all = jnp.concatenate([v_past, v], axis=1)
        q_pos = PAST_LEN + jnp.arange(DEC_SEQ)
        k_pos_s = jnp.arange(PAST_LEN + DEC_SEQ)
        o = _diff_attend(q, k_all, v_all, q_pos, k_pos_s, rel_bias, lam)
        o_a = _attn_out(o, ga, subln_g[l], lam_init)
        o_s, hr_s, hi_s = _s5_branch(u, gs, state_ssm_re[l], state_ssm_im[l], *ssm_params)
        hs = (hs.astype(f32) + jnp.concatenate([o_a, o_s], axis=-1) @ w_out[l].astype(f32)).astype(x_sample.dtype)
        ks_l.append(k.reshape(DEC_BATCH, DEC_SEQ, N_HEADS, 2 * QK_DIM).astype(cache_k.dtype))
        vs_l.append(v.astype(cache_v.dtype))
        srs_l.append(hr_s.astype(state_ssm_re.dtype)); sis_l.append(hi_s.astype(state_ssm_im.dtype))

    y_prompt, y_sample = hp, hs
    k_prompt = jnp.stack(kp_l); v_prompt = jnp.stack(vp_l)
    k_sample = jnp.stack(ks_l); v_sample = jnp.stack(vs_l)
    ssm_re_prompt = jnp.stack(srp_l); ssm_im_prompt = jnp.stack(sip_l)
    ssm_re_sample = jnp.stack(srs_l); ssm_im_sample = jnp.stack(sis_l)
    return (y_prompt, y_sample, k_prompt, v_prompt, k_sample, v_sample,
            ssm_re_prompt, ssm_im_prompt, ssm_re_sample, ssm_im_sample)
```

```python
import numpy as np
import concourse.bass as bass
import concourse.mybir as mybir
from concourse.bass_utils import run_bass_kernel_spmd

F32 = mybir.dt.float32
BF16 = mybir.dt.bfloat16
I32 = mybir.dt.int32
AF = mybir.ActivationFunctionType
ALU = mybir.AluOpType
AX = mybir.AxisListType

SEQ = 8192
DM = 1024
NT = SEQ // 128
EPS = 1e-6


class Buf:
    __slots__ = ("name", "w", "r")

    def __init__(self, name=""):
        self.name = name
        self.w = None
        self.r = []


class Eng:
    def __init__(self, fw, name, h):
        self.name = name
        self.h = h
        self.sem = fw.nc.alloc_semaphore("s_" + name)
        self.cnt = 0
        self.waited = {}


class FW:
    def __init__(self, nc, n_dma_sems=24):
        self.nc = nc
        self.pe = Eng(self, "pe", nc.tensor)
        self.act = Eng(self, "act", nc.scalar)
        self.dve = Eng(self, "dve", nc.vector)
        self.pool = Eng(self, "pool", nc.gpsimd)
        self.sp = Eng(self, "sp", nc.sync)
        self.engs = (self.pe, self.act, self.dve, self.pool, self.sp)
        self.sems = {e.name: e.sem for e in self.engs}
        self.dma_sems = []
        for i in range(n_dma_sems):
            k = "dma%d" % i
            self.sems[k] = nc.alloc_semaphore(k)
            self.dma_sems.append([k, 0])
        self.dma_rr = 0

    def _need(self, eng, reads, writes):
        need = {}

        def add(ev, is_raw):
            if ev is None:
                return
            k, v = ev
            if k == eng.name and not is_raw:
                return
            if need.get(k, 0) < v:
                need[k] = v
        for b in reads:
            add(b.w, True)
        for b in writes:
            add(b.w, False)
            for ev in b.r:
                add(ev, False)
        for k, v in need.items():
            if eng.waited.get(k, 0) >= v:
                continue
            eng.h.wait_ge(self.sems[k], v)
            eng.waited[k] = v

    def _mark(self, ev, reads, writes):
        for b in reads:
            b.r.append(ev)
            if len(b.r) > 48:
                d = {}
                for k, v in b.r:
                    if d.get(k, 0) < v:
                        d[k] = v
                b.r = list(d.items())
        for b in writes:
            b.w = ev
            b.r = []

    only = None
    stage = 0

    def _skip(self):
        return self.only is not None and self.stage != self.only

    def op(self, eng, fn, reads=(), writes=(), sig=True):
        if self._skip():
            return None
        self._need(eng, reads, writes)
        ins = fn()
        if sig:
            ins.then_inc(eng.sem, 1)
            eng.cnt += 1
            ev = (eng.name, eng.cnt)
        else:
            ev = (eng.name, eng.cnt + 1)
        self._mark(ev, reads, writes)
        return ins

    def dma(self, eng, out, in_, reads=(), writes=(), **kw):
        if self._skip():
            return None
        slot = self.dma_sems[self.dma_rr]
        self.dma_rr = (self.dma_rr + 1) % len(self.dma_sems)
        k, c = slot
        if c > 0 and eng.waited.get(k, 0) < c:
            eng.h.wait_ge(self.sems[k], c)
            eng.waited[k] = c
        self._need(eng, reads, writes)
        ins = eng.h.dma_start(out=out, in_=in_, **kw)
        ins.then_inc(self.sems[k], 16)
        slot[1] = c + 16
        ev = (k, c + 16)
        self._mark(ev, reads, writes)
        return ev

    def idma(self, out, in_, idx_ap, reads=(), writes=()):
        if self._skip():
            return None
        eng = self.pool
        slot = self.dma_sems[self.dma_rr]
        self.dma_rr = (self.dma_rr + 1) % len(self.dma_sems)
        k, c = slot
        if c > 0 and eng.waited.get(k, 0) < c:
            eng.h.wait_ge(self.sems[k], c)
            eng.waited[k] = c
        self._need(eng, reads, writes)
        ins = eng.h.indirect_dma_start(out=out, out_offset=None, in_=in_,
                                       in_offset=bass.IndirectOffsetOnAxis(ap=idx_ap, axis=0))
        ins.then_inc(self.sems[k], 16)
        slot[1] = c + 16
        ev = (k, c + 16)
        self._mark(ev, reads, writes)
        return ev

    def wait_all(self, eng, bufs):
        self._need(eng, bufs, ())

    def barrier(self):
        for e in self.engs:
            for f in self.engs:
                if f is e or f.cnt == 0:
                    continue
                if e.waited.get(f.name, 0) < f.cnt:
                    e.h.wait_ge(f.sem, f.cnt)
                    e.waited[f.name] = f.cnt
            for k, c in self.dma_sems:
                if c > 0 and e.waited.get(k, 0) < c:
                    e.h.wait_ge(self.sems[k], c)
                    e.waited[k] = c


class B:
    pass


def _sb(g, name, shape, dtype, n=1):
    out = []
    nb = 1
    for d in shape[1:]:
        nb *= d
    nb *= (2 if dtype == BF16 else 4)
    g.sb_bytes = getattr(g, "sb_bytes", 0) + ((nb + 31) // 32 * 32) * n
    for i in range(n):
        g.uid += 1
        t = g.nc.alloc_sbuf_tensor("%s_%d" % (name, g.uid), list(shape), dtype)
        out.append((t, Buf(name + str(i))))
    return out if n > 1 else out[0]


def build(npool=2560):
    nc = bass.Bass("TRN2", target_bir_lowering=False)
    g = B()
    g.nc = nc
    g.uid = 0
    fw = FW(nc)
    g.fw = fw
    V, S, P, T, SP = fw.dve, fw.act, fw.pool, fw.pe, fw.sp
    vec, act, pool, pe = nc.vector, nc.scalar, nc.gpsimd, nc.tensor

    def din(name, shape, dt=F32):
        return nc.dram_tensor(name, list(shape), dt, kind="ExternalInput").ap()

    def dout(name, shape, dt=F32):
        return nc.dram_tensor(name, list(shape), dt, kind="ExternalOutput").ap()

    xp = din("xp", [SEQ, DM])
    wa = din("wa", [128, 8, 640])
    ng = din("ng", [128, 8])
    ident_d = din("ident", [128, 128])
    gqk_d = din("gqk", [256])
    lamv_d = din("lamv", [256])
    sg_d = din("sg", [128])
    rb31_d = din("rb31", [1])
    rbx_d = din("rbx", [33, 128])
    oh_d = din("oh", [33, 640])
    xs_d = din("xs", [64, DM])
    wsm_d = din("wsm", [128, 5, 8, 512])
    g16_d = din("g16", [1024])
    s5a16_d = din("s5a16", [128, 48])
    s5b16_d = din("s5b16", [128, 2, 16, 16])
    s5c16_d = din("s5c16", [128, 2, 16, 32])
    hst_d = din("hst", [128, 2, 16, 16])
    dsk16_d = din("dsk16", [512])
    pt_d = din("pt", [256], I32)
    pcol_d = din("pcol", [128, 1])
    rbfar_d = din("rbfar", [32])
    oh15_d = din("oh15", [32, 4, 128])
    ohn_d = din("ohn", [33, 4, 4])
    rb33_d = din("rb33", [33, 4])
    dm_d = din("dm", [8, 8])
    sg4_d = din("sg4", [512])
    ck_d = din("ck", [npool * 128, 512])
    cv_d = din("cv", [npool * 128, 512])
    sss_o = dout("sss", [128, 2, 16, 16])
    oraw = nc.dram_tensor("oraw", [64, 512], F32, kind="Internal").ap()
    agin = nc.dram_tensor("agin", [SEQ, 256], F32, kind="Internal").ap()
    agout = nc.dram_tensor("agout", [4 * SEQ, 256], F32, kind="Internal", addr_space="Local").ap()
    ozs_scr = nc.dram_tensor("ozs_scr", [64, DM], F32, kind="Internal").ap()
    b_ozscr = Buf("ozscr"); b_agout = Buf("agout")
    xq = din("xq", [NTB * 128, DM])
    wgs_d = din("wgs", [128, 8, 512])
    wglu_d = din("wglu", [128, 4, 512])
    wout_d = din("wout", [128, 2, 8, 512])
    bglu_d = din("bglu", [512])
    idxb_d = din("idxb", [128, 64], I32)
    yq = dout("yq", [NTB * 128, DM])
    b_yq = Buf("yq")
    b_dbg = Buf("dbg")
    b_ssso = Buf("ssso"); b_ozso = Buf("ozso"); b_oraw = Buf("oraw")
    ks_o = dout("ks", [64, 512])
    vs_o = dout("vs", [64, 512])
    b_kso = Buf("kso"); b_vso = Buf("vso")
    s5a_d = din("s5a", [128, 12])
    s5b_d = din("s5b", [128, 2, 4, 16])
    s5c_d = din("s5c", [128, 2, 4, 32])
    dsk_d = din("dsk", [128])
    sfin_d = dout("sfin", [128, 8])
    b_zo = Buf("zo"); b_sfin = Buf("sfin")
    kp = dout("kp", [SEQ, 128])
    bvd_t = nc.dram_tensor("bvd", [128, 640], F32, kind="Internal")
    bvd = bvd_t.ap()
    b_bvd = Buf("bvd")
    b_oadbg = Buf("oadbg")
    vp = dout("vp", [SEQ, 128])
    u_scr = nc.dram_tensor("u_scr", [SEQ, 128], F32, kind="Internal").ap()
    out_bufs = [Buf("kp"), Buf("vp"), Buf("uscr")]
    bkp, bvp, buscr = out_bufs

    banks = [nc.alloc_psum_tensor("bank%d" % i, [128, 512], F32) for i in range(8)]
    bbuf = [Buf("bank%d" % i) for i in range(8)]

    ident_f, b_identf = _sb(g, "identf", [128, 128], F32)
    ident, b_ident = _sb(g, "ident", [128, 128], BF16)
    fw.dma(SP, ident_f[:], ident_d[:, :], writes=[b_identf])
    fw.op(V, lambda: vec.tensor_copy(out=ident[:], in_=ident_f[:]), reads=[b_identf], writes=[b_ident])

    ngt, b_ng = _sb(g, "ng", [128, 8], F32)
    fw.dma(SP, ngt[:], ng[:, :], writes=[b_ng])
    gqk, b_gqk = _sb(g, "gqk", [128, 256], F32)
    fw.dma(SP, gqk[:], gqk_d.partition_broadcast(128), writes=[b_gqk])
    fw.op(V, lambda: vec.tensor_scalar(out=gqk[:, 0:128], in0=gqk[:, 0:128], scalar1=0.125, scalar2=None,
                                       op0=ALU.mult), reads=[b_gqk], writes=[b_gqk])

    wst, b_wst = _sb(g, "wst", [128, 8, 640], F32)
    wab, b_wab = _sb(g, "wab", [128, 8, 640], BF16)
    fw.dma(SP, wst[:], wa[:, :, :], writes=[b_wst])
    for kt in range(8):
        fw.op(V if kt % 2 == 0 else P,
              (lambda kt=kt: (vec if kt % 2 == 0 else pool).tensor_scalar(
                  out=wab[:, kt, :], in0=wst[:, kt, :], scalar1=ngt[:, kt:kt + 1], scalar2=None, op0=ALU.mult)),
              reads=[b_wst, b_ng], writes=[b_wab])

    QT0, b_qkt = _sb(g, "QT0", [128, SEQ], BF16)
    QT1, _ = _sb(g, "QT1", [128, SEQ], BF16)
    KT, _ = _sb(g, "KT", [128, SEQ], BF16)
    G, b_G = _sb(g, "G", [128, NT, 128], BF16)
    fw.op(P, lambda: pool.memset(QT0[64:128, :], 0.0), writes=[b_qkt])
    fw.op(P, lambda: pool.memset(QT1[0:64, :], 0.0), writes=[b_qkt])
    sgt, b_sg = _sb(g, "sgt", [128, 128], F32)
    fw.dma(SP, sgt[:], sg_d.partition_broadcast(128), writes=[b_sg])
    fw.op(V, lambda: vec.tensor_scalar(out=sgt[:], in0=sgt[:], scalar1=0.8, scalar2=None, op0=ALU.mult),
          reads=[b_sg], writes=[b_sg])
    gab = _sb(g, "ga", [128, 128], F32, 2)
    ga2b = _sb(g, "ga2", [128, 128], F32, 2)
    Vaug, b_vaug = _sb(g, "Vaug", [128, NT, 130], BF16)
    fw.op(P, lambda: pool.memset(Vaug[:, :, 128:130], 1.0), writes=[b_vaug])

    xb = _sb(g, "xt", [128, DM], F32, 3)
    xnb = _sb(g, "xn", [128, DM], BF16, 2)
    xTb = _sb(g, "xT", [128, 8, 128], BF16, 2)
    junk, b_junk = _sb(g, "junk", [128, DM], BF16)
    ssb = _sb(g, "ss", [128, 4], F32, 2)
    qkb = _sb(g, "qk", [128, 256], F32, 2)
    sqb = _sb(g, "sq", [128, 256], F32, 2)
    s4b = _sb(g, "s4", [128, 8], F32, 2)
    qknb = _sb(g, "qkn", [128, 256], F32, 2)
    qkbfb = _sb(g, "qkbf", [128, 256], BF16, 2)
    vsb = _sb(g, "vs", [128, 128], F32, 2)
    usb = _sb(g, "us", [128, 128], F32, 2)
    ubfb = _sb(g, "ubf", [128, 128], BF16, 2)
    UT = wst[:].rearrange("p a b -> p (a b)").bitcast(BF16)[:, 0:SEQ]

    psT = [banks[0].bitcast(BF16) if False else None]
    def a1_body(t):
        fw.stage = 1
        xt, b_xt = xb[t % 3]
        xn, b_xn = xnb[t % 2]
        xT, b_xT = xTb[t % 2]
        ss, b_ss = ssb[t % 2]
        qk, b_qk = qkb[t % 2]
        sq, b_sq = sqb[t % 2]
        s4, b_s4 = s4b[t % 2]
        qkn, b_qkn = qknb[t % 2]
        qkbf, b_qkbf = qkbfb[t % 2]
        vs, b_vs = vsb[t % 2]
        us, b_us = usb[t % 2]
        pT = banks[t % 2][:, :].bitcast(BF16)
        b_pT = bbuf[t % 2]
        pA = banks[2 + (t % 2)]
        b_pA = bbuf[2 + (t % 2)]
        pU = banks[4 + (t % 2)]
        b_pU = bbuf[4 + (t % 2)]
        pQ = banks[6 + (t % 2)][:, :].bitcast(BF16)
        b_pQ = bbuf[6 + (t % 2)]
        rows = slice(t * 128, (t + 1) * 128)

        fw.dma(SP, xt[:], xp[rows, :], writes=[b_xt])
        fw.op(S, lambda: act.activation(out=junk[:], in_=xt[:], func=AF.Square, accum_out=ss[:, 0:1]),
              reads=[b_xt], writes=[b_junk, b_ss])
        fw.op(S, lambda: act.activation(out=ss[:, 1:2], in_=ss[:, 0:1], func=AF.Ln, scale=1.0 / DM, bias=EPS),
              reads=[b_ss], writes=[b_ss])
        fw.op(S, lambda: act.activation(out=ss[:, 2:3], in_=ss[:, 1:2], func=AF.Exp, scale=-0.5),
              reads=[b_ss], writes=[b_ss])
        fw.op(S, lambda: act.activation(out=xn[:], in_=xt[:], func=AF.Copy, scale=ss[:, 2:3]),
              reads=[b_xt, b_ss], writes=[b_xn])
        for kt in range(8):
            fw.op(T, (lambda kt=kt: pe.transpose(out=pT[:, kt * 128:(kt + 1) * 128],
                                                 in_=xn[:, kt * 128:(kt + 1) * 128], identity=ident[:])),
                  reads=[b_xn, b_ident], writes=[b_pT], sig=(kt == 7))
        fw.op(V, lambda: vec.tensor_copy(out=xT[:].rearrange("p a b -> p (a b)"), in_=pT[:, :]),
              reads=[b_pT], writes=[b_xT])
        for kt in range(8):
            fw.op(T, (lambda kt=kt: pe.matmul(pA[:, :], lhsT=xT[:, kt, :], rhs=wab[:, kt, 0:512],
                                              start=(kt == 0), stop=(kt == 7))),
                  reads=[b_xT, b_wab], writes=[b_pA], sig=(kt == 7))
        for kt in range(8):
            fw.op(T, (lambda kt=kt: pe.matmul(pU[:, 0:128], lhsT=xT[:, kt, :], rhs=wab[:, kt, 512:640],
                                              start=(kt == 0), stop=(kt == 7))),
                  reads=[b_xT, b_wab], writes=[b_pU], sig=(kt == 7))
        fw.stage = 2
        fw.op(S, lambda: act.activation(out=qk[:], in_=pA[:, 0:256], func=AF.Copy), reads=[b_pA], writes=[b_qk])
        fw.op(V, lambda: vec.tensor_tensor(out=sq[:], in0=qk[:], in1=qk[:], op=ALU.mult),
              reads=[b_qk], writes=[b_sq])
        fw.op(V, lambda: vec.tensor_reduce(out=s4[:, 0:4], in_=sq[:].rearrange("p (a d) -> p a d", d=64),
                                           axis=AX.X, op=ALU.add), reads=[b_sq], writes=[b_s4])
        fw.op(S, lambda: act.activation(out=s4[:, 4:8], in_=s4[:, 0:4], func=AF.Ln, scale=1.0 / 64, bias=EPS),
              reads=[b_s4], writes=[b_s4])
        fw.op(S, lambda: act.activation(out=s4[:, 0:4], in_=s4[:, 4:8], func=AF.Exp, scale=-0.5),
              reads=[b_s4], writes=[b_s4])
        fw.op(V, lambda: vec.tensor_tensor(out=qkn[:].rearrange("p (a d) -> p a d", d=64),
                                           in0=qk[:].rearrange("p (a d) -> p a d", d=64),
                                           in1=s4[:, 0:4].unsqueeze(2).broadcast_to([128, 4, 64]), op=ALU.mult),
              reads=[b_qk, b_s4], writes=[b_qkn])
        fw.op(V, lambda: vec.tensor_tensor(out=qkn[:], in0=qkn[:], in1=gqk[:], op=ALU.mult),
              reads=[b_qkn, b_gqk], writes=[b_qkn])
        fw.stage = 3
        fw.dma(SP, kp[rows, :], qkn[:, 128:256], reads=[b_qkn], writes=[bkp])
        fw.op(P, lambda: pool.tensor_copy(out=qkbf[:], in_=qkn[:]), reads=[b_qkn], writes=[b_qkbf])
        for j in range(2):
            fw.op(T, (lambda j=j: pe.transpose(out=pQ[:, j * 128:(j + 1) * 128],
                                               in_=qkbf[:, j * 128:(j + 1) * 128], identity=ident[:])),
                  reads=[b_qkbf, b_ident], writes=[b_pQ], sig=(j == 1))
        fw.op(V, lambda: vec.tensor_copy(out=QT0[0:64, rows], in_=pQ[0:64, 0:128]), reads=[b_pQ], writes=[b_qkt])
        fw.op(V, lambda: vec.tensor_copy(out=QT1[64:128, rows], in_=pQ[64:128, 0:128]), reads=[b_pQ], writes=[b_qkt])
        fw.op(V, lambda: vec.tensor_copy(out=KT[:, rows], in_=pQ[:, 128:256]), reads=[b_pQ], writes=[b_qkt])
        ga, b_ga = gab[t % 2]
        ga2, b_ga2 = ga2b[t % 2]
        fw.stage = 2
        fw.op(S, lambda: act.activation(out=ga[:], in_=pA[:, 384:512], func=AF.Exp, scale=-1.0),
              reads=[b_pA], writes=[b_ga])
        fw.op(V, lambda: vec.tensor_scalar(out=ga[:], in0=ga[:], scalar1=1.0, scalar2=None, op0=ALU.add),
              reads=[b_ga], writes=[b_ga])
        fw.op(V, lambda: vec.reciprocal(out=ga[:], in_=ga[:]), reads=[b_ga], writes=[b_ga])
        fw.op(V, lambda: vec.tensor_tensor(out=ga2[:], in0=pA[:, 384:512], in1=ga[:], op=ALU.mult),
              reads=[b_pA, b_ga], writes=[b_ga2])
        fw.stage = 3
        fw.op(P, lambda: pool.tensor_tensor(out=G[:, t, :], in0=ga2[:], in1=sgt[:], op=ALU.mult),
              reads=[b_ga2, b_sg], writes=[b_G])
        fw.stage = 2
        fw.op(S, lambda: act.activation(out=vs[:], in_=pA[:, 256:384], func=AF.Copy), reads=[b_pA], writes=[b_vs])
        fw.stage = 3
        fw.dma(SP, vp[rows, :], vs[:], reads=[b_vs], writes=[bvp])
        fw.op(P, lambda: pool.tensor_copy(out=Vaug[:, t, 0:128], in_=vs[:]), reads=[b_vs], writes=[b_vaug])
        fw.stage = 2
        fw.op(S, lambda: act.activation(out=us[:], in_=pU[:, 0:128], func=AF.Copy), reads=[b_pU], writes=[b_us])
        fw.stage = 3
        fw.dma(SP, u_scr[rows, :], us[:], reads=[b_us], writes=[buscr])
        ubf, b_ubf = ubfb[t % 2]
        fw.op(P, lambda: pool.tensor_copy(out=ubf[:], in_=us[:]), reads=[b_us], writes=[b_ubf])
        fw.op(T, lambda: pe.transpose(out=pU[:, 256:512].bitcast(BF16)[:, 0:128], in_=ubf[:], identity=ident[:]),
              reads=[b_ubf, b_ident], writes=[b_pU])
        fw.op(V, lambda: vec.tensor_copy(out=UT[:, rows], in_=pU[:, 256:512].bitcast(BF16)[:, 0:128]),
              reads=[b_pU], writes=[b_wst])

    import os
    if False:
        for t in range(NT):
            a1_body(t)
    else:
        for k in range(-2, NT):
            for st, off in ((3, 0), (2, 1), (1, 2)):
                n = k + off
                if 0 <= n < NT:
                    fw.only = st
                    a1_body(n)
        fw.only = None

    fw.barrier()
    lamt, b_lam = _sb(g, "lamt", [128, 256], F32)
    lamw, b_lamw = _sb(g, "lamw", [128, 8], F32)
    fw.dma(SP, lamt[:], lamv_d.partition_broadcast(128), writes=[b_lam])
    fw.op(V, lambda: vec.tensor_tensor(out=lamt[:, 0:64], in0=lamt[:, 0:64], in1=lamt[:, 64:128], op=ALU.mult),
          reads=[b_lam], writes=[b_lam])
    fw.op(V, lambda: vec.tensor_tensor(out=lamt[:, 128:192], in0=lamt[:, 128:192], in1=lamt[:, 192:256], op=ALU.mult),
          reads=[b_lam], writes=[b_lam])
    fw.op(V, lambda: vec.tensor_reduce(out=lamw[:, 0:1], in_=lamt[:, 0:64], axis=AX.X, op=ALU.add),
          reads=[b_lam], writes=[b_lamw])
    fw.op(V, lambda: vec.tensor_reduce(out=lamw[:, 1:2], in_=lamt[:, 128:192], axis=AX.X, op=ALU.add),
          reads=[b_lam], writes=[b_lamw])
    fw.op(S, lambda: act.activation(out=lamw[:, 2:4], in_=lamw[:, 0:2], func=AF.Exp), reads=[b_lamw], writes=[b_lamw])
    fw.op(V, lambda: vec.tensor_tensor(out=lamw[:, 4:5], in0=lamw[:, 3:4], in1=lamw[:, 2:3], op=ALU.subtract),
          reads=[b_lamw], writes=[b_lamw])
    fw.op(V, lambda: vec.tensor_scalar(out=lamw[:, 5:6], in0=lamw[:, 4:5], scalar1=-0.2, scalar2=None, op0=ALU.add),
          reads=[b_lamw], writes=[b_lamw])
    neglam = lamw[:, 5:6]
    rb31, b_rb31 = _sb(g, "rb31", [128, 1], F32)
    fw.dma(SP, rb31[:], rb31_d.partition_broadcast(128), writes=[b_rb31])
    rbx, b_rbx = _sb(g, "rbx", [33, 128], F32)
    oh, b_oh = _sb(g, "oh", [33, 640], F32)
    bvs, b_bvs = _sb(g, "bvs", [128, 640], F32)
    fw.dma(SP, rbx[:], rbx_d[:, :], writes=[b_rbx])
    fw.dma(SP, oh[:], oh_d[:, :], writes=[b_oh])
    fw.op(T, lambda: pe.matmul(banks[0][:, 0:512], lhsT=rbx[:, :], rhs=oh[:, 0:512], start=True, stop=True),
          reads=[b_rbx, b_oh], writes=[bbuf[0]])
    fw.op(T, lambda: pe.matmul(banks[1][:, 0:128], lhsT=rbx[:, :], rhs=oh[:, 512:640], start=True, stop=True),
          reads=[b_rbx, b_oh], writes=[bbuf[1]])
    fw.op(V, lambda: vec.tensor_copy(out=bvs[:, 0:512], in_=banks[0][:, 0:512]), reads=[bbuf[0]], writes=[b_bvs])
    fw.op(V, lambda: vec.tensor_copy(out=bvs[:, 512:640], in_=banks[1][:, 0:128]), reads=[bbuf[1]], writes=[b_bvs])
    fw.dma(SP, bvd[:, :], bvs[:], reads=[b_bvs], writes=[b_bvd])
    btf = _sb(g, "btf", [128, 256], F32, 3)
    bth = _sb(g, "bth", [128, 256], BF16, 3)
    btl = _sb(g, "btl", [128, 256], BF16, 3)
    bt2, b_bt2 = _sb(g, "bt2", [128, 256], F32)
    for di, dl in enumerate((1, 0, -1)):
        src = bass.AP(tensor=bvd_t, offset=128 * dl + 255, ap=[[639, 128], [1, 256]])
        fw.dma(SP, btf[di][0][:], src, reads=[b_bvd], writes=[btf[di][1]])
        fw.op(V, (lambda di=di: vec.tensor_copy(out=bth[di][0][:], in_=btf[di][0][:])),
              reads=[btf[di][1]], writes=[bth[di][1]])
        fw.op(V, (lambda di=di: vec.tensor_tensor(out=bt2[:], in0=btf[di][0][:], in1=bth[di][0][:], op=ALU.subtract)),
              reads=[btf[di][1], bth[di][1]], writes=[b_bt2])
        fw.op(V, (lambda di=di: vec.tensor_copy(out=btl[di][0][:], in_=bt2[:])),
              reads=[b_bt2], writes=[btl[di][1]])

    sfinb = _sb(g, "sfin", [128, 16], F32)
    PTb = _sb(g, "PT", [128, 512], BF16, 2)
    ob = _sb(g, "o", [128, 128], F32, 2)
    o2b = _sb(g, "o2", [128, 128], F32, 2)
    osqb = _sb(g, "osq", [128, 128], F32, 2)
    oab = _sb(g, "oa", [128, 128], F32, 2)
    rwb = _sb(g, "rw", [128, 8], F32, 2)
    QTm = (QT0, QT1)
    def att_pair(i, j, pair):
        fw.stage = 1
        q0 = 256 * i
        Sb = banks[pair % 2]
        b_S = bbuf[pair % 2]
        PT, b_PT = PTb[pair % 2]
        dl = 2 * i - j
        near = dl <= 1
        for c in range(2):
            fw.op(T, (lambda c=c, j=j: pe.matmul(Sb[:, c * 256:(c + 1) * 256], lhsT=KT[:, j * 128:(j + 1) * 128],
                                                 rhs=QTm[c][:, q0:q0 + 256], start=True, stop=not near)),
                  reads=[b_qkt], writes=[b_S], sig=(c == 1 and not near))
            if near:
                di = 1 - dl
                fw.op(T, (lambda c=c, di=di: pe.matmul(Sb[:, c * 256:(c + 1) * 256], lhsT=ident[:],
                                                       rhs=bth[di][0][:], start=False, stop=False)),
                      reads=[b_ident, bth[di][1]], writes=[b_S], sig=False)
                fw.op(T, (lambda c=c, di=di: pe.matmul(Sb[:, c * 256:(c + 1) * 256], lhsT=ident[:],
                                                       rhs=btl[di][0][:], start=False, stop=True)),
                      reads=[b_ident, btl[di][1]], writes=[b_S], sig=(c == 1))
        fw.stage = 2
        if near:
            fw.op(S, lambda: act.activation(out=PT[:], in_=Sb[:, :], func=AF.Exp), reads=[b_S], writes=[b_PT])
        else:
            fw.op(S, lambda: act.activation(out=PT[:], in_=Sb[:, :], func=AF.Exp, bias=rb31[:, 0:1]),
                  reads=[b_S, b_rb31], writes=[b_PT])
        for c in range(2):
            for sub in range(2):
                last = 2 * i + sub
                if j > last:
                    continue
                accb = banks[4 + c * 2 + sub]
                fw.op(T, (lambda c=c, sub=sub, j=j, accb=accb, last=last: pe.matmul(
                    accb[:, 0:129], lhsT=PT[:, c * 256 + sub * 128: c * 256 + sub * 128 + 128],
                    rhs=Vaug[:, j, 0:129], start=(j == 0), stop=(j == last))),
                      reads=[b_PT, b_vaug], writes=[bbuf[4 + c * 2 + sub]], sig=True)

    def att_fin(i):
        for sub in range(2):
            tt = 2 * i + sub
            a0, a1 = banks[4 + sub], banks[6 + sub]
            b_a0, b_a1 = bbuf[4 + sub], bbuf[6 + sub]
            o, b_o = ob[tt % 2]
            o2, b_o2 = o2b[tt % 2]
            osq, b_osq = osqb[tt % 2]
            oa, b_oa = oab[tt % 2]
            rw, b_rw = rwb[tt % 2]
            fw.op(V, lambda: vec.reciprocal(out=rw[:, 0:1], in_=a0[:, 128:129]), reads=[b_a0], writes=[b_rw])
            fw.op(V, lambda: vec.reciprocal(out=rw[:, 1:2], in_=a1[:, 128:129]), reads=[b_a1], writes=[b_rw])
            fw.op(V, lambda: vec.tensor_tensor(out=rw[:, 2:3], in0=rw[:, 1:2], in1=neglam, op=ALU.mult),
                  reads=[b_rw, b_lamw], writes=[b_rw])
            fw.op(V, lambda: vec.tensor_scalar(out=o2[:], in0=a1[:, 0:128], scalar1=rw[:, 2:3], scalar2=None,
                                               op0=ALU.mult), reads=[b_a1, b_rw], writes=[b_o2])
            fw.op(V, lambda: vec.scalar_tensor_tensor(out=o[:], in0=a0[:, 0:128], scalar=rw[:, 0:1], in1=o2[:],
                                                      op0=ALU.mult, op1=ALU.add),
                  reads=[b_a0, b_rw, b_o2], writes=[b_o])
            fw.op(P, lambda: pool.tensor_tensor(out=osq[:], in0=o[:], in1=o[:], op=ALU.mult), reads=[b_o], writes=[b_osq])
            fw.op(V, lambda: vec.tensor_reduce(out=rw[:, 3:4], in_=osq[:], axis=AX.X, op=ALU.add),
                  reads=[b_osq], writes=[b_rw])
            fw.op(S, lambda: act.activation(out=rw[:, 4:5], in_=rw[:, 3:4], func=AF.Ln, scale=1.0 / 128, bias=EPS),
                  reads=[b_rw], writes=[b_rw])
            fw.op(S, lambda: act.activation(out=rw[:, 5:6], in_=rw[:, 4:5], func=AF.Exp, scale=-0.5),
                  reads=[b_rw], writes=[b_rw])
            fw.op(V, lambda: vec.scalar_tensor_tensor(out=oa[:], in0=o[:], scalar=rw[:, 5:6], in1=G[:, tt, :],
                                                      op0=ALU.mult, op1=ALU.mult),
                  reads=[b_o, b_rw, b_G], writes=[b_oa])
            fw.dma(P, agin[tt * 128:(tt + 1) * 128, 0:128], oa[:], reads=[b_oa], writes=[b_oadbg])


    plist = [(i, j) for i in range(SEQ // 256) for j in range(2 * i + 2)]
    if False:
        for n, (i, j) in enumerate(plist):
            att_pair(i, j, n)
            if j == 2 * i + 1:
                att_fin(i)
    else:
        fw.only = 1
        att_pair(plist[0][0], plist[0][1], 0)
        for n, (i, j) in enumerate(plist):
            if n + 1 < len(plist):
                fw.only = 1
                att_pair(plist[n + 1][0], plist[n + 1][1], n + 1)
            fw.only = 2
            att_pair(i, j, n)
            if j == 2 * i + 1:
                fw.only = None
                att_fin(i)
        fw.only = None

    fw.barrier()
    hpi, b_hpi = _sb(g, "hpi", [128, 1], F32)
    fw.op(V, lambda: vec.memset(hpi[:], float(np.pi / 2)), writes=[b_hpi])

    def s5_setup(tag, NP, a_d, b_d, c_d, alloc):
        r = B()
        s5a, b_s5a = alloc(tag + "a", [128, 3 * NP], F32)
        s5b, b_s5b = alloc(tag + "b", [128, 2, NP, 16], F32)
        s5c, b_s5c = alloc(tag + "c", [128, 2, NP, 32], F32)
        fw.dma(SP, s5a, a_d[:, :], writes=[b_s5a])
        fw.dma(SP, s5b, b_d[:, :, :, :], writes=[b_s5b])
        fw.dma(SP, s5c, c_d[:, :, :, :], writes=[b_s5c])
        W, b_W = alloc(tag + "w", [128, 20, NP], F32)
        r.b_W = b_W

        def w_(i):
            return W[:, i, :]

        def tt(o, a, b_, op):
            fw.op(V, lambda: vec.tensor_tensor(out=o, in0=a, in1=b_, op=op), reads=[b_W, b_s5a], writes=[b_W])

        def ts(o, a, s1, op0):
            fw.op(V, lambda: vec.tensor_scalar(out=o, in0=a, scalar1=s1, scalar2=None, op0=op0),
                  reads=[b_W, b_s5a], writes=[b_W])

        def ac(o, a, func, scale=1.0, bias=0.0):
            fw.op(S, lambda: act.activation(out=o, in_=a, func=func, scale=scale, bias=bias),
                  reads=[b_W, b_s5a], writes=[b_W])
        r.tt, r.ts, r.w_ = tt, ts, w_
        are, aim, ldt = s5a[:, 0:NP], s5a[:, NP:2 * NP], s5a[:, 2 * NP:3 * NP]
        DT, ADT, TH, MAG, C_, S_, CC, SS, CS, LR, LI = [w_(i) for i in range(11)]
        ac(DT, ldt, AF.Exp)
        tt(ADT, are, DT, ALU.mult)
        tt(TH, aim, DT, ALU.mult)
        ac(MAG, ADT, AF.Exp)
        ac(S_, TH, AF.Sin, scale=1.0 / 32)
        fw.op(S, lambda: act.activation(out=C_, in_=TH, func=AF.Sin, scale=1.0 / 32, bias=hpi[:, 0:1]),
              reads=[b_W, b_hpi], writes=[b_W])

        def csq(c, s_):
            tt(CC, c, c, ALU.mult)
            tt(SS, s_, s_, ALU.mult)
            tt(CS, c, s_, ALU.mult)
            tt(c, CC, SS, ALU.subtract)
            ts(s_, CS, 2.0, ALU.mult)
        r.csq = csq
        for _ in range(5):
            csq(C_, S_)
        tt(LR, MAG, C_, ALU.mult)
        tt(LI, MAG, S_, ALU.mult)
        r.MAG, r.C_, r.S_, r.LR, r.LI = MAG, C_, S_, LR, LI
        NR, DEN, T1, T2, CFR, CFI = [w_(i) for i in range(11, 17)]
        ts(NR, LR, -1.0, ALU.add)
        tt(T1, are, are, ALU.mult)
        tt(T2, aim, aim, ALU.mult)
        tt(DEN, T1, T2, ALU.add)
        fw.op(V, lambda: vec.reciprocal(out=DEN, in_=DEN), reads=[b_W], writes=[b_W])
        tt(T1, NR, are, ALU.mult)
        tt(T2, LI, aim, ALU.mult)
        tt(T1, T1, T2, ALU.add)
        tt(CFR, T1, DEN, ALU.mult)
        tt(T1, LI, are, ALU.mult)
        tt(T2, NR, aim, ALU.mult)
        tt(T1, T1, T2, ALU.subtract)
        tt(CFI, T1, DEN, ALU.mult)
        bb, b_bb = alloc(tag + "bb", [128, 2, NP, 16], F32)
        bt_, b_bt = alloc(tag + "bbt", [128, NP, 16], F32)

        def bc(x):
            return x.unsqueeze(2).broadcast_to([128, NP, 16])

        def tb(o, a, b_, op):
            fw.op(V, lambda: vec.tensor_tensor(out=o, in0=a, in1=b_, op=op), reads=[b_W, b_s5b, b_bb, b_bt],
                  writes=[b_bb, b_bt])
        tb(bb[:, 0], s5b[:, 0], bc(CFR), ALU.mult)
        tb(bt_, s5b[:, 1], bc(CFI), ALU.mult)
        tb(bb[:, 0], bb[:, 0], bt_, ALU.subtract)
        tb(bb[:, 1], s5b[:, 1], bc(CFR), ALU.mult)
        tb(bt_, s5b[:, 0], bc(CFI), ALU.mult)
        tb(bb[:, 1], bb[:, 1], bt_, ALU.add)
        Mz, b_Mz = alloc(tag + "Mz", [128, NP, 128], F32)
        BT, b_BT = alloc(tag + "BT", [128, 2 * NP, 128], BF16)
        Cb, b_Cb = alloc(tag + "Cb", [128, 2, NP, 32], BF16)
        for ri in range(2):
            fw.op(P, lambda: pool.memset(Mz, 0.0), writes=[b_Mz])
            for pr in range(NP):
                for gi in range(2):
                    c0 = (2 * (pr % 4) + gi) * 16
                    fw.op(V, (lambda ri=ri, pr=pr, gi=gi, c0=c0: vec.tensor_copy(
                        out=Mz[gi * 64:(gi + 1) * 64, pr, c0:c0 + 16], in_=bb[gi * 64:(gi + 1) * 64, ri, pr, :])),
                        reads=[b_bb], writes=[b_Mz])
            for m in range(NP):
                pb = banks[m % 2]
                fw.op(T, (lambda m=m, pb=pb: pe.transpose(out=pb[:, 0:128], in_=Mz[:, m, :], identity=ident_f[:])),
                      reads=[b_Mz, b_identf], writes=[bbuf[m % 2]])
                fw.op(V, (lambda m=m, pb=pb, ri=ri: vec.tensor_copy(out=BT[:, ri * NP + m, :], in_=pb[:, 0:128])),
                      reads=[bbuf[m % 2]], writes=[b_BT])
        fw.op(V, lambda: vec.tensor_copy(out=Cb[:, 0], in_=s5c[:, 0]), reads=[b_s5c], writes=[b_Cb])
        fw.op(V, lambda: vec.tensor_scalar(out=Cb[:, 1], in0=s5c[:, 1], scalar1=-1.0, scalar2=None, op0=ALU.mult),
              reads=[b_s5c], writes=[b_Cb])
        r.BT, r.b_BT, r.Cb, r.b_Cb = BT, b_BT, Cb, b_Cb
        return r

    def sb_alloc(name, shape, dt):
        t_, b_ = _sb(g, name, shape, dt)
        return t_[:], b_
    dsk, b_dsk = _sb(g, "dsk", [128, 128], F32)
    fw.dma(SP, dsk[:], dsk_d.partition_broadcast(128), writes=[b_dsk])
    r4 = s5_setup("own", 4, s5a_d, s5b_d, s5c_d, sb_alloc)
    b_W, tt, ts, w_, csq = r4.b_W, r4.tt, r4.ts, r4.w_, r4.csq
    MAG, C_, S_, BT, b_BT, Cb, b_Cb = r4.MAG, r4.C_, r4.S_, r4.BT, r4.b_BT, r4.Cb, r4.b_Cb
    ER = QT0[:, :].bitcast(F32).rearrange("p (a n) -> p a n", a=8)[:, 0:4, :]
    EI = QT0[:, :].bitcast(F32).rearrange("p (a n) -> p a n", a=8)[:, 4:8, :]
    RT = QT1[:, :].bitcast(F32).rearrange("p (a n) -> p a n", a=8)[:, 0:4, :]
    tmpA = QT1[:, :].bitcast(F32).rearrange("p (a n) -> p a n", a=8)[:, 4:8, :]
    tmpB = KT[:, :].bitcast(F32).rearrange("p (a n) -> p a n", a=8)
    b_E = Buf("E"); b_RT = Buf("RT")
    PR, PI = w_(17), w_(18)
    ts(PI, S_, -1.0, ALU.mult)
    ts(PR, C_, 1.0, ALU.mult)
    fw.op(V, lambda: vec.memset(ER[:, :, 0:1], 1.0), reads=[b_W], writes=[b_E])
    fw.op(V, lambda: vec.memset(EI[:, :, 0:1], 0.0), writes=[b_E])
    tsc, b_tsc = _sb(g, "tsc", [128, 256], F32)
    for k in range(9):
        n = 1 << k
        for pr in range(4):
            def e(o, a, s1, b_=None, op1=None):
                if b_ is None:
                    fw.op(V, lambda: vec.tensor_scalar(out=o, in0=a, scalar1=s1, scalar2=None, op0=ALU.mult),
                          reads=[b_E, b_W, b_tsc], writes=[b_E, b_tsc])
                else:
                    fw.op(V, lambda: vec.scalar_tensor_tensor(out=o, in0=a, scalar=s1, in1=b_, op0=ALU.mult, op1=op1),
                          reads=[b_E, b_W, b_tsc], writes=[b_E, b_tsc])
            pr_, pi_ = PR[:, pr:pr + 1], PI[:, pr:pr + 1]
            e(tsc[:, 0:n], EI[:, pr, 0:n], pi_)
            e(ER[:, pr, n:2 * n], ER[:, pr, 0:n], pr_, tsc[:, 0:n], ALU.subtract)
            e(tsc[:, 0:n], EI[:, pr, 0:n], pr_)
            e(EI[:, pr, n:2 * n], ER[:, pr, 0:n], pi_, tsc[:, 0:n], ALU.add)
        csq(PR, PI)
    ones512, b_ones = _sb(g, "ones512", [128, 512], F32)
    fw.op(P, lambda: pool.memset(ones512[:], 1.0), writes=[b_ones])
    for pr in range(4):
        fw.op(V, (lambda pr=pr: vec.tensor_scalar(out=RT[:, pr, :], in0=ones512[:], scalar1=MAG[:, pr:pr + 1],
                                                  scalar2=None, op0=ALU.mult)),
              reads=[b_ones, b_W], writes=[b_RT])
    car, b_car = _sb(g, "car", [128, 4, 8], F32)
    fw.op(V, lambda: vec.memset(car[:], 0.0), writes=[b_car])
    hreb = _sb(g, "hre", [128, 512], BF16, 2)
    himb = _sb(g, "him", [128, 512], BF16, 2)
    ut4b = _sb(g, "ut4", [128, 4, 128], F32, 2)
    b_tA = [Buf("tA%d" % i) for i in range(4)]
    b_tB = [Buf("tB%d" % i) for i in range(8)]
    it = 0
    for blk in range(SEQ // 512):
        cols = slice(blk * 512, (blk + 1) * 512)
        yps = banks[4 + blk % 2]
        b_yps = bbuf[4 + blk % 2]
        for pr in range(4):
            pre, pim = banks[(it % 2) * 2], banks[(it % 2) * 2 + 1]
            b_pre, b_pim = bbuf[(it % 2) * 2], bbuf[(it % 2) * 2 + 1]
            hre, b_hre = hreb[it % 2]
            him, b_him = himb[it % 2]
            it += 1
            fw.op(T, (lambda pr=pr: pe.matmul(pre[:, :], lhsT=BT[:, pr, :], rhs=UT[:, cols], start=True, stop=True)),
                  reads=[b_BT, b_wst], writes=[b_pre])
            fw.op(T, (lambda pr=pr: pe.matmul(pim[:, :], lhsT=BT[:, 4 + pr, :], rhs=UT[:, cols], start=True, stop=True)),
                  reads=[b_BT, b_wst], writes=[b_pim])
            a0, a1, a2, a3 = [tmpA[:, i, :] for i in range(4)]
            wre, wim, gre, gim = [tmpB[:, i, :] for i in range(4)]
            er, ei = ER[:, pr, :], EI[:, pr, :]
            fw.op(V, lambda: vec.tensor_tensor(out=a0, in0=pre[:, :], in1=er, op=ALU.mult), reads=[b_pre, b_E], writes=[b_tA[0]])
            fw.op(V, lambda: vec.tensor_tensor(out=a1, in0=pim[:, :], in1=ei, op=ALU.mult), reads=[b_pim, b_E], writes=[b_tA[1]])
            fw.op(V, lambda: vec.tensor_tensor(out=a2, in0=pim[:, :], in1=er, op=ALU.mult), reads=[b_pim, b_E], writes=[b_tA[2]])
            fw.op(V, lambda: vec.tensor_tensor(out=a3, in0=pre[:, :], in1=ei, op=ALU.mult), reads=[b_pre, b_E], writes=[b_tA[3]])
            fw.op(P, lambda: pool.tensor_tensor(out=wre, in0=a0, in1=a1, op=ALU.subtract), reads=[b_tA[0], b_tA[1]], writes=[b_tB[0]])
            fw.op(P, lambda: pool.tensor_tensor(out=wim, in0=a2, in1=a3, op=ALU.add), reads=[b_tA[2], b_tA[3]], writes=[b_tB[1]])
            fw.op(V, (lambda pr=pr: vec.tensor_tensor_scan(out=gre, data0=RT[:, pr, :], data1=wre, initial=car[:, pr, 0:1],
                                                           op0=ALU.mult, op1=ALU.add)),
                  reads=[b_RT, b_tB[0], b_car], writes=[b_tB[2]])
            fw.op(V, (lambda pr=pr: vec.tensor_tensor_scan(out=gim, data0=RT[:, pr, :], data1=wim, initial=car[:, pr, 1:2],
                                                           op0=ALU.mult, op1=ALU.add)),
                  reads=[b_RT, b_tB[1], b_car], writes=[b_tB[3]])
            ge_r, ge_i = gre[:, 511:512], gim[:, 511:512]
            prr, pii = PR[:, pr:pr + 1], PI[:, pr:pr + 1]
            def cop(o, a, b_, op):
                fw.op(V, lambda: vec.tensor_tensor(out=o, in0=a, in1=b_, op=op),
                      reads=[b_tB[2], b_tB[3], b_W, b_car], writes=[b_car])
            cop(car[:, pr, 2:3], ge_r, prr, ALU.mult)
            cop(car[:, pr, 3:4], ge_i, pii, ALU.mult)
            cop(car[:, pr, 4:5], ge_i, prr, ALU.mult)
            cop(car[:, pr, 5:6], ge_r, pii, ALU.mult)
            if blk == SEQ // 512 - 1:
                e_r, e_i = ER[:, pr, 511:512], EI[:, pr, 511:512]
                sf, b_sf = sfinb
                def fop(o, a, b_, op):
                    fw.op(V, lambda: vec.tensor_tensor(out=o, in0=a, in1=b_, op=op),
                          reads=[b_tB[2], b_tB[3], b_E, b_sf], writes=[b_sf])
                fop(sf[:, 8 + pr:9 + pr], e_r, ge_r, ALU.mult)
                fop(sf[:, 12 + pr:13 + pr], e_i, ge_i, ALU.mult)
                fop(sf[:, pr:pr + 1], sf[:, 8 + pr:9 + pr], sf[:, 12 + pr:13 + pr], ALU.add)
                fop(sf[:, 8 + pr:9 + pr], e_r, ge_i, ALU.mult)
                fop(sf[:, 12 + pr:13 + pr], e_i, ge_r, ALU.mult)
                fop(sf[:, 4 + pr:5 + pr], sf[:, 8 + pr:9 + pr], sf[:, 12 + pr:13 + pr], ALU.subtract)
            cop(car[:, pr, 0:1], car[:, pr, 2:3], car[:, pr, 3:4], ALU.add)
            cop(car[:, pr, 1:2], car[:, pr, 4:5], car[:, pr, 5:6], ALU.subtract)
            fw.op(P, lambda: pool.tensor_tensor(out=a0, in0=gre, in1=er, op=ALU.mult), reads=[b_tB[2], b_E], writes=[b_tA[0]])
            fw.op(P, lambda: pool.tensor_tensor(out=a1, in0=gim, in1=ei, op=ALU.mult), reads=[b_tB[3], b_E], writes=[b_tA[1]])
            fw.op(P, lambda: pool.tensor_tensor(out=a2, in0=gim, in1=er, op=ALU.mult), reads=[b_tB[3], b_E], writes=[b_tA[2]])
            fw.op(P, lambda: pool.tensor_tensor(out=a3, in0=gre, in1=ei, op=ALU.mult), reads=[b_tB[2], b_E], writes=[b_tA[3]])
            fw.op(P, lambda: pool.tensor_tensor(out=hre[:], in0=a0, in1=a1, op=ALU.add), reads=[b_tA[0], b_tA[1]], writes=[b_hre])
            fw.op(P, lambda: pool.tensor_tensor(out=him[:], in0=a2, in1=a3, op=ALU.subtract), reads=[b_tA[2], b_tA[3]], writes=[b_him])
            for sub in range(4):
                fw.op(T, (lambda pr=pr, sub=sub: pe.matmul(yps[:, sub * 128 + pr * 32: sub * 128 + pr * 32 + 32],
                                                           lhsT=hre[:, sub * 128:(sub + 1) * 128], rhs=Cb[:, 0, pr, :],
                                                           start=True, stop=False)),
                      reads=[b_hre, b_Cb], writes=[b_yps], sig=False)
                fw.op(T, (lambda pr=pr, sub=sub: pe.matmul(yps[:, sub * 128 + pr * 32: sub * 128 + pr * 32 + 32],
                                                           lhsT=him[:, sub * 128:(sub + 1) * 128], rhs=Cb[:, 1, pr, :],
                                                           start=False, stop=True)),
                      reads=[b_him, b_Cb], writes=[b_yps], sig=True)
        ut4, b_ut4 = ut4b[blk % 2]
        fw.dma(SP, ut4[:], u_scr[cols, :].rearrange("(s p) c -> p s c", p=128), reads=[buscr], writes=[b_ut4])
        y1, y2, y3 = [tmpB[:, 4 + i, :] for i in range(3)]
        b_y = b_tB[4:7]
        u2 = ut4[:].rearrange("p s c -> p (s c)")
        fw.op(P, lambda: pool.tensor_tensor(out=ut4[:], in0=ut4[:], in1=dsk[:].unsqueeze(1).broadcast_to([128, 4, 128]),
                                            op=ALU.mult), reads=[b_ut4, b_dsk], writes=[b_ut4])
        fw.op(V, lambda: vec.tensor_tensor(out=y1, in0=yps[:, :], in1=u2, op=ALU.add), reads=[b_yps, b_ut4], writes=[b_y[0]])
        fw.op(P, lambda: pool.tensor_tensor(out=y2, in0=y1, in1=y1, op=ALU.mult), reads=[b_y[0]], writes=[b_y[1]])
        fw.op(V, lambda: vec.tensor_scalar(out=y2, in0=y2, scalar1=0.044715, scalar2=1.0, op0=ALU.mult, op1=ALU.add),
              reads=[b_y[1]], writes=[b_y[1]])
        fw.op(P, lambda: pool.tensor_tensor(out=y3, in0=y2, in1=y1, op=ALU.mult), reads=[b_y[0], b_y[1]], writes=[b_y[2]])
        fw.op(S, lambda: act.activation(out=y3, in_=y3, func=AF.Exp, scale=-1.5957691216057308), reads=[b_y[2]], writes=[b_y[2]])
        fw.op(V, lambda: vec.tensor_scalar(out=y3, in0=y3, scalar1=1.0, scalar2=None, op0=ALU.add), reads=[b_y[2]], writes=[b_y[2]])
        fw.op(V, lambda: vec.reciprocal(out=y3, in_=y3), reads=[b_y[2]], writes=[b_y[2]])
        fw.op(V, lambda: vec.tensor_tensor(out=y2, in0=y1, in1=y3, op=ALU.mult), reads=[b_y[0], b_y[2]], writes=[b_y[1]])
        fw.dma(P, agin[cols, 128:256].rearrange("(s p) c -> p s c", p=128), y2.rearrange("p (s c) -> p s c", s=4),
               reads=[b_y[1]], writes=[b_zo])
    fw.dma(P, sfin_d[:, :], sfinb[0][:, 0:8], reads=[sfinb[1]], writes=[b_sfin])
    ccsem = nc.alloc_semaphore("ccsem")
    fw.sems["cc"] = ccsem
    fw._need(P, [b_oadbg, b_zo], [b_agout])
    for k in range(8):
        nc.gpsimd.collective_compute("AllGather", ALU.bypass, replica_groups=[[0, 1, 2, 3], [4, 5, 6, 7]],
                                     ins=[agin[k * 1024:(k + 1) * 1024, :]],
                                     outs=[agout[k * 4096:(k + 1) * 4096, :]]).then_inc(ccsem, 1)
    fw._mark(("cc", 8), [b_oadbg, b_zo], [b_agout])

    fw.barrier()
    class Arena:
        def __init__(self, regions):
            self.regions = regions
            self.offs = [0] * len(regions)

        def alloc(self, name, shape, dt):
            n = 1
            for d in shape[1:]:
                n *= d
            units = n if dt != BF16 else (n + 1) // 2
            units = (units + 7) // 8 * 8
            best = None
            for i, reg in enumerate(self.regions):
                rem = reg.shape[1] - self.offs[i]
                if rem >= units and (best is None or rem < best[1]):
                    best = (i, rem)
            assert best is not None, ("arena full", name, units, [r.shape[1] - o for r, o in zip(self.regions, self.offs)])
            i = best[0]
            reg = self.regions[i]
            ap = reg[0:shape[0], self.offs[i]:self.offs[i] + units]
            self.offs[i] += units
            if dt == BF16:
                ap = ap.bitcast(BF16)[:, 0:n]
            elif dt == I32:
                ap = ap.bitcast(I32)[:, 0:n]
            else:
                ap = ap[:, 0:n]
            if len(shape) == 3:
                ap = ap.rearrange("p (a b) -> p a b", a=shape[1])
            elif len(shape) == 4:
                ap = ap.rearrange("p (a b c) -> p a b c", a=shape[1], b=shape[2])
            return ap, Buf(name)
    ar = Arena([QT0[:, :].bitcast(F32), QT1[:, :].bitcast(F32), KT[:, :].bitcast(F32),
                G[:].rearrange("p a b -> p (a b)").bitcast(F32),
                Vaug[:].rearrange("p a b -> p (a b)").bitcast(F32),
                xb[0][0][:, :], xb[1][0][:, :], xb[2][0][:, :], ones512[:, :],
                ut4b[0][0][:].rearrange("p a b -> p (a b)"), ut4b[1][0][:].rearrange("p a b -> p (a b)"),
                xnb[0][0][:, :].bitcast(F32), xnb[1][0][:, :].bitcast(F32),
                hreb[0][0][:, :].bitcast(F32), hreb[1][0][:, :].bitcast(F32),
                himb[0][0][:, :].bitcast(F32), himb[1][0][:, :].bitcast(F32),
                bvs[:, :], sqb[0][0][:, :], sqb[1][0][:, :], qknb[0][0][:, :], qknb[1][0][:, :],
                btf[0][0][:, :], btf[1][0][:, :], btf[2][0][:, :], bt2[:, :],
                xTb[0][0][:].rearrange("p a b -> p (a b)").bitcast(F32), xTb[1][0][:].rearrange("p a b -> p (a b)").bitcast(F32)])
    print("static SBUF bytes/partition:", g.sb_bytes)
    NS = 16
    xst, b_xst = ar.alloc("xst", [64, DM], F32)
    xsn, b_xsn = ar.alloc("xsn", [64, DM], BF16)
    xTs, b_xTs = ar.alloc("xTs", [128, 8, 64], BF16)
    sss, b_sss = ar.alloc("sss", [64, 8], F32)
    proj, b_proj = ar.alloc("proj", [64, 2560], F32)
    g16, b_g16 = ar.alloc("g16", [64, 1024], F32)
    qkn, b_qkn = ar.alloc("qkn", [64, 1024], F32)
    qk2, b_qk2 = ar.alloc("qk2", [64, 1024], F32)
    qkb, b_qkb = ar.alloc("qkb", [64, 1024], BF16)
    r16, b_r16 = ar.alloc("r16", [64, 32], F32)
    fw.dma(SP, xst, xs_d[:, :], writes=[b_xst])
    fw.dma(SP, g16, g16_d.partition_broadcast(64), writes=[b_g16])
    fw.op(V, lambda: vec.tensor_scalar(out=g16[:, 0:512], in0=g16[:, 0:512], scalar1=0.125, scalar2=None, op0=ALU.mult),
          reads=[b_g16], writes=[b_g16])
    fw.op(S, lambda: act.activation(out=junk[0:64, :], in_=xst, func=AF.Square, accum_out=sss[:, 0:1]),
          reads=[b_xst], writes=[b_junk, b_sss])
    fw.op(S, lambda: act.activation(out=sss[:, 1:2], in_=sss[:, 0:1], func=AF.Ln, scale=1.0 / DM, bias=EPS),
          reads=[b_sss], writes=[b_sss])
    fw.op(S, lambda: act.activation(out=sss[:, 2:3], in_=sss[:, 1:2], func=AF.Exp, scale=-0.5), reads=[b_sss], writes=[b_sss])
    fw.op(S, lambda: act.activation(out=xsn, in_=xst, func=AF.Copy, scale=sss[:, 2:3]), reads=[b_xst, b_sss], writes=[b_xsn])
    pTs = banks[0][:, :].bitcast(BF16)
    for kt in range(8):
        fw.op(T, (lambda kt=kt: pe.transpose(out=pTs[:, kt * 64:(kt + 1) * 64], in_=xsn[:, kt * 128:(kt + 1) * 128],
                                             identity=ident[0:64, 0:64])), reads=[b_xsn, b_ident], writes=[bbuf[0]])
    fw.op(V, lambda: vec.tensor_copy(out=xTs.rearrange("p a b -> p (a b)"), in_=pTs[:, 0:512]), reads=[bbuf[0]], writes=[b_xTs])
    wsf = wst[:].rearrange("p a b -> p (a b)")[:, 0:4096].rearrange("p (a b) -> p a b", a=8)
    wsb = wab[:, :, 0:512]
    for ci in range(5):
        fw.dma(SP, wsf, wsm_d[:, ci, :, :], reads=[b_wab], writes=[b_wst])
        for kt in range(8):
            fw.op(V if kt % 2 else P, (lambda kt=kt: (vec if kt % 2 else pool).tensor_scalar(
                out=wsb[:, kt, :], in0=wsf[:, kt, :], scalar1=ngt[:, kt:kt + 1], scalar2=None, op0=ALU.mult)),
                reads=[b_wst, b_ng], writes=[b_wab])
        pk = banks[1 + ci % 2]
        for kt in range(8):
            fw.op(T, (lambda kt=kt, pk=pk: pe.matmul(pk[0:64, :], lhsT=xTs[:, kt, :], rhs=wsb[:, kt, :],
                                                     start=(kt == 0), stop=(kt == 7))),
                  reads=[b_xTs, b_wab], writes=[bbuf[1 + ci % 2]], sig=True)
        fw.op(S, (lambda ci=ci, pk=pk: act.activation(out=proj[:, ci * 512:(ci + 1) * 512], in_=pk[0:64, :], func=AF.Copy)),
              reads=[bbuf[1 + ci % 2]], writes=[b_proj])
    fw.op(P, lambda: pool.tensor_tensor(out=qk2, in0=proj[:, 0:1024], in1=proj[:, 0:1024], op=ALU.mult),
          reads=[b_proj], writes=[b_qk2])
    fw.op(V, lambda: vec.tensor_reduce(out=r16[:, 0:16], in_=qk2.rearrange("p (a d) -> p a d", d=64), axis=AX.X, op=ALU.add),
          reads=[b_qk2], writes=[b_r16])
    fw.op(S, lambda: act.activation(out=r16[:, 16:32], in_=r16[:, 0:16], func=AF.Ln, scale=1.0 / 64, bias=EPS),
          reads=[b_r16], writes=[b_r16])
    fw.op(S, lambda: act.activation(out=r16[:, 0:16], in_=r16[:, 16:32], func=AF.Exp, scale=-0.5), reads=[b_r16], writes=[b_r16])
    fw.op(V, lambda: vec.tensor_tensor(out=qkn.rearrange("p (a d) -> p a d", d=64),
                                       in0=proj[:, 0:1024].rearrange("p (a d) -> p a d", d=64),
                                       in1=r16[:, 0:16].unsqueeze(2).broadcast_to([64, 16, 64]), op=ALU.mult),
          reads=[b_proj, b_r16], writes=[b_qkn])
    fw.op(V, lambda: vec.tensor_tensor(out=qkn, in0=qkn, in1=g16, op=ALU.mult), reads=[b_qkn, b_g16], writes=[b_qkn])
    fw.dma(P, ks_o[:, :], qkn[:, 512:1024], reads=[b_qkn], writes=[b_kso])
    fw.dma(P, vs_o[:, :], proj[:, 1024:1536], reads=[b_proj], writes=[b_vso])
    QsT0, b_QsT = ar.alloc("QsT0", [128, 4, 64], BF16)
    QsT1, _ = ar.alloc("QsT1", [128, 4, 64], BF16)
    KnT, _ = ar.alloc("KnT", [128, 4, 64], BF16)
    fw.op(P, lambda: pool.memset(QsT0[64:128], 0.0), writes=[b_QsT])
    fw.op(P, lambda: pool.memset(QsT1[0:64], 0.0), writes=[b_QsT])
    fw.op(P, lambda: pool.tensor_copy(out=qkb, in_=qkn), reads=[b_qkn], writes=[b_qkb])
    for j in range(8):
        fw.op(T, (lambda j=j: pe.transpose(out=pTs[:, j * 64:(j + 1) * 64], in_=qkb[:, j * 128:(j + 1) * 128],
                                           identity=ident[0:64, 0:64])), reads=[b_qkb, b_ident], writes=[bbuf[0]])
    pT3 = pTs[:, 0:512].rearrange("p (a b) -> p a b", a=8)
    fw.op(V, lambda: vec.tensor_copy(out=QsT0[0:64], in_=pT3[0:64, 0:4, :]), reads=[bbuf[0]], writes=[b_QsT])
    fw.op(V, lambda: vec.tensor_copy(out=QsT1[64:128], in_=pT3[64:128, 0:4, :]), reads=[bbuf[0]], writes=[b_QsT])
    fw.op(V, lambda: vec.tensor_copy(out=KnT, in_=pT3[:, 4:8, :]), reads=[bbuf[0]], writes=[b_QsT])

    r16s = s5_setup("smp", 16, s5a16_d, s5b16_d, s5c16_d, ar.alloc)
    ub, b_ub = ar.alloc("ub", [64, 512], BF16)
    uTs, b_uTs = ar.alloc("uTs", [128, 4, 64], BF16)
    fw.op(P, lambda: pool.tensor_copy(out=ub, in_=proj[:, 2048:2560]), reads=[b_proj], writes=[b_ub])
    for j in range(4):
        fw.op(T, (lambda j=j: pe.transpose(out=pTs[:, 512 + j * 64:512 + (j + 1) * 64], in_=ub[:, j * 128:(j + 1) * 128],
                                           identity=ident[0:64, 0:64])), reads=[b_ub, b_ident], writes=[bbuf[0]])
    fw.op(V, lambda: vec.tensor_copy(out=uTs.rearrange("p a b -> p (a b)"), in_=pTs[:, 512:768]), reads=[bbuf[0]], writes=[b_uTs])
    for ri in range(2):
        for pr in range(16):
            bk = 4 + 2 * ri + pr // 8
            fw.op(T, (lambda ri=ri, pr=pr, bk=bk: pe.matmul(banks[bk][:, (pr % 8) * 64:(pr % 8 + 1) * 64],
                                                            lhsT=r16s.BT[:, ri * 16 + pr, :], rhs=uTs[:, pr // 4, :],
                                                            start=True, stop=True)),
                  reads=[r16s.b_BT, b_uTs], writes=[bbuf[bk]])
    hst, b_hst = ar.alloc("hst", [128, 2, 16, NS], F32)
    fw.dma(SP, hst, hst_d[:, :, :, :], writes=[b_hst])
    hs, b_hs = ar.alloc("hs", [128, 2, 16, 64], BF16)
    tq = [ar.alloc("tq%d" % i, [128, 16, NS], F32) for i in range(4)]
    LRb = r16s.LR.unsqueeze(2).broadcast_to([128, 16, NS])
    LIb = r16s.LI.unsqueeze(2).broadcast_to([128, 16, NS])
    def bu_t(ri, t, half):
        bk = 4 + 2 * ri + half
        return banks[bk][:, :].rearrange("p (a s t) -> p a s t", a=8, t=4)[:, :, :, t]
    for t in range(4):
        hr, hi = hst[:, 0], hst[:, 1]
        rd = [b_hst, r16s.b_W]
        fw.op(V, lambda: vec.tensor_tensor(out=tq[0][0], in0=hr, in1=LRb, op=ALU.mult), reads=rd, writes=[tq[0][1]])
        fw.op(P, lambda: pool.tensor_tensor(out=tq[1][0], in0=hi, in1=LIb, op=ALU.mult), reads=rd, writes=[tq[1][1]])
        fw.op(V, lambda: vec.tensor_tensor(out=tq[2][0], in0=hi, in1=LRb, op=ALU.mult), reads=rd, writes=[tq[2][1]])
        fw.op(P, lambda: pool.tensor_tensor(out=tq[3][0], in0=hr, in1=LIb, op=ALU.mult), reads=rd, writes=[tq[3][1]])
        fw.op(V, lambda: vec.tensor_tensor(out=tq[0][0], in0=tq[0][0], in1=tq[1][0], op=ALU.subtract),
              reads=[tq[0][1], tq[1][1]], writes=[tq[0][1]])
        fw.op(V, lambda: vec.tensor_tensor(out=tq[2][0], in0=tq[2][0], in1=tq[3][0], op=ALU.add),
              reads=[tq[2][1], tq[3][1]], writes=[tq[2][1]])
        for half in range(2):
            sl = slice(half * 8, (half + 1) * 8)
            fw.op(V, (lambda t=t, half=half, sl=sl: vec.tensor_tensor(out=hst[:, 0, sl, :], in0=bu_t(0, t, half),
                                                                      in1=tq[0][0][:, sl, :], op=ALU.add)),
                  reads=[bbuf[4 + half], tq[0][1]], writes=[b_hst])
            fw.op(V, (lambda t=t, half=half, sl=sl: vec.tensor_tensor(out=hst[:, 1, sl, :], in0=bu_t(1, t, half),
                                                                      in1=tq[2][0][:, sl, :], op=ALU.add)),
                  reads=[bbuf[6 + half], tq[2][1]], writes=[b_hst])
        fw.op(P, (lambda t=t: pool.tensor_copy(out=hs.rearrange("p r a (s t) -> p r a s t", t=4)[:, :, :, :, t], in_=hst)),
              reads=[b_hst], writes=[b_hs])
    fw.dma(P, sss_o[:, :, :, :], hst, reads=[b_hst], writes=[b_ssso])
    yS = banks[2]
    for pr in range(16):
        fw.op(T, (lambda pr=pr: pe.matmul(yS[0:64, pr * 32:(pr + 1) * 32], lhsT=hs[:, 0, pr, :], rhs=r16s.Cb[:, 0, pr, :],
                                          start=True, stop=False)), reads=[b_hs, r16s.b_Cb], writes=[bbuf[2]], sig=False)
        fw.op(T, (lambda pr=pr: pe.matmul(yS[0:64, pr * 32:(pr + 1) * 32], lhsT=hs[:, 1, pr, :], rhs=r16s.Cb[:, 1, pr, :],
                                          start=False, stop=True)), reads=[b_hs, r16s.b_Cb], writes=[bbuf[2]], sig=True)
    d16, b_d16 = ar.alloc("d16", [64, 512], F32)
    ozs, b_ozs = ar.alloc("ozs", [64, 1024], F32)
    zt = [ar.alloc("zt%d" % i, [64, 512], F32) for i in range(3)]
    fw.dma(SP, d16, dsk16_d.partition_broadcast(64), writes=[b_d16])
    y1, y2, y3 = zt[0][0], zt[1][0], zt[2][0]
    bz = [z_[1] for z_ in zt]
    fw.op(P, lambda: pool.tensor_tensor(out=y2, in0=proj[:, 2048:2560], in1=d16, op=ALU.mult), reads=[b_proj, b_d16], writes=[bz[1]])
    fw.op(V, lambda: vec.tensor_tensor(out=y1, in0=yS[0:64, :], in1=y2, op=ALU.add), reads=[bbuf[2], bz[1]], writes=[bz[0]])
    fw.op(P, lambda: pool.tensor_tensor(out=y2, in0=y1, in1=y1, op=ALU.mult), reads=[bz[0]], writes=[bz[1]])
    fw.op(V, lambda: vec.tensor_scalar(out=y2, in0=y2, scalar1=0.044715, scalar2=1.0, op0=ALU.mult, op1=ALU.add),
          reads=[bz[1]], writes=[bz[1]])
    fw.op(P, lambda: pool.tensor_tensor(out=y3, in0=y2, in1=y1, op=ALU.mult), reads=[bz[0], bz[1]], writes=[bz[2]])
    fw.op(S, lambda: act.activation(out=y3, in_=y3, func=AF.Exp, scale=-1.5957691216057308), reads=[bz[2]], writes=[bz[2]])
    fw.op(V, lambda: vec.tensor_scalar(out=y3, in0=y3, scalar1=1.0, scalar2=None, op0=ALU.add), reads=[bz[2]], writes=[bz[2]])
    fw.op(V, lambda: vec.reciprocal(out=y3, in_=y3), reads=[bz[2]], writes=[bz[2]])
    fw.op(V, lambda: vec.tensor_tensor(out=ozs[:, 512:1024], in0=y1, in1=y3, op=ALU.mult), reads=[bz[0], bz[2]], writes=[b_ozs])

    pti, b_pti = ar.alloc("pti", [128, 256], I32)
    ptf, b_ptf = ar.alloc("ptf", [128, 256], F32)
    idx, b_idx = ar.alloc("idx", [128, 256], I32)
    pcol, b_pcol = ar.alloc("pcol", [128, 8], F32)
    fw.dma(SP, pti, pt_d.partition_broadcast(128), writes=[b_pti])
    fw.dma(SP, pcol[:, 0:1], pcol_d[:, :], writes=[b_pcol])
    fw.op(V, lambda: vec.tensor_copy(out=ptf, in_=pti), reads=[b_pti], writes=[b_ptf])
    fw.op(V, lambda: vec.tensor_scalar(out=ptf, in0=ptf, scalar1=128.0, scalar2=pcol[:, 0:1], op0=ALU.mult, op1=ALU.add),
          reads=[b_ptf, b_pcol], writes=[b_ptf])
    fw.op(V, lambda: vec.tensor_copy(out=idx, in_=ptf), reads=[b_ptf], writes=[b_idx])
    bfar, b_bfar = ar.alloc("bfar", [128, 32], F32)
    b15, b_b15 = ar.alloc("b15", [128, 32], F32)
    bN, b_bN = ar.alloc("bN", [4, 32], F32)
    oh15, b_oh15 = ar.alloc("oh15", [32, 4, 128], F32)
    ohn, b_ohn = ar.alloc("ohn", [33, 4, 4], F32)
    rb33, b_rb33 = ar.alloc("rb33", [33, 4], F32)
    Dm, b_Dm = ar.alloc("Dm", [8, 8], F32)
    fw.dma(SP, bfar, rbfar_d.partition_broadcast(128), writes=[b_bfar])
    fw.dma(SP, oh15, oh15_d[:, :, :], writes=[b_oh15])
    fw.dma(SP, ohn, ohn_d[:, :, :], writes=[b_ohn])
    fw.dma(SP, rb33, rb33_d[:, :], writes=[b_rb33])
    fw.dma(SP, Dm, dm_d[:, :], writes=[b_Dm])
    fw.op(V, lambda: vec.scalar_tensor_tensor(out=Dm[:, 0:4], in0=Dm[:, 4:8], scalar=lamw[0:8, 5:6], in1=Dm[:, 0:4],
                                              op0=ALU.mult, op1=ALU.add), reads=[b_Dm, b_lamw], writes=[b_Dm])
    for qi in range(4):
        fw.op(T, (lambda qi=qi: pe.matmul(banks[3][:, qi * 4:(qi + 1) * 4], lhsT=oh15[:, qi, :], rhs=rb33[0:32, :],
                                          start=True, stop=True)), reads=[b_oh15, b_rb33], writes=[bbuf[3]])
        fw.op(T, (lambda qi=qi: pe.matmul(banks[3][0:4, 64 + qi * 4:64 + (qi + 1) * 4], lhsT=ohn[:, qi, :], rhs=rb33[:, :],
                                          start=True, stop=True)), reads=[b_ohn, b_rb33], writes=[bbuf[3]])
    for c in range(2):
        fw.op(V, (lambda c=c: vec.tensor_copy(
            out=b15.rearrange("p (h c q) -> p h c q", h=4, c=2)[:, :, c, :],
            in_=banks[3][:, 0:16].rearrange("p (q h) -> p h q", q=4))), reads=[bbuf[3]], writes=[b_b15])
        fw.op(V, (lambda c=c: vec.tensor_copy(
            out=bN.rearrange("p (h c q) -> p h c q", h=4, c=2)[:, :, c, :],
            in_=banks[3][0:4, 64:80].rearrange("p (q h) -> p h q", q=4))), reads=[bbuf[3]], writes=[b_bN])
    kfb = [ar.alloc("kf%d" % i, [128, 512], F32) for i in range(4)]
    vpb = [ar.alloc("vp%d" % i, [128, 4, 130], F32) for i in range(3)]
    print("arena remaining:", [r.shape[1] - o for r, o in zip(ar.regions, ar.offs)])
    vfb = [ar.alloc("vf%d" % i, [128, 512], F32) for i in range(3)]
    kbb = [ar.alloc("kb%d" % i, [128, 512], BF16) for i in range(2)]
    kTb = [ar.alloc("kT%d" % i, [128, 4, 128], BF16) for i in range(2)]
    sbb = [ar.alloc("sb%d" % i, [128, 32], F32) for i in range(2)]
    ptb_ = [ar.alloc("pts%d" % i, [128, 32], F32) for i in range(2)]
    vn, b_vn = ar.alloc("vn", [4, 4, 130], F32)
    onr, b_onr = ar.alloc("onr", [8, 4, 128], F32)
    rl8, b_rl8 = ar.alloc("rl8", [8, 4], F32)
    osb, b_osb = zt[2][0][0:4, :], zt[2][1]
    for i in range(3):
        fw.op(P, (lambda i=i: pool.memset(vpb[i][0][:, :, 128:130], 1.0)), writes=[vpb[i][1]])
    fw.op(P, lambda: pool.memset(vn[:, :, 128:130], 1.0), writes=[b_vn])
    ck2 = ck_d
    cv3 = cv_d.rearrange("r (h e) -> r h e", h=4)
    QsTm = (QsT0, QsT1)
    b_pKs = [Buf("pK0"), Buf("pK1")]

    def smp_page(sq, pg, pg_i):
        fw.stage = 1
        tok = slice(sq * 4, sq * 4 + 4)
        Sp = banks[pg_i % 2]
        b_Sp = bbuf[pg_i % 2]
        sb_, b_sb = sbb[pg_i % 2]
        pts, b_pts = ptb_[pg_i % 2]
        if pg < 16:
            kf, b_kf = kfb[pg_i % 4]
            vp_, b_vp = vpb[pg_i % 3]
            kb, b_kb = kbb[pg_i % 2]
            kT, b_kT = kTb[pg_i % 2]
            col = sq * 16 + pg
            fw.idma(kf, ck2, idx[:, col:col + 1], reads=[b_idx], writes=[b_kf])
            vf, b_vf = vfb[pg_i % 3]
            fw.idma(vf, cv_d, idx[:, col:col + 1], reads=[b_idx], writes=[b_vf])
            fw.op(S, lambda: act.activation(out=vp_[:, :, 0:128], in_=vf.rearrange("p (h e) -> p h e", h=4), func=AF.Copy),
                  reads=[b_vf], writes=[b_vp])
            fw.op(S, lambda: act.activation(out=kb, in_=kf, func=AF.Copy), reads=[b_kf], writes=[b_kb])
            fw.stage = 2
            pK = banks[2][:, (pg_i % 2) * 256:(pg_i % 2 + 1) * 256].bitcast(BF16)
            for h in range(4):
                fw.op(T, (lambda h=h, pK=pK, kb=kb: pe.transpose(out=pK[:, h * 128:(h + 1) * 128],
                                                               in_=kb[:, h * 128:(h + 1) * 128], identity=ident[:])),
                      reads=[b_kb, b_ident], writes=[b_pKs[pg_i % 2]], sig=(h == 3))
            fw.op(V, lambda: vec.tensor_copy(out=kT.rearrange("p a b -> p (a b)"), in_=pK[:, :]), reads=[b_pKs[pg_i % 2]], writes=[b_kT])
            fw.stage = 3
            for h in range(4):
                for c in range(2):
                    fw.op(T, (lambda h=h, c=c, kT=kT, Sp=Sp: pe.matmul(
                        Sp[:, h * 8 + c * 4:h * 8 + c * 4 + 4], lhsT=kT[:, h, :], rhs=QsTm[c][:, h, tok],
                        start=True, stop=True)), reads=[b_kT, b_QsT], writes=[b_Sp], sig=(h == 3 and c == 1))
            fw.stage = 3
            bias_t, b_bias = (b15, b_b15) if pg == 15 else (bfar, b_bfar)
            fw.op(V, lambda: vec.tensor_tensor(out=sb_, in0=Sp[:, 0:32], in1=bias_t, op=ALU.add),
                  reads=[b_Sp, b_bias], writes=[b_sb])
            fw.op(S, lambda: act.activation(out=pts, in_=sb_, func=AF.Exp), reads=[b_sb], writes=[b_pts])
            fw.stage = 4
            for h in range(4):
                fw.op(T, (lambda h=h, pts=pts, vp_=vp_, pg=pg: pe.matmul(
                    banks[4 + h][0:8, 0:129], lhsT=pts[:, h * 8:(h + 1) * 8], rhs=vp_[:, h, 0:129],
                    start=(pg == 0), stop=False)), reads=[b_pts, b_vp], writes=[bbuf[4 + h]], sig=True)
        else:
            fw.dma(SP, vn[:, :, 0:128], vs_o[tok, :].rearrange("t (h e) -> t h e", h=4), reads=[b_vso], writes=[b_vn])
            fw.stage = 3
            for h in range(4):
                for c in range(2):
                    fw.op(T, (lambda h=h, c=c, Sp=Sp: pe.matmul(
                        Sp[0:4, h * 8 + c * 4:h * 8 + c * 4 + 4], lhsT=KnT[:, h, tok], rhs=QsTm[c][:, h, tok],
                        start=True, stop=True)), reads=[b_QsT], writes=[b_Sp], sig=(h == 3 and c == 1))
            fw.stage = 3
            fw.op(V, lambda: vec.tensor_tensor(out=sb_[0:4, :], in0=Sp[0:4, 0:32], in1=bN, op=ALU.add),
                  reads=[b_Sp, b_bN], writes=[b_sb])
            fw.op(S, lambda: act.activation(out=pts[0:4, :], in_=sb_[0:4, :], func=AF.Exp), reads=[b_sb], writes=[b_pts])
            fw.stage = 4
            for h in range(4):
                fw.op(T, (lambda h=h, pts=pts: pe.matmul(
                    banks[4 + h][0:8, 0:129], lhsT=pts[0:4, h * 8:(h + 1) * 8], rhs=vn[:, h, 0:129],
                    start=False, stop=True)), reads=[b_pts, b_vn], writes=[bbuf[4 + h]], sig=True)

    def smp_fin(sq):
        tok = slice(sq * 4, sq * 4 + 4)
        for h in range(4):
            fw.op(V, (lambda h=h: vec.reciprocal(out=rl8[:, h:h + 1], in_=banks[4 + h][0:8, 128:129])),
                  reads=[bbuf[4 + h]], writes=[b_rl8])
            fw.op(V, (lambda h=h: vec.tensor_scalar(out=onr[:, h, :], in0=banks[4 + h][0:8, 0:128], scalar1=rl8[:, h:h + 1],
                                                    scalar2=None, op0=ALU.mult)), reads=[bbuf[4 + h], b_rl8], writes=[b_onr])
        for h in range(4):
            fw.op(T, (lambda h=h: pe.matmul(banks[3][0:4, h * 128:(h + 1) * 128], lhsT=Dm[:, 0:4], rhs=onr[:, h, :],
                                            start=True, stop=True)), reads=[b_Dm, b_onr], writes=[bbuf[3]])
        fw.op(V, lambda: vec.tensor_copy(out=osb, in_=banks[3][0:4, :]), reads=[bbuf[3]], writes=[b_osb])
        fw.dma(SP, oraw[tok, :], osb, reads=[b_osb], writes=[b_oraw])

    pl2 = [(sq, pg) for sq in range(NS) for pg in range(17)]
    for k in range(-3, len(pl2)):
        for st, off in ((4, 0), (3, 1), (2, 2), (1, 3)):
            n = k + off
            if 0 <= n < len(pl2):
                fw.only = st
                smp_page(pl2[n][0], pl2[n][1], n)
                if st == 4 and pl2[n][1] == 16:
                    fw.only = None
                    smp_fin(pl2[n][0])
    fw.only = None

    ot, b_ot = kfb[0][0][0:64, :], kfb[0][1]
    fw.dma(SP, ot, oraw[:, :], reads=[b_oraw], writes=[b_ot])
    sg4, b_sg4 = kfb[1][0][0:64, :], kfb[1][1]
    fw.dma(SP, sg4, sg4_d.partition_broadcast(64), writes=[b_sg4])
    gt = [zt[0], zt[1]]
    g1, g2 = gt[0][0], gt[1][0]
    bg = [gt[0][1], gt[1][1]]
    gaS = proj[:, 1536:2048]
    fw.op(S, lambda: act.activation(out=g1, in_=gaS, func=AF.Exp, scale=-1.0), reads=[b_proj], writes=[bg[0]])
    fw.op(V, lambda: vec.tensor_scalar(out=g1, in0=g1, scalar1=1.0, scalar2=None, op0=ALU.add), reads=[bg[0]], writes=[bg[0]])
    fw.op(V, lambda: vec.reciprocal(out=g1, in_=g1), reads=[bg[0]], writes=[bg[0]])
    fw.op(V, lambda: vec.tensor_tensor(out=g1, in0=g1, in1=gaS, op=ALU.mult), reads=[bg[0], b_proj], writes=[bg[0]])
    fw.op(V, lambda: vec.tensor_tensor(out=g1, in0=g1, in1=sg4, op=ALU.mult), reads=[bg[0], b_sg4], writes=[bg[0]])
    fw.op(V, lambda: vec.tensor_scalar(out=g1, in0=g1, scalar1=0.8, scalar2=None, op0=ALU.mult), reads=[bg[0]], writes=[bg[0]])
    fw.op(P, lambda: pool.tensor_tensor(out=g2, in0=ot, in1=ot, op=ALU.mult), reads=[b_ot], writes=[bg[1]])
    fw.op(V, lambda: vec.tensor_reduce(out=sss[:, 0:4], in_=g2.rearrange("p (a d) -> p a d", d=128), axis=AX.X, op=ALU.add),
          reads=[bg[1]], writes=[b_sss])
    fw.op(S, lambda: act.activation(out=sss[:, 4:8], in_=sss[:, 0:4], func=AF.Ln, scale=1.0 / 128, bias=EPS),
          reads=[b_sss], writes=[b_sss])
    fw.op(S, lambda: act.activation(out=sss[:, 0:4], in_=sss[:, 4:8], func=AF.Exp, scale=-0.5), reads=[b_sss], writes=[b_sss])
    fw.op(V, lambda: vec.tensor_tensor(out=g2.rearrange("p (a d) -> p a d", d=128), in0=ot.rearrange("p (a d) -> p a d", d=128),
                                       in1=sss[:, 0:4].unsqueeze(2).broadcast_to([64, 4, 128]), op=ALU.mult),
          reads=[b_ot, b_sss], writes=[bg[1]])
    fw.op(V, lambda: vec.tensor_tensor(out=ozs[:, 0:512], in0=g2, in1=g1, op=ALU.mult), reads=[bg[0], bg[1]], writes=[b_ozs])
    fw.dma(P, ozs_scr[:, :], ozs, reads=[b_ozs], writes=[b_ozscr])

    fw.barrier()
    ar2 = Arena(ar.regions)
    wgs, b_wgs = ar2.alloc("wgs", [128, 8, 512], BF16)
    wglu, b_wglu = ar2.alloc("wglu", [128, 4, 512], BF16)
    wout, b_wout = ar2.alloc("wout", [128, 8, 1024], BF16)
    bglu, b_bglu = ar2.alloc("bglu", [128, 512], F32)
    idxb, b_idxb = ar2.alloc("idxb", [128, 64], I32)
    fw.dma(SP, bglu, bglu_d.partition_broadcast(128), writes=[b_bglu])
    fw.dma(SP, idxb, idxb_d[:, :], writes=[b_idxb])
    fw.dma(SP, wsf, wgs_d[:, :, :], reads=[b_wab], writes=[b_wst])
    for kt in range(8):
        fw.op(V if kt % 2 else P, (lambda kt=kt: (vec if kt % 2 else pool).tensor_scalar(
            out=wgs[:, kt, :], in0=wsf[:, kt, :], scalar1=ngt[:, kt:kt + 1], scalar2=None, op0=ALU.mult)),
            reads=[b_wst, b_ng], writes=[b_wgs])
    fw.dma(SP, wsf[:, 0:4, :], wglu_d[:, :, :], writes=[b_wst])
    fw.op(V, lambda: vec.tensor_copy(out=wglu, in_=wsf[:, 0:4, :]), reads=[b_wst], writes=[b_wglu])
    for hf in range(2):
        fw.dma(SP, wsf, wout_d[:, hf, :, :], writes=[b_wst])
        for kt in range(8):
            fw.op(V if kt % 2 else P, (lambda kt=kt, hf=hf: (vec if kt % 2 else pool).tensor_copy(
                out=wout[:, kt, hf * 512:(hf + 1) * 512], in_=wsf[:, kt, :])), reads=[b_wst], writes=[b_wout])
    Bx = [ar2.alloc("Bx%d" % i, [128, DM], F32) for i in range(3)]
    Bgst = [ar2.alloc("Bg%d" % i, [128, 4, 256], F32) for i in range(2)]
    Boz = [ar2.alloc("Boz%d" % i, [128, DM], F32) for i in range(2)]
    Bxn, b_Bxn = ar2.alloc("Bxn", [128, DM], BF16)
    BxT, b_BxT = ar2.alloc("BxT", [128, 8, 128], BF16)
    Bss, b_Bss = ar2.alloc("Bss", [128, 4], F32)
    Bsg, b_Bsg = ar2.alloc("Bsg", [128, 512], F32)
    Bzb, b_Bzb = ar2.alloc("Bzb", [128, 512], BF16)
    BzT, b_BzT = ar2.alloc("BzT", [128, 4, 128], BF16)
    Bpr, b_Bpr = ar2.alloc("Bpr", [128, 512], F32)
    Bobs = [ar2.alloc("Bob%d" % i, [128, DM], BF16) for i in range(2)]
    BoT, b_BoT = ar2.alloc("BoT", [128, 8, 128], BF16)
    By, b_By = ar2.alloc("By", [128, DM], F32)
    fw.op(P, lambda: pool.memset(Boz[0][0], 0.0), writes=[Boz[0][1]])
    def phb(t):
        fw.stage = 1
        rows = slice(t * 128, (t + 1) * 128)
        xt, b_xt = Bx[t % 3]
        Bob, b_Bob = Bobs[t % 2]
        gst, b_gst = Bgst[t % 2]
        ozt, b_oz = Boz[t % 2]
        pT = banks[0][:, :].bitcast(BF16); b_pT = bbuf[0]
        pG = banks[1 if t % 2 == 0 else 7]; b_pG = bbuf[1 if t % 2 == 0 else 7]
        pL = banks[2]; b_pL = bbuf[2]
        pO = banks[3][:, :].bitcast(BF16); b_pO = bbuf[3]
        pZ = banks[6][:, :].bitcast(BF16); b_pZ = bbuf[6]
        pY = (banks[4], banks[5]); b_pY = (bbuf[4], bbuf[5])
        fw.dma(SP, xt, xq[rows, :], writes=[b_xt])
        if t < 16:
            for r in range(4):
                fw.idma(gst[:, r, :], agout, idxb[:, t * 4 + r:t * 4 + r + 1], reads=[b_idxb, b_agout], writes=[b_gst])
            fw.op(P, lambda: pool.tensor_copy(out=ozt[:, 0:512].rearrange("p (h e) -> p h e", h=4), in_=gst[:, :, 0:128]),
                  reads=[b_gst], writes=[b_oz])
            fw.op(V, lambda: vec.tensor_copy(out=ozt[:, 512:1024].rearrange("p (h e) -> p h e", h=4), in_=gst[:, :, 128:256]),
                  reads=[b_gst], writes=[b_oz])
        else:
            fw.dma(SP, ozt[0:64, :], ozs_scr[:, :], reads=[b_ozscr], writes=[b_oz])
        fw.op(S, lambda: act.activation(out=junk[:], in_=xt, func=AF.Square, accum_out=Bss[:, 0:1]),
              reads=[b_xt], writes=[b_junk, b_Bss])
        fw.op(S, lambda: act.activation(out=Bss[:, 1:2], in_=Bss[:, 0:1], func=AF.Ln, scale=1.0 / DM, bias=EPS),
              reads=[b_Bss], writes=[b_Bss])
        fw.op(S, lambda: act.activation(out=Bss[:, 2:3], in_=Bss[:, 1:2], func=AF.Exp, scale=-0.5), reads=[b_Bss], writes=[b_Bss])
        fw.op(S, lambda: act.activation(out=Bxn, in_=xt, func=AF.Copy, scale=Bss[:, 2:3]), reads=[b_xt, b_Bss], writes=[b_Bxn])
        for kt in range(8):
            fw.op(T, (lambda kt=kt: pe.transpose(out=pT[:, kt * 128:(kt + 1) * 128], in_=Bxn[:, kt * 128:(kt + 1) * 128],
                                                 identity=ident[:])), reads=[b_Bxn, b_ident], writes=[b_pT], sig=(kt == 7))
        fw.op(V, lambda: vec.tensor_copy(out=BxT.rearrange("p a b -> p (a b)"), in_=pT[:, :]), reads=[b_pT], writes=[b_BxT])
        for kt in range(8):
            fw.op(T, (lambda kt=kt: pe.matmul(pG[:, :], lhsT=BxT[:, kt, :], rhs=wgs[:, kt, :], start=(kt == 0), stop=(kt == 7))),
                  reads=[b_BxT, b_wgs], writes=[b_pG], sig=(kt == 7))
        fw.stage = 2
        fw.op(S, lambda: act.activation(out=Bsg, in_=pG[:, :], func=AF.Exp, scale=-1.0), reads=[b_pG], writes=[b_Bsg])
        fw.op(S, lambda: act.activation(out=Bsg, in_=Bsg, func=AF.Ln, bias=1.0), reads=[b_Bsg], writes=[b_Bsg])
        fw.op(S, lambda: act.activation(out=Bsg, in_=Bsg, func=AF.Exp, scale=-1.0), reads=[b_Bsg], writes=[b_Bsg])
        fw.op(V, lambda: vec.tensor_tensor(out=Bsg, in0=pG[:, :], in1=Bsg, op=ALU.mult), reads=[b_pG, b_Bsg], writes=[b_Bsg])
        fw.op(P, lambda: pool.tensor_copy(out=Bzb, in_=ozt[:, 512:1024]), reads=[b_oz], writes=[b_Bzb])
        for kt in range(4):
            fw.op(T, (lambda kt=kt: pe.transpose(out=pZ[:, kt * 128:(kt + 1) * 128], in_=Bzb[:, kt * 128:(kt + 1) * 128],
                                                 identity=ident[:])), reads=[b_Bzb, b_ident], writes=[b_pZ], sig=(kt == 3))
        fw.op(V, lambda: vec.tensor_copy(out=BzT.rearrange("p a b -> p (a b)"), in_=pZ[:, 0:512]), reads=[b_pZ], writes=[b_BzT])
        for kt in range(4):
            fw.op(T, (lambda kt=kt: pe.matmul(pL[:, :], lhsT=BzT[:, kt, :], rhs=wglu[:, kt, :], start=(kt == 0), stop=(kt == 3))),
                  reads=[b_BzT, b_wglu], writes=[b_pL], sig=(kt == 3))
        fw.op(V, lambda: vec.tensor_tensor(out=Bpr, in0=pL[:, :], in1=bglu, op=ALU.add), reads=[b_pL, b_bglu], writes=[b_Bpr])
        fw.op(S, lambda: act.activation(out=Bpr, in_=Bpr, func=AF.Exp, scale=-1.0), reads=[b_Bpr], writes=[b_Bpr])
        fw.op(S, lambda: act.activation(out=Bpr, in_=Bpr, func=AF.Ln, bias=1.0), reads=[b_Bpr], writes=[b_Bpr])
        fw.op(S, lambda: act.activation(out=Bpr, in_=Bpr, func=AF.Exp, scale=-1.0), reads=[b_Bpr], writes=[b_Bpr])
        fw.op(P, lambda: pool.tensor_tensor(out=Bpr, in0=Bpr, in1=ozt[:, 512:1024], op=ALU.mult), reads=[b_Bpr, b_oz], writes=[b_Bpr])
        fw.op(P, lambda: pool.tensor_tensor(out=Bob[:, 512:1024], in0=Bpr, in1=Bsg, op=ALU.mult), reads=[b_Bpr, b_Bsg], writes=[b_Bob])
        fw.op(P, lambda: pool.tensor_copy(out=Bob[:, 0:512], in_=ozt[:, 0:512]), reads=[b_oz], writes=[b_Bob])
        fw.stage = 3
        for kt in range(8):
            fw.op(T, (lambda kt=kt: pe.transpose(out=pO[:, kt * 128:(kt + 1) * 128], in_=Bob[:, kt * 128:(kt + 1) * 128],
                                                 identity=ident[:])), reads=[b_Bob, b_ident], writes=[b_pO], sig=(kt == 7))
        fw.op(V, lambda: vec.tensor_copy(out=BoT.rearrange("p a b -> p (a b)"), in_=pO[:, :]), reads=[b_pO], writes=[b_BoT])
        for nb in range(2):
            for kt in range(8):
                fw.op(T, (lambda kt=kt, nb=nb: pe.matmul(pY[nb][:, :], lhsT=BoT[:, kt, :], rhs=wout[:, kt, nb * 512:(nb + 1) * 512],
                                                         start=(kt == 0), stop=(kt == 7))),
                      reads=[b_BoT, b_wout], writes=[b_pY[nb]], sig=(kt == 7))
            fw.op(V, (lambda nb=nb: vec.tensor_tensor(out=By[:, nb * 512:(nb + 1) * 512], in0=pY[nb][:, :],
                                                      in1=xt[:, nb * 512:(nb + 1) * 512], op=ALU.add)),
                  reads=[b_pY[nb], b_xt], writes=[b_By])
        fw.dma(P, yq[rows, :], By, reads=[b_By], writes=[b_yq])
    for k in range(-2, NTB):
        for st, off in ((3, 0), (2, 1), (1, 2)):
            n = k + off
            if 0 <= n < NTB:
                fw.only = st
                phb(n)
    fw.only = None
    fw.barrier()
    return nc


NTB = 17


def build_b():
    nc = bass.Bass("TRN2", target_bir_lowering=False)
    g = B()
    g.nc = nc
    g.uid = 0
    fw = FW(nc)
    V, S, P, T, SP = fw.dve, fw.act, fw.pool, fw.pe, fw.sp
    vec, act, pool, pe = nc.vector, nc.scalar, nc.gpsimd, nc.tensor

    def din(name, shape, dt=F32):
        return nc.dram_tensor(name, list(shape), dt, kind="ExternalInput").ap()
    xq = din("xq", [NTB * 128, DM])
    oz = din("oz", [NTB * 128, DM])
    wgs_d = din("wgs", [128, 8, 512])
    wglu_d = din("wglu", [128, 4, 512])
    wout_d = din("wout", [128, 8, 1024])
    bglu_d = din("bglu", [512])
    ng = din("ng", [128, 8])
    ident_d = din("ident", [128, 128])
    yq = nc.dram_tensor("yq", [NTB * 128, DM], F32, kind="ExternalOutput").ap()
    b_yq = Buf("yq")
    banks = [nc.alloc_psum_tensor("bank%d" % i, [128, 512], F32) for i in range(8)]
    bbuf = [Buf("bank%d" % i) for i in range(8)]

    ident_f, b_identf = _sb(g, "identf", [128, 128], F32)
    ident, b_ident = _sb(g, "ident", [128, 128], BF16)
    fw.dma(SP, ident_f[:], ident_d[:, :], writes=[b_identf])
    fw.op(V, lambda: vec.tensor_copy(out=ident[:], in_=ident_f[:]), reads=[b_identf], writes=[b_ident])
    ngt, b_ng = _sb(g, "ng", [128, 8], F32)
    fw.dma(SP, ngt[:], ng[:, :], writes=[b_ng])
    bglu, b_bglu = _sb(g, "bglu", [128, 512], F32)
    fw.dma(SP, bglu[:], bglu_d.partition_broadcast(128), writes=[b_bglu])
    wst, b_wst = _sb(g, "wst", [128, 8, 1024], F32)
    wgs, b_wgs = _sb(g, "wgs", [128, 8, 512], BF16)
    wglu, b_wglu = _sb(g, "wglu", [128, 4, 512], BF16)
    wout, b_wout = _sb(g, "wout", [128, 8, 1024], BF16)
    fw.dma(SP, wst[:, :, 0:512], wgs_d[:, :, :], writes=[b_wst])
    for kt in range(8):
        fw.op(V, (lambda kt=kt: vec.tensor_scalar(out=wgs[:, kt, :], in0=wst[:, kt, 0:512], scalar1=ngt[:, kt:kt + 1],
                                                  scalar2=None, op0=ALU.mult)), reads=[b_wst, b_ng], writes=[b_wgs])
    fw.dma(SP, wst[:, 0:4, 0:512], wglu_d[:, :, :], reads=[], writes=[b_wst])
    fw.op(V, lambda: vec.tensor_copy(out=wglu[:], in_=wst[:, 0:4, 0:512]), reads=[b_wst], writes=[b_wglu])
    fw.dma(SP, wst[:], wout_d[:, :, :], writes=[b_wst])
    for kt in range(8):
        fw.op(V if kt % 2 else P, (lambda kt=kt: (vec if kt % 2 else pool).tensor_copy(out=wout[:, kt, :], in_=wst[:, kt, :])),
              reads=[b_wst], writes=[b_wout])

    xb = _sb(g, "xt", [128, DM], F32, 2)
    ozb = _sb(g, "ozt", [128, DM], F32, 2)
    xnb = _sb(g, "xn", [128, DM], BF16, 2)
    xTb = _sb(g, "xT", [128, 8, 128], BF16, 2)
    junk, b_junk = _sb(g, "junk", [128, DM], BF16)
    ssb = _sb(g, "ss", [128, 4], F32, 2)
    sgb = _sb(g, "sg", [128, 512], F32, 2)
    zbb = _sb(g, "zb", [128, 512], BF16, 2)
    zTb = _sb(g, "zT", [128, 4, 128], BF16, 2)
    prb = _sb(g, "pr", [128, 512], F32, 2)
    obb = _sb(g, "ob", [128, DM], BF16, 2)
    oTb = _sb(g, "oT", [128, 8, 128], BF16, 2)
    yb = _sb(g, "y", [128, DM], F32, 2)
    for t in range(NTB):
        rows = slice(t * 128, (t + 1) * 128)
        xt, b_xt = xb[t % 2]
        ozt, b_oz = ozb[t % 2]
        xn, b_xn = xnb[t % 2]
        xT, b_xT = xTb[t % 2]
        ss, b_ss = ssb[t % 2]
        sg, b_sg = sgb[t % 2]
        zb, b_zb = zbb[t % 2]
        zT, b_zT = zTb[t % 2]
        pr, b_pr = prb[t % 2]
        ob, b_ob = obb[t % 2]
        oT, b_oT = oTb[t % 2]
        y, b_y = yb[t % 2]
        pT = banks[0][:, :].bitcast(BF16); b_pT = bbuf[0]
        pG = banks[1 if t % 2 == 0 else 7]; b_pG = bbuf[1 if t % 2 == 0 else 7]
        pL = banks[2]; b_pL = bbuf[2]
        pO = banks[3][:, :].bitcast(BF16); b_pO = bbuf[3]
        pZ = banks[6][:, :].bitcast(BF16); b_pZ = bbuf[6]
        pY = (banks[4], banks[5]); b_pY = (bbuf[4], bbuf[5])
        fw.dma(SP, xt[:], xq[rows, :], writes=[b_xt])
        fw.dma(SP, ozt[:], oz[rows, :], writes=[b_oz])
        fw.op(S, lambda: act.activation(out=junk[:], in_=xt[:], func=AF.Square, accum_out=ss[:, 0:1]),
              reads=[b_xt], writes=[b_junk, b_ss])
        fw.op(S, lambda: act.activation(out=ss[:, 1:2], in_=ss[:, 0:1], func=AF.Ln, scale=1.0 / DM, bias=EPS),
              reads=[b_ss], writes=[b_ss])
        fw.op(S, lambda: act.activation(out=ss[:, 2:3], in_=ss[:, 1:2], func=AF.Exp, scale=-0.5), reads=[b_ss], writes=[b_ss])
        fw.op(S, lambda: act.activation(out=xn[:], in_=xt[:], func=AF.Copy, scale=ss[:, 2:3]), reads=[b_xt, b_ss], writes=[b_xn])
        for kt in range(8):
            fw.op(T, (lambda kt=kt: pe.transpose(out=pT[:, kt * 128:(kt + 1) * 128], in_=xn[:, kt * 128:(kt + 1) * 128],
                                                 identity=ident[:])), reads=[b_xn, b_ident], writes=[b_pT], sig=(kt == 7))
        fw.op(V, lambda: vec.tensor_copy(out=xT[:].rearrange("p a b -> p (a b)"), in_=pT[:, :]), reads=[b_pT], writes=[b_xT])
        for kt in range(8):
            fw.op(T, (lambda kt=kt: pe.matmul(pG[:, :], lhsT=xT[:, kt, :], rhs=wgs[:, kt, :], start=(kt == 0), stop=(kt == 7))),
                  reads=[b_xT, b_wgs], writes=[b_pG], sig=(kt == 7))
        fw.op(S, lambda: act.activation(out=sg[:], in_=pG[:, :], func=AF.Exp, scale=-1.0), reads=[b_pG], writes=[b_sg])
        fw.op(V, lambda: vec.tensor_scalar(out=sg[:], in0=sg[:], scalar1=1.0, scalar2=None, op0=ALU.add), reads=[b_sg], writes=[b_sg])
        fw.op(V, lambda: vec.reciprocal(out=sg[:], in_=sg[:]), reads=[b_sg], writes=[b_sg])
        fw.op(V, lambda: vec.tensor_tensor(out=sg[:], in0=pG[:, :], in1=sg[:], op=ALU.mult), reads=[b_pG, b_sg], writes=[b_sg])
        fw.op(P, lambda: pool.tensor_copy(out=zb[:], in_=ozt[:, 512:1024]), reads=[b_oz], writes=[b_zb])
        for kt in range(4):
            fw.op(T, (lambda kt=kt: pe.transpose(out=pZ[:, kt * 128:(kt + 1) * 128], in_=zb[:, kt * 128:(kt + 1) * 128],
                                                 identity=ident[:])), reads=[b_zb, b_ident], writes=[b_pZ], sig=(kt == 3))
        fw.op(V, lambda: vec.tensor_copy(out=zT[:].rearrange("p a b -> p (a b)"), in_=pZ[:, 0:512]), reads=[b_pZ], writes=[b_zT])
        for kt in range(4):
            fw.op(T, (lambda kt=kt: pe.matmul(pL[:, :], lhsT=zT[:, kt, :], rhs=wglu[:, kt, :], start=(kt == 0), stop=(kt == 3))),
                  reads=[b_zT, b_wglu], writes=[b_pL], sig=(kt == 3))
        fw.op(V, lambda: vec.tensor_tensor(out=pr[:], in0=pL[:, :], in1=bglu[:], op=ALU.add), reads=[b_pL, b_bglu], writes=[b_pr])
        fw.op(S, lambda: act.activation(out=pr[:], in_=pr[:], func=AF.Exp, scale=-1.0), reads=[b_pr], writes=[b_pr])
        fw.op(V, lambda: vec.tensor_scalar(out=pr[:], in0=pr[:], scalar1=1.0, scalar2=None, op0=ALU.add), reads=[b_pr], writes=[b_pr])
        fw.op(V, lambda: vec.reciprocal(out=pr[:], in_=pr[:]), reads=[b_pr], writes=[b_pr])
        fw.op(P, lambda: pool.tensor_tensor(out=pr[:], in0=pr[:], in1=ozt[:, 512:1024], op=ALU.mult), reads=[b_pr, b_oz], writes=[b_pr])
        fw.op(P, lambda: pool.tensor_tensor(out=ob[:, 512:1024], in0=pr[:], in1=sg[:], op=ALU.mult), reads=[b_pr, b_sg], writes=[b_ob])
        fw.op(P, lambda: pool.tensor_copy(out=ob[:, 0:512], in_=ozt[:, 0:512]), reads=[b_oz], writes=[b_ob])
        for kt in range(8):
            fw.op(T, (lambda kt=kt: pe.transpose(out=pO[:, kt * 128:(kt + 1) * 128], in_=ob[:, kt * 128:(kt + 1) * 128],
                                                 identity=ident[:])), reads=[b_ob, b_ident], writes=[b_pO], sig=(kt == 7))
        fw.op(V, lambda: vec.tensor_copy(out=oT[:].rearrange("p a b -> p (a b)"), in_=pO[:, :]), reads=[b_pO], writes=[b_oT])
        for nb in range(2):
            for kt in range(8):
                fw.op(T, (lambda kt=kt, nb=nb: pe.matmul(pY[nb][:, :], lhsT=oT[:, kt, :], rhs=wout[:, kt, nb * 512:(nb + 1) * 512],
                                                         start=(kt == 0), stop=(kt == 7))),
                      reads=[b_oT, b_wout], writes=[b_pY[nb]], sig=(kt == 7))
            fw.op(V, (lambda nb=nb: vec.tensor_tensor(out=y[:, nb * 512:(nb + 1) * 512], in0=pY[nb][:, :],
                                                      in1=xt[:, nb * 512:(nb + 1) * 512], op=ALU.add)),
                  reads=[b_pY[nb], b_xt], writes=[b_y])
        fw.dma(P, yq[rows, :], y[:], reads=[b_y], writes=[b_yq])
    fw.barrier()
    return nc


def _onehot():
    rel = np.arange(640) - 255
    n = np.maximum(rel, 0)
    nf = np.maximum(n, 1).astype(np.float32)
    large = 16 + (np.log(nf / np.float32(16)) / np.float32(np.log(128 / 16)) * np.float32(16)).astype(np.int32)
    large = np.minimum(large, 31)
    bucket = np.where(n < 16, n, large)
    oh = np.zeros((33, 640), np.float32)
    for m in range(640):
        if rel[m] < 0:
            oh[32, m] = 1.0
        else:
            oh[bucket[m], m] = 1.0
    return oh


def _prep_core(c, inp):
    b, h = c // 4, c % 4
    w_in = inp["w_in"][0]
    cols = np.concatenate([np.arange(h * 128, (h + 1) * 128) + off for off in (0, 512, 1024, 1536, 2048)])
    wa = w_in[:, cols]
    wa = np.ascontiguousarray(wa.reshape(8, 128, 640).transpose(1, 0, 2))
    ng = np.ascontiguousarray(inp["norm_g"][0].reshape(8, 128).T)
    qg, kg = inp["q_norm_g"][0], inp["k_norm_g"][0]
    gqk = np.concatenate([qg, qg, kg, kg]).astype(np.float32)
    lamv = np.concatenate([inp["lambda_q1"][0], inp["lambda_k1"][0], inp["lambda_q2"][0], inp["lambda_k2"][0]]).astype(np.float32)
    rb = inp["rel_bias"][:, h].astype(np.float32)
    rbx = np.ascontiguousarray(np.repeat(np.concatenate([rb, np.array([-30000.0], np.float32)]).reshape(33, 1), 128, axis=1))
    gs_ = np.arange(8 * h, 8 * h + 8)
    def pairlay(a):
        return np.ascontiguousarray(a.reshape(4, 2, 64).transpose(1, 2, 0).reshape(128, 4))
    are = pairlay(inp["ssm_a_re"][0][gs_]); aim = pairlay(inp["ssm_a_im"][0][gs_])
    ldt = pairlay(np.repeat(inp["ssm_log_dt"][0][gs_][:, None], 64, axis=1))
    s5a = np.concatenate([are, aim, ldt], axis=1).astype(np.float32)
    def blay(bm):
        return bm.reshape(4, 2, 64, 16).transpose(1, 2, 0, 3).reshape(128, 4, 16)
    s5b = np.ascontiguousarray(np.stack([blay(inp["ssm_b_re"][0][gs_]), blay(inp["ssm_b_im"][0][gs_])], axis=1)).astype(np.float32)
    def clay(cm):
        o = np.zeros((2, 64, 4, 2, 16), np.float32)
        cm4 = cm.reshape(4, 2, 16, 64)
        for gi in range(2):
            o[gi, :, :, gi, :] = cm4[:, gi].transpose(2, 0, 1)
        return o.reshape(128, 4, 32)
    s5c = np.ascontiguousarray(np.stack([clay(inp["ssm_c_re"][0][gs_]), clay(inp["ssm_c_im"][0][gs_])], axis=1))
    w_in0 = inp["w_in"][0]
    def klay2(w):
        return w.reshape(8, 128, w.shape[1]).transpose(1, 0, 2)
    wsm = np.ascontiguousarray(np.stack([klay2(w_in0[:, i * 512:(i + 1) * 512]) for i in range(5)], axis=1)).astype(np.float32)
    A_re, A_im, LDT = inp["ssm_a_re"][0], inp["ssm_a_im"][0], inp["ssm_log_dt"][0]
    def pl16(a):
        return a.reshape(16, 2, 64).transpose(1, 2, 0).reshape(128, 16)
    s5a16 = np.ascontiguousarray(np.concatenate([pl16(A_re), pl16(A_im), pl16(np.repeat(LDT[:, None], 64, axis=1))], axis=1)).astype(np.float32)
    def bl16(bm):
        return bm.reshape(16, 2, 64, 16).transpose(1, 2, 0, 3).reshape(128, 16, 16)
    s5b16 = np.ascontiguousarray(np.stack([bl16(inp["ssm_b_re"][0]), bl16(inp["ssm_b_im"][0])], axis=1)).astype(np.float32)
    def cl16(cm):
        o = np.zeros((2, 64, 16, 2, 16), np.float32)
        cm4 = cm.reshape(16, 2, 16, 64)
        for gi in range(2):
            o[gi, :, :, gi, :] = cm4[:, gi].transpose(2, 0, 1)
        return o.reshape(128, 16, 32)
    s5c16 = np.ascontiguousarray(np.stack([cl16(inp["ssm_c_re"][0]), cl16(inp["ssm_c_im"][0])], axis=1))
    def hl(st):
        return st.reshape(16, 16, 2, 64).transpose(2, 3, 1, 0).reshape(128, 16, 16)
    hst = np.ascontiguousarray(np.stack([hl(inp["state_ssm_re"][0][16 * c:16 * c + 16]),
                                         hl(inp["state_ssm_im"][0][16 * c:16 * c + 16])], axis=1)).astype(np.float32)
    RB = inp["rel_bias"].astype(np.float32)
    def bkt(n):
        n = np.asarray(n)
        nf = np.maximum(n, 1).astype(np.float32)
        large = 16 + (np.log(nf / np.float32(16)) / np.float32(np.log(128 / 16)) * np.float32(16)).astype(np.int32)
        return np.where(n < 16, n, np.minimum(large, 31))
    oh15 = np.zeros((32, 4, 128), np.float32)
    for qi in range(4):
        bk_ = bkt(128 + qi - np.arange(128))
        oh15[bk_, qi, np.arange(128)] = 1.0
    ohn = np.zeros((33, 4, 4), np.float32)
    for qi in range(4):
        for kj in range(4):
            if kj <= qi:
                ohn[qi - kj, qi, kj] = 1.0
            else:
                ohn[32, qi, kj] = 1.0
    rb33 = np.concatenate([RB, np.full((1, 4), -30000.0, np.float32)], axis=0)
    dm = np.zeros((8, 8), np.float32)
    for qi in range(4):
        dm[qi, qi] = 1.0
        dm[4 + qi, 4 + qi] = 1.0
    jq = c % 4
    xq = np.zeros((NTB * 128, DM), np.float32)
    xq[:2048] = inp["x_prompt"][b, 2048 * jq:2048 * (jq + 1)]
    xq[2048:2048 + 64] = inp["x_sample"].reshape(512, DM)[64 * c:64 * (c + 1)]
    idxb = np.zeros((128, 64), np.int32)
    for t in range(16):
        for r in range(4):
            gtok = 2048 * jq + 128 * t + np.arange(128)
            idxb[:, t * 4 + r] = (gtok // 1024) * 4096 + r * 1024 + (gtok % 1024)
    wo = inp["w_out"][0]
    return {
        "xq": xq, "idxb": idxb,
        "wgs": np.ascontiguousarray(klay2(w_in0[:, 2560:3072])).astype(np.float32),
        "wglu": np.ascontiguousarray(inp["w_glu"][0].reshape(4, 128, 512).transpose(1, 0, 2)).astype(np.float32),
        "wout": np.ascontiguousarray(np.stack([klay2(wo[:, 0:512]), klay2(wo[:, 512:1024])], axis=1)).astype(np.float32),
        "bglu": inp["b_glu"][0].astype(np.float32),
        "xs": np.ascontiguousarray(inp["x_sample"].reshape(512, DM)[64 * c:64 * (c + 1)]),
        "wsm": wsm, "g16": np.concatenate([np.tile(inp["q_norm_g"][0], 8), np.tile(inp["k_norm_g"][0], 8)]).astype(np.float32),
        "s5a16": s5a16, "s5b16": s5b16, "s5c16": s5c16, "hst": hst, "dsk16": inp["ssm_d"][0].astype(np.float32),
        "pt": np.ascontiguousarray(inp["page_table"][16 * c:16 * c + 16].reshape(256)).astype(np.int32),
        "pcol": np.arange(128, dtype=np.float32).reshape(128, 1),
        "rbfar": np.repeat(RB[31], 8).astype(np.float32), "oh15": oh15, "ohn": ohn, "rb33": rb33, "dm": dm,
        "sg4": np.tile(inp["subln_g"][0], 4).astype(np.float32),
        "ck": inp["cache_k"][0].reshape(-1, 512), "cv": inp["cache_v"][0].reshape(-1, 512),
        "s5a": s5a, "s5b": s5b, "s5c": s5c, "dsk": np.ascontiguousarray(inp["ssm_d"][0][128 * h:128 * h + 128]),
        "lamv": lamv, "sg": inp["subln_g"][0].astype(np.float32), "rb31": rb[31:32].copy(),
        "rbx": rbx, "oh": _onehot(),
        "xp": np.ascontiguousarray(inp["x_prompt"][b]),
        "wa": wa, "ng": ng, "ident": np.eye(128, dtype=np.float32), "gqk": gqk,
    }


_NC = None
_NCB = None
_OZS = None


def kernel(**inp):
    global _NC
    inp = {k: np.asarray(v) for k, v in inp.items()}
    if _NC is None:
        _NC = build(int(inp["cache_k"].shape[1]))
    nc = _NC
    in_maps = [_prep_core(c, inp) for c in range(8)]
    res = run_bass_kernel_spmd(nc, in_maps, core_ids=list(range(8)))
    R = res.results
    B_, DB, DS_ = 2, 128, 4
    y_prompt = np.zeros((B_, SEQ, DM), np.float32)
    y_sample = np.zeros((DB, DS_, DM), np.float32)
    k_prompt = np.zeros((1, B_, SEQ, 4, 128), np.float32)
    v_prompt = np.zeros((1, B_, SEQ, 4, 128), np.float32)
    k_sample = np.zeros((1, DB, DS_, 4, 128), np.float32)
    v_sample = np.zeros((1, DB, DS_, 4, 128), np.float32)
    srp = np.zeros((1, B_, 32, 64), np.float32)
    sip = np.zeros((1, B_, 32, 64), np.float32)
    srs = np.zeros((1, DB, 32, 64), np.float32)
    sis = np.zeros((1, DB, 32, 64), np.float32)
    for c in range(8):
        b, h = c // 4, c % 4
        k_prompt[0, b, :, h, :] = R[c]["kp"]
        v_prompt[0, b, :, h, :] = R[c]["vp"]
        k_sample[0, 16 * c:16 * (c + 1)] = R[c]["ks"].reshape(16, 4, 4, 128)
        v_sample[0, 16 * c:16 * (c + 1)] = R[c]["vs"].reshape(16, 4, 4, 128)
        st = R[c]["sss"].reshape(2, 64, 2, 16, 16)
        for ri, dst in ((0, srs), (1, sis)):
            dst[0, 16 * c:16 * c + 16] = st[:, :, ri].transpose(3, 2, 0, 1).reshape(16, 32, 64)
        sf = R[c]["sfin"].reshape(2, 64, 2, 4)
        for pr in range(4):
            for gi in range(2):
                srp[0, b, 8 * h + 2 * pr + gi, :] = sf[gi, :, 0, pr]
                sip[0, b, 8 * h + 2 * pr + gi, :] = sf[gi, :, 1, pr]
    for c in range(8):
        b, j = c // 4, c % 4
        yq = R[c]["yq"]
        y_prompt[b, 2048 * j:2048 * (j + 1)] = yq[:2048]
        y_sample.reshape(DB * DS_, DM)[64 * c:64 * (c + 1)] = yq[2048:2048 + 64]
    return (y_prompt, y_sample, k_prompt, v_prompt, k_sample, v_sample, srp, sip, srs, sis)
```

```python
import numpy as np
import concourse.bass as bass
import concourse.mybir as mybir
from concourse.bass_utils import run_bass_kernel_spmd

F32 = mybir.dt.float32
BF16 = mybir.dt.bfloat16
I32 = mybir.dt.int32
AF = mybir.ActivationFunctionType
ALU = mybir.AluOpType
AX = mybir.AxisListType

SEQ = 8192
DM = 1024
NT = SEQ // 128
EPS = 1e-6


class Buf:
    __slots__ = ("name", "w", "r")

    def __init__(self, name=""):
        self.name = name
        self.w = None
        self.r = []


class Eng:
    def __init__(self, fw, name, h):
        self.name = name
        self.h = h
        self.sem = fw.nc.alloc_semaphore("s_" + name)
        self.cnt = 0
        self.waited = {}


class FW:
    def __init__(self, nc, n_dma_sems=24):
        self.nc = nc
        self.pe = Eng(self, "pe", nc.tensor)
        self.act = Eng(self, "act", nc.scalar)
        self.dve = Eng(self, "dve", nc.vector)
        self.pool = Eng(self, "pool", nc.gpsimd)
        self.sp = Eng(self, "sp", nc.sync)
        self.engs = (self.pe, self.act, self.dve, self.pool, self.sp)
        self.sems = {e.name: e.sem for e in self.engs}
        self.dma_sems = []
        for i in range(n_dma_sems):
            k = "dma%d" % i
            self.sems[k] = nc.alloc_semaphore(k)
            self.dma_sems.append([k, 0])
        self.dma_rr = 0

    def _need(self, eng, reads, writes):
        need = {}

        def add(ev, is_raw):
            if ev is None:
                return
            k, v = ev
            if k == eng.name and not is_raw:
                return
            if need.get(k, 0) < v:
                need[k] = v
        for b in reads:
            add(b.w, True)
        for b in writes:
            add(b.w, False)
            for ev in b.r:
                add(ev, False)
        for k, v in need.items():
            if eng.waited.get(k, 0) >= v:
                continue
            eng.h.wait_ge(self.sems[k], v)
            eng.waited[k] = v

    def _mark(self, ev, reads, writes):
        for b in reads:
            b.r.append(ev)
            if len(b.r) > 48:
                d = {}
                for k, v in b.r:
                    if d.get(k, 0) < v:
                        d[k] = v
                b.r = list(d.items())
        for b in writes:
            b.w = ev
            b.r = []

    only = None
    stage = 0

    def _skip(self):
        return self.only is not None and self.stage != self.only

    def op(self, eng, fn, reads=(), writes=(), sig=True):
        if self._skip():
            return None
        self._need(eng, reads, writes)
        ins = fn()
        if sig:
            ins.then_inc(eng.sem, 1)
            eng.cnt += 1
            ev = (eng.name, eng.cnt)
        else:
            ev = (eng.name, eng.cnt + 1)
        self._mark(ev, reads, writes)
        return ins

    def dma(self, eng, out, in_, reads=(), writes=(), **kw):
        if self._skip():
            return None
        slot = self.dma_sems[self.dma_rr]
        self.dma_rr = (self.dma_rr + 1) % len(self.dma_sems)
        k, c = slot
        if c > 0 and eng.waited.get(k, 0) < c:
            eng.h.wait_ge(self.sems[k], c)
            eng.waited[k] = c
        self._need(eng, reads, writes)
        ins = eng.h.dma_start(out=out, in_=in_, **kw)
        ins.then_inc(self.sems[k], 16)
        slot[1] = c + 16
        ev = (k, c + 16)
        self._mark(ev, reads, writes)
        return ev

    def idma(self, out, in_, idx_ap, reads=(), writes=()):
        if self._skip():
            return None
        eng = self.pool
        slot = self.dma_sems[self.dma_rr]
        self.dma_rr = (self.dma_rr + 1) % len(self.dma_sems)
        k, c = slot
        if c > 0 and eng.waited.get(k, 0) < c:
            eng.h.wait_ge(self.sems[k], c)
            eng.waited[k] = c
        self._need(eng, reads, writes)
        ins = eng.h.indirect_dma_start(out=out, out_offset=None, in_=in_,
                                       in_offset=bass.IndirectOffsetOnAxis(ap=idx_ap, axis=0))
        ins.then_inc(self.sems[k], 16)
        slot[1] = c + 16
        ev = (k, c + 16)
        self._mark(ev, reads, writes)
        return ev

    def wait_all(self, eng, bufs):
        self._need(eng, bufs, ())

    def barrier(self):
        for e in self.engs:
            for f in self.engs:
                if f is e or f.cnt == 0:
                    continue
                if e.waited.get(f.name, 0) < f.cnt:
                    e.h.wait_ge(f.sem, f.cnt)
                    e.waited[f.name] = f.cnt
            for k, c in self.dma_sems:
                if c > 0 and e.waited.get(k, 0) < c:
                    e.h.wait_ge(self.sems[k], c)
                    e.waited[k] = c


class B:
    pass


def _sb(g, name, shape, dtype, n=1):
    out = []
    nb = 1
    for d in shape[1:]:
        nb *= d
    nb *= (2 if dtype == BF16 else 4)
    g.sb_bytes = getattr(g, "sb_bytes", 0) + ((nb + 31) // 32 * 32) * n
    for i in range(n):
        g.uid += 1
        t = g.nc.alloc_sbuf_tensor("%s_%d" % (name, g.uid), list(shape), dtype)
        out.append((t, Buf(name + str(i))))
    return out if n > 1 else out[0]


def build(npool=2560):
    nc = bass.Bass("TRN2", target_bir_lowering=False)
    g = B()
    g.nc = nc
    g.uid = 0
    fw = FW(nc)
    g.fw = fw
    V, S, P, T, SP = fw.dve, fw.act, fw.pool, fw.pe, fw.sp
    vec, act, pool, pe = nc.vector, nc.scalar, nc.gpsimd, nc.tensor

    def din(name, shape, dt=F32):
        return nc.dram_tensor(name, list(shape), dt, kind="ExternalInput").ap()

    def dout(name, shape, dt=F32):
        return nc.dram_tensor(name, list(shape), dt, kind="ExternalOutput").ap()

    xp = din("xp", [SEQ, DM])
    wa = din("wa", [128, 8, 640])
    ng = din("ng", [128, 8])
    ident_d = din("ident", [128, 128])
    gqk_d = din("gqk", [256])
    lamv_d = din("lamv", [256])
    sg_d = din("sg", [128])
    rb31_d = din("rb31", [1])
    rbx_d = din("rbx", [33, 128])
    oh_d = din("oh", [33, 640])
    xs_d = din("xs", [64, DM])
    wsm_d = din("wsm", [128, 5, 8, 512])
    g16_d = din("g16", [1024])
    s5a16_d = din("s5a16", [128, 48])
    s5b16_d = din("s5b16", [128, 2, 16, 16])
    s5c16_d = din("s5c16", [128, 2, 16, 32])
    hst_d = din("hst", [128, 2, 16, 16])
    dsk16_d = din("dsk16", [512])
    pt_d = din("pt", [256], I32)
    pcol_d = din("pcol", [128, 1])
    rbfar_d = din("rbfar", [32])
    oh15_d = din("oh15", [32, 4, 128])
    ohn_d = din("ohn", [33, 4, 4])
    rb33_d = din("rb33", [33, 4])
    dm_d = din("dm", [8, 8])
    sg4_d = din("sg4", [512])
    ck_d = din("ck", [npool * 128, 512])
    cv_d = din("cv", [npool * 128, 512])
    sss_o = dout("sss", [128, 2, 16, 16])
    oraw = nc.dram_tensor("oraw", [64, 512], F32, kind="Internal").ap()
    agin = nc.dram_tensor("agin", [SEQ, 256], F32, kind="Internal").ap()
    agout = nc.dram_tensor("agout", [4 * SEQ, 256], F32, kind="Internal", addr_space="Local").ap()
    ozs_scr = nc.dram_tensor("ozs_scr", [64, DM], F32, kind="Internal").ap()
    b_ozscr = Buf("ozscr"); b_agout = Buf("agout")
    xq = din("xq", [NTB * 128, DM])
    wgs_d = din("wgs", [128, 8, 512])
    wglu_d = din("wglu", [128, 4, 512])
    wout_d = din("wout", [128, 2, 8, 512])
    bglu_d = din("bglu", [512])
    idxb_d = din("idxb", [128, 64], I32)
    yq = dout("yq", [NTB * 128, DM])
    b_yq = Buf("yq")
    b_dbg = Buf("dbg")
    b_ssso = Buf("ssso"); b_ozso = Buf("ozso"); b_oraw = Buf("oraw")
    ks_o = dout("ks", [64, 512])
    vs_o = dout("vs", [64, 512])
    b_kso = Buf("kso"); b_vso = Buf("vso")
    s5a_d = din("s5a", [128, 12])
    s5b_d = din("s5b", [128, 2, 4, 16])
    s5c_d = din("s5c", [128, 2, 4, 32])
    dsk_d = din("dsk", [128])
    sfin_d = dout("sfin", [128, 8])
    b_zo = Buf("zo"); b_sfin = Buf("sfin")
    kp = dout("kp", [SEQ, 128])
    bvd_t = nc.dram_tensor("bvd", [128, 640], F32, kind="Internal")
    bvd = bvd_t.ap()
    b_bvd = Buf("bvd")
    b_oadbg = Buf("oadbg")
    vp = dout("vp", [SEQ, 128])
    u_scr = nc.dram_tensor("u_scr", [SEQ, 128], F32, kind="Internal").ap()
    out_bufs = [Buf("kp"), Buf("vp"), Buf("uscr")]
    bkp, bvp, buscr = out_bufs

    banks = [nc.alloc_psum_tensor("bank%d" % i, [128, 512], F32) for i in range(8)]
    bbuf = [Buf("bank%d" % i) for i in range(8)]

    ident_f, b_identf = _sb(g, "identf", [128, 128], F32)
    ident, b_ident = _sb(g, "ident", [128, 128], BF16)
    fw.dma(SP, ident_f[:], ident_d[:, :], writes=[b_identf])
    fw.op(V, lambda: vec.tensor_copy(out=ident[:], in_=ident_f[:]), reads=[b_identf], writes=[b_ident])

    ngt, b_ng = _sb(g, "ng", [128, 8], F32)
    fw.dma(SP, ngt[:], ng[:, :], writes=[b_ng])
    gqk, b_gqk = _sb(g, "gqk", [128, 256], F32)
    fw.dma(SP, gqk[:], gqk_d.partition_broadcast(128), writes=[b_gqk])
    fw.op(V, lambda: vec.tensor_scalar(out=gqk[:, 0:128], in0=gqk[:, 0:128], scalar1=0.125, scalar2=None,
                                       op0=ALU.mult), reads=[b_gqk], writes=[b_gqk])

    wst, b_wst = _sb(g, "wst", [128, 8, 640], F32)
    wab, b_wab = _sb(g, "wab", [128, 8, 640], BF16)
    fw.dma(SP, wst[:], wa[:, :, :], writes=[b_wst])
    for kt in range(8):
        fw.op(V if kt % 2 == 0 else P,
              (lambda kt=kt: (vec if kt % 2 == 0 else pool).tensor_scalar(
                  out=wab[:, kt, :], in0=wst[:, kt, :], scalar1=ngt[:, kt:kt + 1], scalar2=None, op0=ALU.mult)),
              reads=[b_wst, b_ng], writes=[b_wab])

    QT0, b_qkt = _sb(g, "QT0", [128, SEQ], BF16)
    QT1, _ = _sb(g, "QT1", [128, SEQ], BF16)
    KT, _ = _sb(g, "KT", [128, SEQ], BF16)
    G, b_G = _sb(g, "G", [128, NT, 128], BF16)
    fw.op(P, lambda: pool.memset(QT0[64:128, :], 0.0), writes=[b_qkt])
    fw.op(P, lambda: pool.memset(QT1[0:64, :], 0.0), writes=[b_qkt])
    sgt, b_sg = _sb(g, "sgt", [128, 128], F32)
    fw.dma(SP, sgt[:], sg_d.partition_broadcast(128), writes=[b_sg])
    fw.op(V, lambda: vec.tensor_scalar(out=sgt[:], in0=sgt[:], scalar1=0.8, scalar2=None, op0=ALU.mult),
          reads=[b_sg], writes=[b_sg])
    gab = _sb(g, "ga", [128, 128], F32, 2)
    ga2b = _sb(g, "ga2", [128, 128], F32, 2)
    Vaug, b_vaug = _sb(g, "Vaug", [128, NT, 130], BF16)
    fw.op(P, lambda: pool.memset(Vaug[:, :, 128:130], 1.0), writes=[b_vaug])

    xb = _sb(g, "xt", [128, DM], F32, 3)
    xnb = _sb(g, "xn", [128, DM], BF16, 2)
    xTb = _sb(g, "xT", [128, 8, 128], BF16, 2)
    junk, b_junk = _sb(g, "junk", [128, DM], BF16)
    ssb = _sb(g, "ss", [128, 4], F32, 2)
    qkb = _sb(g, "qk", [128, 256], F32, 2)
    sqb = _sb(g, "sq", [128, 256], F32, 2)
    s4b = _sb(g, "s4", [128, 8], F32, 2)
    qknb = _sb(g, "qkn", [128, 256], F32, 2)
    qkbfb = _sb(g, "qkbf", [128, 256], BF16, 2)
    vsb = _sb(g, "vs", [128, 128], F32, 2)
    usb = _sb(g, "us", [128, 128], F32, 2)
    ubfb = _sb(g, "ubf", [128, 128], BF16, 2)
    UT = wst[:].rearrange("p a b -> p (a b)").bitcast(BF16)[:, 0:SEQ]

    psT = [banks[0].bitcast(BF16) if False else None]
    def a1_body(t):
        fw.stage = 1
        xt, b_xt = xb[t % 3]
        xn, b_xn = xnb[t % 2]
        xT, b_xT = xTb[t % 2]
        ss, b_ss = ssb[t % 2]
        qk, b_qk = qkb[t % 2]
        sq, b_sq = sqb[t % 2]
        s4, b_s4 = s4b[t % 2]
        qkn, b_qkn = qknb[t % 2]
        qkbf, b_qkbf = qkbfb[t % 2]
        vs, b_vs = vsb[t % 2]
        us, b_us = usb[t % 2]
        pT = banks[t % 2][:, :].bitcast(BF16)
        b_pT = bbuf[t % 2]
        pA = banks[2 + (t % 2)]
        b_pA = bbuf[2 + (t % 2)]
        pU = banks[4 + (t % 2)]
        b_pU = bbuf[4 + (t % 2)]
        pQ = banks[6 + (t % 2)][:, :].bitcast(BF16)
        b_pQ = bbuf[6 + (t % 2)]
        rows = slice(t * 128, (t + 1) * 128)

        fw.dma(SP, xt[:], xp[rows, :], writes=[b_xt])
        fw.op(S, lambda: act.activation(out=junk[:], in_=xt[:], func=AF.Square, accum_out=ss[:, 0:1]),
              reads=[b_xt], writes=[b_junk, b_ss])
        fw.op(S, lambda: act.activation(out=ss[:, 1:2], in_=ss[:, 0:1], func=AF.Ln, scale=1.0 / DM, bias=EPS),
              reads=[b_ss], writes=[b_ss])
        fw.op(S, lambda: act.activation(out=ss[:, 2:3], in_=ss[:, 1:2], func=AF.Exp, scale=-0.5),
              reads=[b_ss], writes=[b_ss])
        fw.op(S, lambda: act.activation(out=xn[:], in_=xt[:], func=AF.Copy, scale=ss[:, 2:3]),
              reads=[b_xt, b_ss], writes=[b_xn])
        for kt in range(8):
            fw.op(T, (lambda kt=kt: pe.transpose(out=pT[:, kt * 128:(kt + 1) * 128],
                                                 in_=xn[:, kt * 128:(kt + 1) * 128], identity=ident[:])),
                  reads=[b_xn, b_ident], writes=[b_pT], sig=(kt == 7))
        fw.op(V, lambda: vec.tensor_copy(out=xT[:].rearrange("p a b -> p (a b)"), in_=pT[:, :]),
              reads=[b_pT], writes=[b_xT])
        for kt in range(8):
            fw.op(T, (lambda kt=kt: pe.matmul(pA[:, :], lhsT=xT[:, kt, :], rhs=wab[:, kt, 0:512],
                                              start=(kt == 0), stop=(kt == 7))),
                  reads=[b_xT, b_wab], writes=[b_pA], sig=(kt == 7))
        for kt in range(8):
            fw.op(T, (lambda kt=kt: pe.matmul(pU[:, 0:128], lhsT=xT[:, kt, :], rhs=wab[:, kt, 512:640],
                                              start=(kt == 0), stop=(kt == 7))),
                  reads=[b_xT, b_wab], writes=[b_pU], sig=(kt == 7))
        fw.stage = 2
        fw.op(S, lambda: act.activation(out=qk[:], in_=pA[:, 0:256], func=AF.Copy), reads=[b_pA], writes=[b_qk])
        fw.op(V, lambda: vec.tensor_tensor(out=sq[:], in0=qk[:], in1=qk[:], op=ALU.mult),
              reads=[b_qk], writes=[b_sq])
        fw.op(V, lambda: vec.tensor_reduce(out=s4[:, 0:4], in_=sq[:].rearrange("p (a d) -> p a d", d=64),
                                           axis=AX.X, op=ALU.add), reads=[b_sq], writes=[b_s4])
        fw.op(S, lambda: act.activation(out=s4[:, 4:8], in_=s4[:, 0:4], func=AF.Ln, scale=1.0 / 64, bias=EPS),
              reads=[b_s4], writes=[b_s4])
        fw.op(S, lambda: act.activation(out=s4[:, 0:4], in_=s4[:, 4:8], func=AF.Exp, scale=-0.5),
              reads=[b_s4], writes=[b_s4])
        fw.op(V, lambda: vec.tensor_tensor(out=qkn[:].rearrange("p (a d) -> p a d", d=64),
                                           in0=qk[:].rearrange("p (a d) -> p a d", d=64),
                                           in1=s4[:, 0:4].unsqueeze(2).broadcast_to([128, 4, 64]), op=ALU.mult),
              reads=[b_qk, b_s4], writes=[b_qkn])
        fw.op(V, lambda: vec.tensor_tensor(out=qkn[:], in0=qkn[:], in1=gqk[:], op=ALU.mult),
              reads=[b_qkn, b_gqk], writes=[b_qkn])
        fw.stage = 3
        fw.dma(SP, kp[rows, :], qkn[:, 128:256], reads=[b_qkn], writes=[bkp])
        fw.op(P, lambda: pool.tensor_copy(out=qkbf[:], in_=qkn[:]), reads=[b_qkn], writes=[b_qkbf])
        for j in range(2):
            fw.op(T, (lambda j=j: pe.transpose(out=pQ[:, j * 128:(j + 1) * 128],
                                               in_=qkbf[:, j * 128:(j + 1) * 128], identity=ident[:])),
                  reads=[b_qkbf, b_ident], writes=[b_pQ], sig=(j == 1))
        fw.op(V, lambda: vec.tensor_copy(out=QT0[0:64, rows], in_=pQ[0:64, 0:128]), reads=[b_pQ], writes=[b_qkt])
        fw.op(V, lambda: vec.tensor_copy(out=QT1[64:128, rows], in_=pQ[64:128, 0:128]), reads=[b_pQ], writes=[b_qkt])
        fw.op(V, lambda: vec.tensor_copy(out=KT[:, rows], in_=pQ[:, 128:256]), reads=[b_pQ], writes=[b_qkt])
        ga, b_ga = gab[t % 2]
        ga2, b_ga2 = ga2b[t % 2]
        fw.stage = 2
        fw.op(S, lambda: act.activation(out=ga[:], in_=pA[:, 384:512], func=AF.Exp, scale=-1.0),
              reads=[b_pA], writes=[b_ga])
        fw.op(V, lambda: vec.tensor_scalar(out=ga[:], in0=ga[:], scalar1=1.0, scalar2=None, op0=ALU.add),
              reads=[b_ga], writes=[b_ga])
        fw.op(V, lambda: vec.reciprocal(out=ga[:], in_=ga[:]), reads=[b_ga], writes=[b_ga])
        fw.op(V, lambda: vec.tensor_tensor(out=ga2[:], in0=pA[:, 384:512], in1=ga[:], op=ALU.mult),
              reads=[b_pA, b_ga], writes=[b_ga2])
        fw.stage = 3
        fw.op(P, lambda: pool.tensor_tensor(out=G[:, t, :], in0=ga2[:], in1=sgt[:], op=ALU.mult),
              reads=[b_ga2, b_sg], writes=[b_G])
        fw.stage = 2
        fw.op(S, lambda: act.activation(out=vs[:], in_=pA[:, 256:384], func=AF.Copy), reads=[b_pA], writes=[b_vs])
        fw.stage = 3
        fw.dma(SP, vp[rows, :], vs[:], reads=[b_vs], writes=[bvp])
        fw.op(P, lambda: pool.tensor_copy(out=Vaug[:, t, 0:128], in_=vs[:]), reads=[b_vs], writes=[b_vaug])
        fw.stage = 2
        fw.op(S, lambda: act.activation(out=us[:], in_=pU[:, 0:128], func=AF.Copy), reads=[b_pU], writes=[b_us])
        fw.stage = 3
        fw.dma(SP, u_scr[rows, :], us[:], reads=[b_us], writes=[buscr])
        ubf, b_ubf = ubfb[t % 2]
        fw.op(P, lambda: pool.tensor_copy(out=ubf[:], in_=us[:]), reads=[b_us], writes=[b_ubf])
        fw.op(T, lambda: pe.transpose(out=pU[:, 256:512].bitcast(BF16)[:, 0:128], in_=ubf[:], identity=ident[:]),
              reads=[b_ubf, b_ident], writes=[b_pU])
        fw.op(V, lambda: vec.tensor_copy(out=UT[:, rows], in_=pU[:, 256:512].bitcast(BF16)[:, 0:128]),
              reads=[b_pU], writes=[b_wst])

    import os
    if False:
        for t in range(NT):
            a1_body(t)
    else:
        fw.only = 1
        a1_body(0)
        a1_body(1)
        fw.only = 2
        a1_body(0)
        for t in range(NT):
            if t + 2 < NT:
                fw.only = 1
                a1_body(t + 2)
            if t + 1 < NT:
                fw.only = 2
                a1_body(t + 1)
            fw.only = 3
            a1_body(t)
        fw.only = None

    fw.barrier()
    lamt, b_lam = _sb(g, "lamt", [128, 256], F32)
    lamw, b_lamw = _sb(g, "lamw", [128, 8], F32)
    fw.dma(SP, lamt[:], lamv_d.partition_broadcast(128), writes=[b_lam])
    fw.op(V, lambda: vec.tensor_tensor(out=lamt[:, 0:64], in0=lamt[:, 0:64], in1=lamt[:, 64:128], op=ALU.mult),
          reads=[b_lam], writes=[b_lam])
    fw.op(V, lambda: vec.tensor_tensor(out=lamt[:, 128:192], in0=lamt[:, 128:192], in1=lamt[:, 192:256], op=ALU.mult),
          reads=[b_lam], writes=[b_lam])
    fw.op(V, lambda: vec.tensor_reduce(out=lamw[:, 0:1], in_=lamt[:, 0:64], axis=AX.X, op=ALU.add),
          reads=[b_lam], writes=[b_lamw])
    fw.op(V, lambda: vec.tensor_reduce(out=lamw[:, 1:2], in_=lamt[:, 128:192], axis=AX.X, op=ALU.add),
          reads=[b_lam], writes=[b_lamw])
    fw.op(S, lambda: act.activation(out=lamw[:, 2:4], in_=lamw[:, 0:2], func=AF.Exp), reads=[b_lamw], writes=[b_lamw])
    fw.op(V, lambda: vec.tensor_tensor(out=lamw[:, 4:5], in0=lamw[:, 3:4], in1=lamw[:, 2:3], op=ALU.subtract),
          reads=[b_lamw], writes=[b_lamw])
    fw.op(V, lambda: vec.tensor_scalar(out=lamw[:, 5:6], in0=lamw[:, 4:5], scalar1=-0.2, scalar2=None, op0=ALU.add),
          reads=[b_lamw], writes=[b_lamw])
    neglam = lamw[:, 5:6]
    rb31, b_rb31 = _sb(g, "rb31", [128, 1], F32)
    fw.dma(SP, rb31[:], rb31_d.partition_broadcast(128), writes=[b_rb31])
    rbx, b_rbx = _sb(g, "rbx", [33, 128], F32)
    oh, b_oh = _sb(g, "oh", [33, 640], F32)
    bvs, b_bvs = _sb(g, "bvs", [128, 640], F32)
    fw.dma(SP, rbx[:], rbx_d[:, :], writes=[b_rbx])
    fw.dma(SP, oh[:], oh_d[:, :], writes=[b_oh])
    fw.op(T, lambda: pe.matmul(banks[0][:, 0:512], lhsT=rbx[:, :], rhs=oh[:, 0:512], start=True, stop=True),
          reads=[b_rbx, b_oh], writes=[bbuf[0]])
    fw.op(T, lambda: pe.matmul(banks[1][:, 0:128], lhsT=rbx[:, :], rhs=oh[:, 512:640], start=True, stop=True),
          reads=[b_rbx, b_oh], writes=[bbuf[1]])
    fw.op(V, lambda: vec.tensor_copy(out=bvs[:, 0:512], in_=banks[0][:, 0:512]), reads=[bbuf[0]], writes=[b_bvs])
    fw.op(V, lambda: vec.tensor_copy(out=bvs[:, 512:640], in_=banks[1][:, 0:128]), reads=[bbuf[1]], writes=[b_bvs])
    fw.op(V, lambda: vec.tensor_scalar(out=bvs[:], in0=bvs[:], scalar1=rb31[:, 0:1], scalar2=None, op0=ALU.subtract),
          reads=[b_bvs, b_rb31], writes=[b_bvs])
    fw.dma(SP, bvd[:, :], bvs[:], reads=[b_bvs], writes=[b_bvd])
    btf = _sb(g, "btf", [128, 256], F32, 3)
    bth = _sb(g, "bth", [128, 256], BF16, 3)
    btl = _sb(g, "btl", [128, 256], BF16, 3)
    bt2, b_bt2 = _sb(g, "bt2", [128, 256], F32)
    for di, dl in enumerate((1, 0, -1)):
        src = bass.AP(tensor=bvd_t, offset=128 * dl + 255, ap=[[639, 128], [1, 256]])
        fw.dma(SP, btf[di][0][:], src, reads=[b_bvd], writes=[btf[di][1]])
        fw.op(V, (lambda di=di: vec.tensor_copy(out=bth[di][0][:], in_=btf[di][0][:])),
              reads=[btf[di][1]], writes=[bth[di][1]])
        fw.op(V, (lambda di=di: vec.tensor_tensor(out=bt2[:], in0=btf[di][0][:], in1=bth[di][0][:], op=ALU.subtract)),
              reads=[btf[di][1], bth[di][1]], writes=[b_bt2])
        fw.op(V, (lambda di=di: vec.tensor_copy(out=btl[di][0][:], in_=bt2[:])),
              reads=[b_bt2], writes=[btl[di][1]])

    sfinb = _sb(g, "sfin", [128, 16], F32)
    PTb = _sb(g, "PT", [128, 512], BF16, 2)
    ob = _sb(g, "o", [128, 128], F32, 2)
    o2b = _sb(g, "o2", [128, 128], F32, 2)
    osqb = _sb(g, "osq", [128, 128], F32, 2)
    oab = _sb(g, "oa", [128, 128], F32, 2)
    rwb = _sb(g, "rw", [128, 8], F32, 2)
    QTm = (QT0, QT1)
    def att_pair(i, j, pair):
        fw.stage = 1
        q0 = 256 * i
        Sb = banks[pair % 2]
        b_S = bbuf[pair % 2]
        PT, b_PT = PTb[pair % 2]
        dl = 2 * i - j
        near = dl <= 1
        for c in range(2):
            fw.op(T, (lambda c=c, j=j: pe.matmul(Sb[:, c * 256:(c + 1) * 256], lhsT=KT[:, j * 128:(j + 1) * 128],
                                                 rhs=QTm[c][:, q0:q0 + 256], start=True, stop=not near)),
                  reads=[b_qkt], writes=[b_S], sig=(c == 1 and not near))
            if near:
                di = 1 - dl
                fw.op(T, (lambda c=c, di=di: pe.matmul(Sb[:, c * 256:(c + 1) * 256], lhsT=ident[:],
                                                       rhs=bth[di][0][:], start=False, stop=False)),
                      reads=[b_ident, bth[di][1]], writes=[b_S], sig=False)
                fw.op(T, (lambda c=c, di=di: pe.matmul(Sb[:, c * 256:(c + 1) * 256], lhsT=ident[:],
                                                       rhs=btl[di][0][:], start=False, stop=True)),
                      reads=[b_ident, btl[di][1]], writes=[b_S], sig=(c == 1))
        fw.stage = 2
        if near:
            fw.op(S, lambda: act.activation(out=PT[:], in_=Sb[:, :], func=AF.Exp), reads=[b_S], writes=[b_PT])
        else:
            fw.op(S, lambda: act.activation(out=PT[:], in_=Sb[:, :], func=AF.Exp), reads=[b_S], writes=[b_PT])
        for c in range(2):
            for sub in range(2):
                last = 2 * i + sub
                if j > last:
                    continue
                accb = banks[4 + c * 2 + sub]
                fw.op(T, (lambda c=c, sub=sub, j=j, accb=accb, last=last: pe.matmul(
                    accb[:, 0:129], lhsT=PT[:, c * 256 + sub * 128: c * 256 + sub * 128 + 128],
                    rhs=Vaug[:, j, 0:129], start=(j == 0), stop=(j == last))),
                      reads=[b_PT, b_vaug], writes=[bbuf[4 + c * 2 + sub]], sig=True)

    def att_fin(i):
        for sub in range(2):
            tt = 2 * i + sub
            a0, a1 = banks[4 + sub], banks[6 + sub]
            b_a0, b_a1 = bbuf[4 + sub], bbuf[6 + sub]
            o, b_o = ob[tt % 2]
            o2, b_o2 = o2b[tt % 2]
            osq, b_osq = osqb[tt % 2]
            oa, b_oa = oab[tt % 2]
            rw, b_rw = rwb[tt % 2]
            fw.op(V, lambda: vec.reciprocal(out=rw[:, 0:1], in_=a0[:, 128:129]), reads=[b_a0], writes=[b_rw])
            fw.op(V, lambda: vec.reciprocal(out=rw[:, 1:2], in_=a1[:, 128:129]), reads=[b_a1], writes=[b_rw])
            fw.op(V, lambda: vec.tensor_tensor(out=rw[:, 2:3], in0=rw[:, 1:2], in1=neglam, op=ALU.mult),
                  reads=[b_rw, b_lamw], writes=[b_rw])
            fw.op(V, lambda: vec.tensor_scalar(out=o2[:], in0=a1[:, 0:128], scalar1=rw[:, 2:3], scalar2=None,
                                               op0=ALU.mult), reads=[b_a1, b_rw], writes=[b_o2])
            fw.op(V, lambda: vec.scalar_tensor_tensor(out=o[:], in0=a0[:, 0:128], scalar=rw[:, 0:1], in1=o2[:],
                                                      op0=ALU.mult, op1=ALU.add),
                  reads=[b_a0, b_rw, b_o2], writes=[b_o])
            fw.op(P, lambda: pool.tensor_tensor(out=osq[:], in0=o[:], in1=o[:], op=ALU.mult), reads=[b_o], writes=[b_osq])
            fw.op(V, lambda: vec.tensor_reduce(out=rw[:, 3:4], in_=osq[:], axis=AX.X, op=ALU.add),
                  reads=[b_osq], writes=[b_rw])
            fw.op(S, lambda: act.activation(out=rw[:, 4:5], in_=rw[:, 3:4], func=AF.Ln, scale=1.0 / 128, bias=EPS),
                  reads=[b_rw], writes=[b_rw])
            fw.op(S, lambda: act.activation(out=rw[:, 5:6], in_=rw[:, 4:5], func=AF.Exp, scale=-0.5),
                  reads=[b_rw], writes=[b_rw])
            fw.op(V, lambda: vec.scalar_tensor_tensor(out=oa[:], in0=o[:], scalar=rw[:, 5:6], in1=G[:, tt, :],
                                                      op0=ALU.mult, op1=ALU.mult),
                  reads=[b_o, b_rw, b_G], writes=[b_oa])
            fw.dma(P, agin[tt * 128:(tt + 1) * 128, 0:128], oa[:], reads=[b_oa], writes=[b_oadbg])


    plist = [(i, j) for i in range(SEQ // 256) for j in range(2 * i + 2)]
    if False:
        for n, (i, j) in enumerate(plist):
            att_pair(i, j, n)
            if j == 2 * i + 1:
                att_fin(i)
    else:
        fw.only = 1
        att_pair(plist[0][0], plist[0][1], 0)
        for n, (i, j) in enumerate(plist):
            if n + 1 < len(plist):
                fw.only = 1
                att_pair(plist[n + 1][0], plist[n + 1][1], n + 1)
            fw.only = 2
            att_pair(i, j, n)
            if j == 2 * i + 1:
                fw.only = None
                att_fin(i)
        fw.only = None

    fw.barrier()
    hpi, b_hpi = _sb(g, "hpi", [128, 1], F32)
    fw.op(V, lambda: vec.memset(hpi[:], float(np.pi / 2)), writes=[b_hpi])

    def s5_setup(tag, NP, a_d, b_d, c_d, alloc):
        r = B()
        s5a, b_s5a = alloc(tag + "a", [128, 3 * NP], F32)
        s5b, b_s5b = alloc(tag + "b", [128, 2, NP, 16], F32)
        s5c, b_s5c = alloc(tag + "c", [128, 2, NP, 32], F32)
        fw.dma(SP, s5a, a_d[:, :], writes=[b_s5a])
        fw.dma(SP, s5b, b_d[:, :, :, :], writes=[b_s5b])
        fw.dma(SP, s5c, c_d[:, :, :, :], writes=[b_s5c])
        W, b_W = alloc(tag + "w", [128, 20, NP], F32)
        r.b_W = b_W

        def w_(i):
            return W[:, i, :]

        def tt(o, a, b_, op):
            fw.op(V, lambda: vec.tensor_tensor(out=o, in0=a, in1=b_, op=op), reads=[b_W, b_s5a], writes=[b_W])

        def ts(o, a, s1, op0):
            fw.op(V, lambda: vec.tensor_scalar(out=o, in0=a, scalar1=s1, scalar2=None, op0=op0),
                  reads=[b_W, b_s5a], writes=[b_W])

        def ac(o, a, func, scale=1.0, bias=0.0):
            fw.op(S, lambda: act.activation(out=o, in_=a, func=func, scale=scale, bias=bias),
                  reads=[b_W, b_s5a], writes=[b_W])
        r.tt, r.ts, r.w_ = tt, ts, w_
        are, aim, ldt = s5a[:, 0:NP], s5a[:, NP:2 * NP], s5a[:, 2 * NP:3 * NP]
        DT, ADT, TH, MAG, C_, S_, CC, SS, CS, LR, LI = [w_(i) for i in range(11)]
        ac(DT, ldt, AF.Exp)
        tt(ADT, are, DT, ALU.mult)
        tt(TH, aim, DT, ALU.mult)
        ac(MAG, ADT, AF.Exp)
        ac(S_, TH, AF.Sin, scale=1.0 / 32)
        fw.op(S, lambda: act.activation(out=C_, in_=TH, func=AF.Sin, scale=1.0 / 32, bias=hpi[:, 0:1]),
              reads=[b_W, b_hpi], writes=[b_W])

        def csq(c, s_):
            tt(CC, c, c, ALU.mult)
            tt(SS, s_, s_, ALU.mult)
            tt(CS, c, s_, ALU.mult)
            tt(c, CC, SS, ALU.subtract)
            ts(s_, CS, 2.0, ALU.mult)
        r.csq = csq
        for _ in range(5):
            csq(C_, S_)
        tt(LR, MAG, C_, ALU.mult)
        tt(LI, MAG, S_, ALU.mult)
        r.MAG, r.C_, r.S_, r.LR, r.LI = MAG, C_, S_, LR, LI
        NR, DEN, T1, T2, CFR, CFI = [w_(i) for i in range(11, 17)]
        ts(NR, LR, -1.0, ALU.add)
        tt(T1, are, are, ALU.mult)
        tt(T2, aim, aim, ALU.mult)
        tt(DEN, T1, T2, ALU.add)
        fw.op(V, lambda: vec.reciprocal(out=DEN, in_=DEN), reads=[b_W], writes=[b_W])
        tt(T1, NR, are, ALU.mult)
        tt(T2, LI, aim, ALU.mult)
        tt(T1, T1, T2, ALU.add)
        tt(CFR, T1, DEN, ALU.mult)
        tt(T1, LI, are, ALU.mult)
        tt(T2, NR, aim, ALU.mult)
        tt(T1, T1, T2, ALU.subtract)
        tt(CFI, T1, DEN, ALU.mult)
        bb, b_bb = alloc(tag + "bb", [128, 2, NP, 16], F32)
        bt_, b_bt = alloc(tag + "bbt", [128, NP, 16], F32)

        def bc(x):
            return x.unsqueeze(2).broadcast_to([128, NP, 16])

        def tb(o, a, b_, op):
            fw.op(V, lambda: vec.tensor_tensor(out=o, in0=a, in1=b_, op=op), reads=[b_W, b_s5b, b_bb, b_bt],
                  writes=[b_bb, b_bt])
        tb(bb[:, 0], s5b[:, 0], bc(CFR), ALU.mult)
        tb(bt_, s5b[:, 1], bc(CFI), ALU.mult)
        tb(bb[:, 0], bb[:, 0], bt_, ALU.subtract)
        tb(bb[:, 1], s5b[:, 1], bc(CFR), ALU.mult)
        tb(bt_, s5b[:, 0], bc(CFI), ALU.mult)
        tb(bb[:, 1], bb[:, 1], bt_, ALU.add)
        Mz, b_Mz = alloc(tag + "Mz", [128, NP, 128], F32)
        BT, b_BT = alloc(tag + "BT", [128, 2 * NP, 128], BF16)
        Cb, b_Cb = alloc(tag + "Cb", [128, 2, NP, 32], BF16)
        for ri in range(2):
            fw.op(P, lambda: pool.memset(Mz, 0.0), writes=[b_Mz])
            for pr in range(NP):
                for gi in range(2):
                    c0 = (2 * (pr % 4) + gi) * 16
                    fw.op(V, (lambda ri=ri, pr=pr, gi=gi, c0=c0: vec.tensor_copy(
                        out=Mz[gi * 64:(gi + 1) * 64, pr, c0:c0 + 16], in_=bb[gi * 64:(gi + 1) * 64, ri, pr, :])),
                        reads=[b_bb], writes=[b_Mz])
            for m in range(NP):
                pb = banks[m % 2]
                fw.op(T, (lambda m=m, pb=pb: pe.transpose(out=pb[:, 0:128], in_=Mz[:, m, :], identity=ident_f[:])),
                      reads=[b_Mz, b_identf], writes=[bbuf[m % 2]])
                fw.op(V, (lambda m=m, pb=pb, ri=ri: vec.tensor_copy(out=BT[:, ri * NP + m, :], in_=pb[:, 0:128])),
                      reads=[bbuf[m % 2]], writes=[b_BT])
        fw.op(V, lambda: vec.tensor_copy(out=Cb[:, 0], in_=s5c[:, 0]), reads=[b_s5c], writes=[b_Cb])
        fw.op(V, lambda: vec.tensor_scalar(out=Cb[:, 1], in0=s5c[:, 1], scalar1=-1.0, scalar2=None, op0=ALU.mult),
              reads=[b_s5c], writes=[b_Cb])
        r.BT, r.b_BT, r.Cb, r.b_Cb = BT, b_BT, Cb, b_Cb
        return r

    def sb_alloc(name, shape, dt):
        t_, b_ = _sb(g, name, shape, dt)
        return t_[:], b_
    dsk, b_dsk = _sb(g, "dsk", [128, 128], F32)
    fw.dma(SP, dsk[:], dsk_d.partition_broadcast(128), writes=[b_dsk])
    r4 = s5_setup("own", 4, s5a_d, s5b_d, s5c_d, sb_alloc)
    b_W, tt, ts, w_, csq = r4.b_W, r4.tt, r4.ts, r4.w_, r4.csq
    MAG, C_, S_, BT, b_BT, Cb, b_Cb = r4.MAG, r4.C_, r4.S_, r4.BT, r4.b_BT, r4.Cb, r4.b_Cb
    ER = QT0[:, :].bitcast(F32).rearrange("p (a n) -> p a n", a=8)[:, 0:4, :]
    EI = QT0[:, :].bitcast(F32).rearrange("p (a n) -> p a n", a=8)[:, 4:8, :]
    RT = QT1[:, :].bitcast(F32).rearrange("p (a n) -> p a n", a=8)[:, 0:4, :]
    tmpA = QT1[:, :].bitcast(F32).rearrange("p (a n) -> p a n", a=8)[:, 4:8, :]
    tmpB = KT[:, :].bitcast(F32).rearrange("p (a n) -> p a n", a=8)
    b_E = Buf("E"); b_RT = Buf("RT")
    PR, PI = w_(17), w_(18)
    ts(PI, S_, -1.0, ALU.mult)
    ts(PR, C_, 1.0, ALU.mult)
    fw.op(V, lambda: vec.memset(ER[:, :, 0:1], 1.0), reads=[b_W], writes=[b_E])
    fw.op(V, lambda: vec.memset(EI[:, :, 0:1], 0.0), writes=[b_E])
    tsc, b_tsc = _sb(g, "tsc", [128, 256], F32)
    for k in range(9):
        n = 1 << k
        for pr in range(4):
            def e(o, a, s1, b_=None, op1=None):
                if b_ is None:
                    fw.op(V, lambda: vec.tensor_scalar(out=o, in0=a, scalar1=s1, scalar2=None, op0=ALU.mult),
                          reads=[b_E, b_W, b_tsc], writes=[b_E, b_tsc])
                else:
                    fw.op(V, lambda: vec.scalar_tensor_tensor(out=o, in0=a, scalar=s1, in1=b_, op0=ALU.mult, op1=op1),
                          reads=[b_E, b_W, b_tsc], writes=[b_E, b_tsc])
            pr_, pi_ = PR[:, pr:pr + 1], PI[:, pr:pr + 1]
            e(tsc[:, 0:n], EI[:, pr, 0:n], pi_)
            e(ER[:, pr, n:2 * n], ER[:, pr, 0:n], pr_, tsc[:, 0:n], ALU.subtract)
            e(tsc[:, 0:n], EI[:, pr, 0:n], pr_)
            e(EI[:, pr, n:2 * n], ER[:, pr, 0:n], pi_, tsc[:, 0:n], ALU.add)
        csq(PR, PI)
    ones512, b_ones = _sb(g, "ones512", [128, 512], F32)
    fw.op(P, lambda: pool.memset(ones512[:], 1.0), writes=[b_ones])
    for pr in range(4):
        fw.op(V, (lambda pr=pr: vec.tensor_scalar(out=RT[:, pr, :], in0=ones512[:], scalar1=MAG[:, pr:pr + 1],
                                                  scalar2=None, op0=ALU.mult)),
              reads=[b_ones, b_W], writes=[b_RT])
    car, b_car = _sb(g, "car", [128, 4, 8], F32)
    fw.op(V, lambda: vec.memset(car[:], 0.0), writes=[b_car])
    hreb = _sb(g, "hre", [128, 512], BF16, 2)
    himb = _sb(g, "him", [128, 512], BF16, 2)
    ut4b = _sb(g, "ut4", [128, 4, 128], F32, 2)
    b_tA = [Buf("tA%d" % i) for i in range(4)]
    b_tB = [Buf("tB%d" % i) for i in range(8)]
    it = 0
    for blk in range(SEQ // 512):
        cols = slice(blk * 512, (blk + 1) * 512)
        yps = banks[4 + blk % 2]
        b_yps = bbuf[4 + blk % 2]
        for pr in range(4):
            pre, pim = banks[(it % 2) * 2], banks[(it % 2) * 2 + 1]
            b_pre, b_pim = bbuf[(it % 2) * 2], bbuf[(it % 2) * 2 + 1]
            hre, b_hre = hreb[it % 2]
            him, b_him = himb[it % 2]
            it += 1
            fw.op(T, (lambda pr=pr: pe.matmul(pre[:, :], lhsT=BT[:, pr, :], rhs=UT[:, cols], start=True, stop=True)),
                  reads=[b_BT, b_wst], writes=[b_pre])
            fw.op(T, (lambda pr=pr: pe.matmul(pim[:, :], lhsT=BT[:, 4 + pr, :], rhs=UT[:, cols], start=True, stop=True)),
                  reads=[b_BT, b_wst], writes=[b_pim])
            a0, a1, a2, a3 = [tmpA[:, i, :] for i in range(4)]
            wre, wim, gre, gim = [tmpB[:, i, :] for i in range(4)]
            er, ei = ER[:, pr, :], EI[:, pr, :]
            fw.op(V, lambda: vec.tensor_tensor(out=a0, in0=pre[:, :], in1=er, op=ALU.mult), reads=[b_pre, b_E], writes=[b_tA[0]])
            fw.op(V, lambda: vec.tensor_tensor(out=a1, in0=pim[:, :], in1=ei, op=ALU.mult), reads=[b_pim, b_E], writes=[b_tA[1]])
            fw.op(V, lambda: vec.tensor_tensor(out=a2, in0=pim[:, :], in1=er, op=ALU.mult), reads=[b_pim, b_E], writes=[b_tA[2]])
            fw.op(V, lambda: vec.tensor_tensor(out=a3, in0=pre[:, :], in1=ei, op=ALU.mult), reads=[b_pre, b_E], writes=[b_tA[3]])
            fw.op(P, lambda: pool.tensor_tensor(out=wre, in0=a0, in1=a1, op=ALU.subtract), reads=[b_tA[0], b_tA[1]], writes=[b_tB[0]])
            fw.op(P, lambda: pool.tensor_tensor(out=wim, in0=a2, in1=a3, op=ALU.add), reads=[b_tA[2], b_tA[3]], writes=[b_tB[1]])
            fw.op(V, (lambda pr=pr: vec.tensor_tensor_scan(out=gre, data0=RT[:, pr, :], data1=wre, initial=car[:, pr, 0:1],
                                                           op0=ALU.mult, op1=ALU.add)),
                  reads=[b_RT, b_tB[0], b_car], writes=[b_tB[2]])
            fw.op(V, (lambda pr=pr: vec.tensor_tensor_scan(out=gim, data0=RT[:, pr, :], data1=wim, initial=car[:, pr, 1:2],
                                                           op0=ALU.mult, op1=ALU.add)),
                  reads=[b_RT, b_tB[1], b_car], writes=[b_tB[3]])
            ge_r, ge_i = gre[:, 511:512], gim[:, 511:512]
            prr, pii = PR[:, pr:pr + 1], PI[:, pr:pr + 1]
            def cop(o, a, b_, op):
                fw.op(V, lambda: vec.tensor_tensor(out=o, in0=a, in1=b_, op=op),
                      reads=[b_tB[2], b_tB[3], b_W, b_car], writes=[b_car])
            cop(car[:, pr, 2:3], ge_r, prr, ALU.mult)
            cop(car[:, pr, 3:4], ge_i, pii, ALU.mult)
            cop(car[:, pr, 4:5], ge_i, prr, ALU.mult)
            cop(car[:, pr, 5:6], ge_r, pii, ALU.mult)
            if blk == SEQ // 512 - 1:
                e_r, e_i = ER[:, pr, 511:512], EI[:, pr, 511:512]
                sf, b_sf = sfinb
                def fop(o, a, b_, op):
                    fw.op(V, lambda: vec.tensor_tensor(out=o, in0=a, in1=b_, op=op),
                          reads=[b_tB[2], b_tB[3], b_E, b_sf], writes=[b_sf])
                fop(sf[:, 8 + pr:9 + pr], e_r, ge_r, ALU.mult)
                fop(sf[:, 12 + pr:13 + pr], e_i, ge_i, ALU.mult)
                fop(sf[:, pr:pr + 1], sf[:, 8 + pr:9 + pr], sf[:, 12 + pr:13 + pr], ALU.add)
                fop(sf[:, 8 + pr:9 + pr], e_r, ge_i, ALU.mult)
                fop(sf[:, 12 + pr:13 + pr], e_i, ge_r, ALU.mult)
                fop(sf[:, 4 + pr:5 + pr], sf[:, 8 + pr:9 + pr], sf[:, 12 + pr:13 + pr], ALU.subtract)
            cop(car[:, pr, 0:1], car[:, pr, 2:3], car[:, pr, 3:4], ALU.add)
            cop(car[:, pr, 1:2], car[:, pr, 4:5], car[:, pr, 5:6], ALU.subtract)
            fw.op(P, lambda: pool.tensor_tensor(out=a0, in0=gre, in1=er, op=ALU.mult), reads=[b_tB[2], b_E], writes=[b_tA[0]])
            fw.op(P, lambda: pool.tensor_tensor(out=a1, in0=gim, in1=ei, op=ALU.mult), reads=[b_tB[3], b_E], writes=[b_tA[1]])
            fw.op(P, lambda: pool.tensor_tensor(out=a2, in0=gim, in1=er, op=ALU.mult), reads=[b_tB[3], b_E], writes=[b_tA[2]])
            fw.op(P, lambda: pool.tensor_tensor(out=a3, in0=gre, in1=ei, op=ALU.mult), reads=[b_tB[2], b_E], writes=[b_tA[3]])
            fw.op(P, lambda: pool.tensor_tensor(out=hre[:], in0=a0, in1=a1, op=ALU.add), reads=[b_tA[0], b_tA[1]], writes=[b_hre])
            fw.op(P, lambda: pool.tensor_tensor(out=him[:], in0=a2, in1=a3, op=ALU.subtract), reads=[b_tA[2], b_tA[3]], writes=[b_him])
            for sub in range(4):
                fw.op(T, (lambda pr=pr, sub=sub: pe.matmul(yps[:, sub * 128 + pr * 32: sub * 128 + pr * 32 + 32],
                                                           lhsT=hre[:, sub * 128:(sub + 1) * 128], rhs=Cb[:, 0, pr, :],
                                                           start=True, stop=False)),
                      reads=[b_hre, b_Cb], writes=[b_yps], sig=False)
                fw.op(T, (lambda pr=pr, sub=sub: pe.matmul(yps[:, sub * 128 + pr * 32: sub * 128 + pr * 32 + 32],
                                                           lhsT=him[:, sub * 128:(sub + 1) * 128], rhs=Cb[:, 1, pr, :],
                                                           start=False, stop=True)),
                      reads=[b_him, b_Cb], writes=[b_yps], sig=True)
        ut4, b_ut4 = ut4b[blk % 2]
        fw.dma(SP, ut4[:], u_scr[cols, :].rearrange("(s p) c -> p s c", p=128), reads=[buscr], writes=[b_ut4])
        y1, y2, y3 = [tmpB[:, 4 + i, :] for i in range(3)]
        b_y = b_tB[4:7]
        u2 = ut4[:].rearrange("p s c -> p (s c)")
        fw.op(P, lambda: pool.tensor_tensor(out=ut4[:], in0=ut4[:], in1=dsk[:].unsqueeze(1).broadcast_to([128, 4, 128]),
                                            op=ALU.mult), reads=[b_ut4, b_dsk], writes=[b_ut4])
        fw.op(V, lambda: vec.tensor_tensor(out=y1, in0=yps[:, :], in1=u2, op=ALU.add), reads=[b_yps, b_ut4], writes=[b_y[0]])
        fw.op(P, lambda: pool.tensor_tensor(out=y2, in0=y1, in1=y1, op=ALU.mult), reads=[b_y[0]], writes=[b_y[1]])
        fw.op(V, lambda: vec.tensor_scalar(out=y2, in0=y2, scalar1=0.044715, scalar2=1.0, op0=ALU.mult, op1=ALU.add),
              reads=[b_y[1]], writes=[b_y[1]])
        fw.op(P, lambda: pool.tensor_tensor(out=y3, in0=y2, in1=y1, op=ALU.mult), reads=[b_y[0], b_y[1]], writes=[b_y[2]])
        fw.op(S, lambda: act.activation(out=y3, in_=y3, func=AF.Exp, scale=-1.5957691216057308), reads=[b_y[2]], writes=[b_y[2]])
        fw.op(V, lambda: vec.tensor_scalar(out=y3, in0=y3, scalar1=1.0, scalar2=None, op0=ALU.add), reads=[b_y[2]], writes=[b_y[2]])
        fw.op(V, lambda: vec.reciprocal(out=y3, in_=y3), reads=[b_y[2]], writes=[b_y[2]])
        fw.op(V, lambda: vec.tensor_tensor(out=y2, in0=y1, in1=y3, op=ALU.mult), reads=[b_y[0], b_y[2]], writes=[b_y[1]])
        fw.dma(P, agin[cols, 128:256].rearrange("(s p) c -> p s c", p=128), y2.rearrange("p (s c) -> p s c", s=4),
               reads=[b_y[1]], writes=[b_zo])
    fw.dma(P, sfin_d[:, :], sfinb[0][:, 0:8], reads=[sfinb[1]], writes=[b_sfin])
    ccsem = nc.alloc_semaphore("ccsem")
    fw.sems["cc"] = ccsem
    fw._need(P, [b_oadbg, b_zo], [b_agout])
    for k in range(8):
        nc.gpsimd.collective_compute("AllGather", ALU.bypass, replica_groups=[[0, 1, 2, 3], [4, 5, 6, 7]],
                                     ins=[agin[k * 1024:(k + 1) * 1024, :]],
                                     outs=[agout[k * 4096:(k + 1) * 4096, :]]).then_inc(ccsem, 1)
    fw._mark(("cc", 8), [b_oadbg, b_zo], [b_agout])

    fw.barrier()
    class Arena:
        def __init__(self, regions):
            self.regions = regions
            self.offs = [0] * len(regions)

        def alloc(self, name, shape, dt):
            n = 1
            for d in shape[1:]:
                n *= d
            units = n if dt != BF16 else (n + 1) // 2
            units = (units + 7) // 8 * 8
            best = None
            for i, reg in enumerate(self.regions):
                rem = reg.shape[1] - self.offs[i]
                if rem >= units and (best is None or rem < best[1]):
                    best = (i, rem)
            assert best is not None, ("arena full", name, units, [r.shape[1] - o for r, o in zip(self.regions, self.offs)])
            i = best[0]
            reg = self.regions[i]
            ap = reg[0:shape[0], self.offs[i]:self.offs[i] + units]
            self.offs[i] += units
            if dt == BF16:
                ap = ap.bitcast(BF16)[:, 0:n]
            elif dt == I32:
                ap = ap.bitcast(I32)[:, 0:n]
            else:
                ap = ap[:, 0:n]
            if len(shape) == 3:
                ap = ap.rearrange("p (a b) -> p a b", a=shape[1])
            elif len(shape) == 4:
                ap = ap.rearrange("p (a b c) -> p a b c", a=shape[1], b=shape[2])
            return ap, Buf(name)
    ar = Arena([QT0[:, :].bitcast(F32), QT1[:, :].bitcast(F32), KT[:, :].bitcast(F32),
                G[:].rearrange("p a b -> p (a b)").bitcast(F32),
                Vaug[:].rearrange("p a b -> p (a b)").bitcast(F32),
                xb[0][0][:, :], xb[1][0][:, :], xb[2][0][:, :], ones512[:, :],
                ut4b[0][0][:].rearrange("p a b -> p (a b)"), ut4b[1][0][:].rearrange("p a b -> p (a b)"),
                xnb[0][0][:, :].bitcast(F32), xnb[1][0][:, :].bitcast(F32),
                hreb[0][0][:, :].bitcast(F32), hreb[1][0][:, :].bitcast(F32),
                himb[0][0][:, :].bitcast(F32), himb[1][0][:, :].bitcast(F32),
                bvs[:, :], sqb[0][0][:, :], sqb[1][0][:, :], qknb[0][0][:, :], qknb[1][0][:, :],
                btf[0][0][:, :], btf[1][0][:, :], btf[2][0][:, :], bt2[:, :],
                xTb[0][0][:].rearrange("p a b -> p (a b)").bitcast(F32), xTb[1][0][:].rearrange("p a b -> p (a b)").bitcast(F32)])
    print("static SBUF bytes/partition:", g.sb_bytes)
    NS = 16
    xst, b_xst = ar.alloc("xst", [64, DM], F32)
    xsn, b_xsn = ar.alloc("xsn", [64, DM], BF16)
    xTs, b_xTs = ar.alloc("xTs", [128, 8, 64], BF16)
    sss, b_sss = ar.alloc("sss", [64, 8], F32)
    proj, b_proj = ar.alloc("proj", [64, 2560], F32)
    g16, b_g16 = ar.alloc("g16", [64, 1024], F32)
    qkn, b_qkn = ar.alloc("qkn", [64, 1024], F32)
    qk2, b_qk2 = ar.alloc("qk2", [64, 1024], F32)
    qkb, b_qkb = ar.alloc("qkb", [64, 1024], BF16)
    r16, b_r16 = ar.alloc("r16", [64, 32], F32)
    fw.dma(SP, xst, xs_d[:, :], writes=[b_xst])
    fw.dma(SP, g16, g16_d.partition_broadcast(64), writes=[b_g16])
    fw.op(V, lambda: vec.tensor_scalar(out=g16[:, 0:512], in0=g16[:, 0:512], scalar1=0.125, scalar2=None, op0=ALU.mult),
          reads=[b_g16], writes=[b_g16])
    fw.op(S, lambda: act.activation(out=junk[0:64, :], in_=xst, func=AF.Square, accum_out=sss[:, 0:1]),
          reads=[b_xst], writes=[b_junk, b_sss])
    fw.op(S, lambda: act.activation(out=sss[:, 1:2], in_=sss[:, 0:1], func=AF.Ln, scale=1.0 / DM, bias=EPS),
          reads=[b_sss], writes=[b_sss])
    fw.op(S, lambda: act.activation(out=sss[:, 2:3], in_=sss[:, 1:2], func=AF.Exp, scale=-0.5), reads=[b_sss], writes=[b_sss])
    fw.op(S, lambda: act.activation(out=xsn, in_=xst, func=AF.Copy, scale=sss[:, 2:3]), reads=[b_xst, b_sss], writes=[b_xsn])
    pTs = banks[0][:, :].bitcast(BF16)
    for kt in range(8):
        fw.op(T, (lambda kt=kt: pe.transpose(out=pTs[:, kt * 64:(kt + 1) * 64], in_=xsn[:, kt * 128:(kt + 1) * 128],
                                             identity=ident[0:64, 0:64])), reads=[b_xsn, b_ident], writes=[bbuf[0]])
    fw.op(V, lambda: vec.tensor_copy(out=xTs.rearrange("p a b -> p (a b)"), in_=pTs[:, 0:512]), reads=[bbuf[0]], writes=[b_xTs])
    wsf = wst[:].rearrange("p a b -> p (a b)")[:, 0:4096].rearrange("p (a b) -> p a b", a=8)
    wsb = wab[:, :, 0:512]
    for ci in range(5):
        fw.dma(SP, wsf, wsm_d[:, ci, :, :], reads=[b_wab], writes=[b_wst])
        for kt in range(8):
            fw.op(V if kt % 2 else P, (lambda kt=kt: (vec if kt % 2 else pool).tensor_scalar(
                out=wsb[:, kt, :], in0=wsf[:, kt, :], scalar1=ngt[:, kt:kt + 1], scalar2=None, op0=ALU.mult)),
                reads=[b_wst, b_ng], writes=[b_wab])
        pk = banks[1 + ci % 2]
        for kt in range(8):
            fw.op(T, (lambda kt=kt, pk=pk: pe.matmul(pk[0:64, :], lhsT=xTs[:, kt, :], rhs=wsb[:, kt, :],
                                                     start=(kt == 0), stop=(kt == 7))),
                  reads=[b_xTs, b_wab], writes=[bbuf[1 + ci % 2]], sig=True)
        fw.op(S, (lambda ci=ci, pk=pk: act.activation(out=proj[:, ci * 512:(ci + 1) * 512], in_=pk[0:64, :], func=AF.Copy)),
              reads=[bbuf[1 + ci % 2]], writes=[b_proj])
    fw.op(P, lambda: pool.tensor_tensor(out=qk2, in0=proj[:, 0:1024], in1=proj[:, 0:1024], op=ALU.mult),
          reads=[b_proj], writes=[b_qk2])
    fw.op(V, lambda: vec.tensor_reduce(out=r16[:, 0:16], in_=qk2.rearrange("p (a d) -> p a d", d=64), axis=AX.X, op=ALU.add),
          reads=[b_qk2], writes=[b_r16])
    fw.op(S, lambda: act.activation(out=r16[:, 16:32], in_=r16[:, 0:16], func=AF.Ln, scale=1.0 / 64, bias=EPS),
          reads=[b_r16], writes=[b_r16])
    fw.op(S, lambda: act.activation(out=r16[:, 0:16], in_=r16[:, 16:32], func=AF.Exp, scale=-0.5), reads=[b_r16], writes=[b_r16])
    fw.op(V, lambda: vec.tensor_tensor(out=qkn.rearrange("p (a d) -> p a d", d=64),
                                       in0=proj[:, 0:1024].rearrange("p (a d) -> p a d", d=64),
                                       in1=r16[:, 0:16].unsqueeze(2).broadcast_to([64, 16, 64]), op=ALU.mult),
          reads=[b_proj, b_r16], writes=[b_qkn])
    fw.op(V, lambda: vec.tensor_tensor(out=qkn, in0=qkn, in1=g16, op=ALU.mult), reads=[b_qkn, b_g16], writes=[b_qkn])
    fw.dma(P, ks_o[:, :], qkn[:, 512:1024], reads=[b_qkn], writes=[b_kso])
    fw.dma(P, vs_o[:, :], proj[:, 1024:1536], reads=[b_proj], writes=[b_vso])
    QsT0, b_QsT = ar.alloc("QsT0", [128, 4, 64], BF16)
    QsT1, _ = ar.alloc("QsT1", [128, 4, 64], BF16)
    KnT, _ = ar.alloc("KnT", [128, 4, 64], BF16)
    fw.op(P, lambda: pool.memset(QsT0[64:128], 0.0), writes=[b_QsT])
    fw.op(P, lambda: pool.memset(QsT1[0:64], 0.0), writes=[b_QsT])
    fw.op(P, lambda: pool.tensor_copy(out=qkb, in_=qkn), reads=[b_qkn], writes=[b_qkb])
    for j in range(8):
        fw.op(T, (lambda j=j: pe.transpose(out=pTs[:, j * 64:(j + 1) * 64], in_=qkb[:, j * 128:(j + 1) * 128],
                                           identity=ident[0:64, 0:64])), reads=[b_qkb, b_ident], writes=[bbuf[0]])
    pT3 = pTs[:, 0:512].rearrange("p (a b) -> p a b", a=8)
    fw.op(V, lambda: vec.tensor_copy(out=QsT0[0:64], in_=pT3[0:64, 0:4, :]), reads=[bbuf[0]], writes=[b_QsT])
    fw.op(V, lambda: vec.tensor_copy(out=QsT1[64:128], in_=pT3[64:128, 0:4, :]), reads=[bbuf[0]], writes=[b_QsT])
    fw.op(V, lambda: vec.tensor_copy(out=KnT, in_=pT3[:, 4:8, :]), reads=[bbuf[0]], writes=[b_QsT])

    r16s = s5_setup("smp", 16, s5a16_d, s5b16_d, s5c16_d, ar.alloc)
    ub, b_ub = ar.alloc("ub", [64, 512], BF16)
    uTs, b_uTs = ar.alloc("uTs", [128, 4, 64], BF16)
    fw.op(P, lambda: pool.tensor_copy(out=ub, in_=proj[:, 2048:2560]), reads=[b_proj], writes=[b_ub])
    for j in range(4):
        fw.op(T, (lambda j=j: pe.transpose(out=pTs[:, 512 + j * 64:512 + (j + 1) * 64], in_=ub[:, j * 128:(j + 1) * 128],
                                           identity=ident[0:64, 0:64])), reads=[b_ub, b_ident], writes=[bbuf[0]])
    fw.op(V, lambda: vec.tensor_copy(out=uTs.rearrange("p a b -> p (a b)"), in_=pTs[:, 512:768]), reads=[bbuf[0]], writes=[b_uTs])
    for ri in range(2):
        for pr in range(16):
            bk = 4 + 2 * ri + pr // 8
            fw.op(T, (lambda ri=ri, pr=pr, bk=bk: pe.matmul(banks[bk][:, (pr % 8) * 64:(pr % 8 + 1) * 64],
                                                            lhsT=r16s.BT[:, ri * 16 + pr, :], rhs=uTs[:, pr // 4, :],
                                                            start=True, stop=True)),
                  reads=[r16s.b_BT, b_uTs], writes=[bbuf[bk]])
    hst, b_hst = ar.alloc("hst", [128, 2, 16, NS], F32)
    fw.dma(SP, hst, hst_d[:, :, :, :], writes=[b_hst])
    hs, b_hs = ar.alloc("hs", [128, 2, 16, 64], BF16)
    tq = [ar.alloc("tq%d" % i, [128, 16, NS], F32) for i in range(4)]
    LRb = r16s.LR.unsqueeze(2).broadcast_to([128, 16, NS])
    LIb = r16s.LI.unsqueeze(2).broadcast_to([128, 16, NS])
    def bu_t(ri, t, half):
        bk = 4 + 2 * ri + half
        return banks[bk][:, :].rearrange("p (a s t) -> p a s t", a=8, t=4)[:, :, :, t]
    for t in range(4):
        hr, hi = hst[:, 0], hst[:, 1]
        rd = [b_hst, r16s.b_W]
        fw.op(V, lambda: vec.tensor_tensor(out=tq[0][0], in0=hr, in1=LRb, op=ALU.mult), reads=rd, writes=[tq[0][1]])
        fw.op(P, lambda: pool.tensor_tensor(out=tq[1][0], in0=hi, in1=LIb, op=ALU.mult), reads=rd, writes=[tq[1][1]])
        fw.op(V, lambda: vec.tensor_tensor(out=tq[2][0], in0=hi, in1=LRb, op=ALU.mult), reads=rd, writes=[tq[2][1]])
        fw.op(P, lambda: pool.tensor_tensor(out=tq[3][0], in0=hr, in1=LIb, op=ALU.mult), reads=rd, writes=[tq[3][1]])
        fw.op(V, lambda: vec.tensor_tensor(out=tq[0][0], in0=tq[0][0], in1=tq[1][0], op=ALU.subtract),
              reads=[tq[0][1], tq[1][1]], writes=[tq[0][1]])
        fw.op(V, lambda: vec.tensor_tensor(out=tq[2][0], in0=tq[2][0], in1=tq[3][0], op=ALU.add),
              reads=[tq[2][1], tq[3][1]], writes=[tq[2][1]])
        for half in range(2):
            sl = slice(half * 8, (half + 1) * 8)
            fw.op(V, (lambda t=t, half=half, sl=sl: vec.tensor_tensor(out=hst[:, 0, sl, :], in0=bu_t(0, t, half),
                                                                      in1=tq[0][0][:, sl, :], op=ALU.add)),
                  reads=[bbuf[4 + half], tq[0][1]], writes=[b_hst])
            fw.op(V, (lambda t=t, half=half, sl=sl: vec.tensor_tensor(out=hst[:, 1, sl, :], in0=bu_t(1, t, half),
                                                                      in1=tq[2][0][:, sl, :], op=ALU.add)),
                  reads=[bbuf[6 + half], tq[2][1]], writes=[b_hst])
        fw.op(P, (lambda t=t: pool.tensor_copy(out=hs.rearrange("p r a (s t) -> p r a s t", t=4)[:, :, :, :, t], in_=hst)),
              reads=[b_hst], writes=[b_hs])
    fw.dma(P, sss_o[:, :, :, :], hst, reads=[b_hst], writes=[b_ssso])
    yS = banks[2]
    for pr in range(16):
        fw.op(T, (lambda pr=pr: pe.matmul(yS[0:64, pr * 32:(pr + 1) * 32], lhsT=hs[:, 0, pr, :], rhs=r16s.Cb[:, 0, pr, :],
                                          start=True, stop=False)), reads=[b_hs, r16s.b_Cb], writes=[bbuf[2]], sig=False)
        fw.op(T, (lambda pr=pr: pe.matmul(yS[0:64, pr * 32:(pr + 1) * 32], lhsT=hs[:, 1, pr, :], rhs=r16s.Cb[:, 1, pr, :],
                                          start=False, stop=True)), reads=[b_hs, r16s.b_Cb], writes=[bbuf[2]], sig=True)
    d16, b_d16 = ar.alloc("d16", [64, 512], F32)
    ozs, b_ozs = ar.alloc("ozs", [64, 1024], F32)
    zt = [ar.alloc("zt%d" % i, [64, 512], F32) for i in range(3)]
    fw.dma(SP, d16, dsk16_d.partition_broadcast(64), writes=[b_d16])
    y1, y2, y3 = zt[0][0], zt[1][0], zt[2][0]
    bz = [z_[1] for z_ in zt]
    fw.op(P, lambda: pool.tensor_tensor(out=y2, in0=proj[:, 2048:2560], in1=d16, op=ALU.mult), reads=[b_proj, b_d16], writes=[bz[1]])
    fw.op(V, lambda: vec.tensor_tensor(out=y1, in0=yS[0:64, :], in1=y2, op=ALU.add), reads=[bbuf[2], bz[1]], writes=[bz[0]])
    fw.op(P, lambda: pool.tensor_tensor(out=y2, in0=y1, in1=y1, op=ALU.mult), reads=[bz[0]], writes=[bz[1]])
    fw.op(V, lambda: vec.tensor_scalar(out=y2, in0=y2, scalar1=0.044715, scalar2=1.0, op0=ALU.mult, op1=ALU.add),
          reads=[bz[1]], writes=[bz[1]])
    fw.op(P, lambda: pool.tensor_tensor(out=y3, in0=y2, in1=y1, op=ALU.mult), reads=[bz[0], bz[1]], writes=[bz[2]])
    fw.op(S, lambda: act.activation(out=y3, in_=y3, func=AF.Exp, scale=-1.5957691216057308), reads=[bz[2]], writes=[bz[2]])
    fw.op(V, lambda: vec.tensor_scalar(out=y3, in0=y3, scalar1=1.0, scalar2=None, op0=ALU.add), reads=[bz[2]], writes=[bz[2]])
    fw.op(V, lambda: vec.reciprocal(out=y3, in_=y3), reads=[bz[2]], writes=[bz[2]])
    fw.op(V, lambda: vec.tensor_tensor(out=ozs[:, 512:1024], in0=y1, in1=y3, op=ALU.mult), reads=[bz[0], bz[2]], writes=[b_ozs])

    pti, b_pti = ar.alloc("pti", [128, 256], I32)
    ptf, b_ptf = ar.alloc("ptf", [128, 256], F32)
    idx, b_idx = ar.alloc("idx", [128, 256], I32)
    pcol, b_pcol = ar.alloc("pcol", [128, 8], F32)
    fw.dma(SP, pti, pt_d.partition_broadcast(128), writes=[b_pti])
    fw.dma(SP, pcol[:, 0:1], pcol_d[:, :], writes=[b_pcol])
    fw.op(V, lambda: vec.tensor_copy(out=ptf, in_=pti), reads=[b_pti], writes=[b_ptf])
    fw.op(V, lambda: vec.tensor_scalar(out=ptf, in0=ptf, scalar1=128.0, scalar2=pcol[:, 0:1], op0=ALU.mult, op1=ALU.add),
          reads=[b_ptf, b_pcol], writes=[b_ptf])
    fw.op(V, lambda: vec.tensor_copy(out=idx, in_=ptf), reads=[b_ptf], writes=[b_idx])
    bfar, b_bfar = ar.alloc("bfar", [128, 32], F32)
    b15, b_b15 = ar.alloc("b15", [128, 32], F32)
    bN, b_bN = ar.alloc("bN", [4, 32], F32)
    oh15, b_oh15 = ar.alloc("oh15", [32, 4, 128], F32)
    ohn, b_ohn = ar.alloc("ohn", [33, 4, 4], F32)
    rb33, b_rb33 = ar.alloc("rb33", [33, 4], F32)
    Dm, b_Dm = ar.alloc("Dm", [8, 8], F32)
    fw.dma(SP, bfar, rbfar_d.partition_broadcast(128), writes=[b_bfar])
    fw.dma(SP, oh15, oh15_d[:, :, :], writes=[b_oh15])
    fw.dma(SP, ohn, ohn_d[:, :, :], writes=[b_ohn])
    fw.dma(SP, rb33, rb33_d[:, :], writes=[b_rb33])
    fw.dma(SP, Dm, dm_d[:, :], writes=[b_Dm])
    fw.op(V, lambda: vec.scalar_tensor_tensor(out=Dm[:, 0:4], in0=Dm[:, 4:8], scalar=lamw[0:8, 5:6], in1=Dm[:, 0:4],
                                              op0=ALU.mult, op1=ALU.add), reads=[b_Dm, b_lamw], writes=[b_Dm])
    for qi in range(4):
        fw.op(T, (lambda qi=qi: pe.matmul(banks[3][:, qi * 4:(qi + 1) * 4], lhsT=oh15[:, qi, :], rhs=rb33[0:32, :],
                                          start=True, stop=True)), reads=[b_oh15, b_rb33], writes=[bbuf[3]])
        fw.op(T, (lambda qi=qi: pe.matmul(banks[3][0:4, 64 + qi * 4:64 + (qi + 1) * 4], lhsT=ohn[:, qi, :], rhs=rb33[:, :],
                                          start=True, stop=True)), reads=[b_ohn, b_rb33], writes=[bbuf[3]])
    for c in range(2):
        fw.op(V, (lambda c=c: vec.tensor_copy(
            out=b15.rearrange("p (h c q) -> p h c q", h=4, c=2)[:, :, c, :],
            in_=banks[3][:, 0:16].rearrange("p (q h) -> p h q", q=4))), reads=[bbuf[3]], writes=[b_b15])
        fw.op(V, (lambda c=c: vec.tensor_copy(
            out=bN.rearrange("p (h c q) -> p h c q", h=4, c=2)[:, :, c, :],
            in_=banks[3][0:4, 64:80].rearrange("p (q h) -> p h q", q=4))), reads=[bbuf[3]], writes=[b_bN])
    kfb = [ar.alloc("kf%d" % i, [128, 512], F32) for i in range(4)]
    vpb = [ar.alloc("vp%d" % i, [128, 4, 130], F32) for i in range(3)]
    print("arena remaining:", [r.shape[1] - o for r, o in zip(ar.regions, ar.offs)])
    vfb = [ar.alloc("vf%d" % i, [128, 512], F32) for i in range(3)]
    kbb = [ar.alloc("kb%d" % i, [128, 512], BF16) for i in range(2)]
    kTb = [ar.alloc("kT%d" % i, [128, 4, 128], BF16) for i in range(2)]
    sbb = [ar.alloc("sb%d" % i, [128, 32], F32) for i in range(2)]
    ptb_ = [ar.alloc("pts%d" % i, [128, 32], F32) for i in range(2)]
    vn, b_vn = ar.alloc("vn", [4, 4, 130], F32)
    onr, b_onr = ar.alloc("onr", [8, 4, 128], F32)
    rl8, b_rl8 = ar.alloc("rl8", [8, 4], F32)
    osb, b_osb = zt[2][0][0:4, :], zt[2][1]
    for i in range(3):
        fw.op(P, (lambda i=i: pool.memset(vpb[i][0][:, :, 128:130], 1.0)), writes=[vpb[i][1]])
    fw.op(P, lambda: pool.memset(vn[:, :, 128:130], 1.0), writes=[b_vn])
    ck2 = ck_d
    cv3 = cv_d.rearrange("r (h e) -> r h e", h=4)
    QsTm = (QsT0, QsT1)
    b_pKs = [Buf("pK0"), Buf("pK1")]

    def smp_page(sq, pg, pg_i):
        fw.stage = 1
        tok = slice(sq * 4, sq * 4 + 4)
        Sp = banks[pg_i % 2]
        b_Sp = bbuf[pg_i % 2]
        sb_, b_sb = sbb[pg_i % 2]
        pts, b_pts = ptb_[pg_i % 2]
        if pg < 16:
            kf, b_kf = kfb[pg_i % 4]
            vp_, b_vp = vpb[pg_i % 3]
            kb, b_kb = kbb[pg_i % 2]
            kT, b_kT = kTb[pg_i % 2]
            col = sq * 16 + pg
            fw.idma(kf, ck2, idx[:, col:col + 1], reads=[b_idx], writes=[b_kf])
            vf, b_vf = vfb[pg_i % 3]
            fw.idma(vf, cv_d, idx[:, col:col + 1], reads=[b_idx], writes=[b_vf])
            fw.op(S, lambda: act.activation(out=vp_[:, :, 0:128], in_=vf.rearrange("p (h e) -> p h e", h=4), func=AF.Copy),
                  reads=[b_vf], writes=[b_vp])
            fw.op(S, lambda: act.activation(out=kb, in_=kf, func=AF.Copy), reads=[b_kf], writes=[b_kb])
            fw.stage = 2
            pK = banks[2][:, (pg_i % 2) * 256:(pg_i % 2 + 1) * 256].bitcast(BF16)
            for h in range(4):
                fw.op(T, (lambda h=h, pK=pK, kb=kb: pe.transpose(out=pK[:, h * 128:(h + 1) * 128],
                                                               in_=kb[:, h * 128:(h + 1) * 128], identity=ident[:])),
                      reads=[b_kb, b_ident], writes=[b_pKs[pg_i % 2]], sig=(h == 3))
            fw.op(V, lambda: vec.tensor_copy(out=kT.rearrange("p a b -> p (a b)"), in_=pK[:, :]), reads=[b_pKs[pg_i % 2]], writes=[b_kT])
            fw.stage = 3
            for h in range(4):
                for c in range(2):
                    fw.op(T, (lambda h=h, c=c, kT=kT, Sp=Sp: pe.matmul(
                        Sp[:, h * 8 + c * 4:h * 8 + c * 4 + 4], lhsT=kT[:, h, :], rhs=QsTm[c][:, h, tok],
                        start=True, stop=True)), reads=[b_kT, b_QsT], writes=[b_Sp], sig=(h == 3 and c == 1))
            fw.stage = 3
            bias_t, b_bias = (b15, b_b15) if pg == 15 else (bfar, b_bfar)
            fw.op(V, lambda: vec.tensor_tensor(out=sb_, in0=Sp[:, 0:32], in1=bias_t, op=ALU.add),
                  reads=[b_Sp, b_bias], writes=[b_sb])
            fw.op(S, lambda: act.activation(out=pts, in_=sb_, func=AF.Exp), reads=[b_sb], writes=[b_pts])
            fw.stage = 4
            for h in range(4):
                fw.op(T, (lambda h=h, pts=pts, vp_=vp_, pg=pg: pe.matmul(
                    banks[4 + h][0:8, 0:129], lhsT=pts[:, h * 8:(h + 1) * 8], rhs=vp_[:, h, 0:129],
                    start=(pg == 0), stop=False)), reads=[b_pts, b_vp], writes=[bbuf[4 + h]], sig=True)
        else:
            fw.dma(SP, vn[:, :, 0:128], vs_o[tok, :].rearrange("t (h e) -> t h e", h=4), reads=[b_vso], writes=[b_vn])
            fw.stage = 3
            for h in range(4):
                for c in range(2):
                    fw.op(T, (lambda h=h, c=c, Sp=Sp: pe.matmul(
                        Sp[0:4, h * 8 + c * 4:h * 8 + c * 4 + 4], lhsT=KnT[:, h, tok], rhs=QsTm[c][:, h, tok],
                        start=True, stop=True)), reads=[b_QsT], writes=[b_Sp], sig=(h == 3 and c == 1))
            fw.stage = 3
            fw.op(V, lambda: vec.tensor_tensor(out=sb_[0:4, :], in0=Sp[0:4, 0:32], in1=bN, op=ALU.add),
                  reads=[b_Sp, b_bN], writes=[b_sb])
            fw.op(S, lambda: act.activation(out=pts[0:4, :], in_=sb_[0:4, :], func=AF.Exp), reads=[b_sb], writes=[b_pts])
            fw.stage = 4
            for h in range(4):
                fw.op(T, (lambda h=h, pts=pts: pe.matmul(
                    banks[4 + h][0:8, 0:129], lhsT=pts[0:4, h * 8:(h + 1) * 8], rhs=vn[:, h, 0:129],
                    start=False, stop=True)), reads=[b_pts, b_vn], writes=[bbuf[4 + h]], sig=True)

    def smp_fin(sq):
        tok = slice(sq * 4, sq * 4 + 4)
        for h in range(4):
            fw.op(V, (lambda h=h: vec.reciprocal(out=rl8[:, h:h + 1], in_=banks[4 + h][0:8, 128:129])),
                  reads=[bbuf[4 + h]], writes=[b_rl8])
            fw.op(V, (lambda h=h: vec.tensor_scalar(out=onr[:, h, :], in0=banks[4 + h][0:8, 0:128], scalar1=rl8[:, h:h + 1],
                                                    scalar2=None, op0=ALU.mult)), reads=[bbuf[4 + h], b_rl8], writes=[b_onr])
        for h in range(4):
            fw.op(T, (lambda h=h: pe.matmul(banks[3][0:4, h * 128:(h + 1) * 128], lhsT=Dm[:, 0:4], rhs=onr[:, h, :],
                                            start=True, stop=True)), reads=[b_Dm, b_onr], writes=[bbuf[3]])
        fw.op(V, lambda: vec.tensor_copy(out=osb, in_=banks[3][0:4, :]), reads=[bbuf[3]], writes=[b_osb])
        fw.dma(SP, oraw[tok, :], osb, reads=[b_osb], writes=[b_oraw])

    pl2 = [(sq, pg) for sq in range(NS) for pg in range(17)]
    for k in range(-3, len(pl2)):
        for st, off in ((4, 0), (3, 1), (2, 2), (1, 3)):
            n = k + off
            if 0 <= n < len(pl2):
                fw.only = st
                smp_page(pl2[n][0], pl2[n][1], n)
                if st == 4 and pl2[n][1] == 16:
                    fw.only = None
                    smp_fin(pl2[n][0])
    fw.only = None

    ot, b_ot = kfb[0][0][0:64, :], kfb[0][1]
    fw.dma(SP, ot, oraw[:, :], reads=[b_oraw], writes=[b_ot])
    sg4, b_sg4 = kfb[1][0][0:64, :], kfb[1][1]
    fw.dma(SP, sg4, sg4_d.partition_broadcast(64), writes=[b_sg4])
    gt = [zt[0], zt[1]]
    g1, g2 = gt[0][0], gt[1][0]
    bg = [gt[0][1], gt[1][1]]
    gaS = proj[:, 1536:2048]
    fw.op(S, lambda: act.activation(out=g1, in_=gaS, func=AF.Exp, scale=-1.0), reads=[b_proj], writes=[bg[0]])
    fw.op(V, lambda: vec.tensor_scalar(out=g1, in0=g1, scalar1=1.0, scalar2=None, op0=ALU.add), reads=[bg[0]], writes=[bg[0]])
    fw.op(V, lambda: vec.reciprocal(out=g1, in_=g1), reads=[bg[0]], writes=[bg[0]])
    fw.op(V, lambda: vec.tensor_tensor(out=g1, in0=g1, in1=gaS, op=ALU.mult), reads=[bg[0], b_proj], writes=[bg[0]])
    fw.op(V, lambda: vec.tensor_tensor(out=g1, in0=g1, in1=sg4, op=ALU.mult), reads=[bg[0], b_sg4], writes=[bg[0]])
    fw.op(V, lambda: vec.tensor_scalar(out=g1, in0=g1, scalar1=0.8, scalar2=None, op0=ALU.mult), reads=[bg[0]], writes=[bg[0]])
    fw.op(P, lambda: pool.tensor_tensor(out=g2, in0=ot, in1=ot, op=ALU.mult), reads=[b_ot], writes=[bg[1]])
    fw.op(V, lambda: vec.tensor_reduce(out=sss[:, 0:4], in_=g2.rearrange("p (a d) -> p a d", d=128), axis=AX.X, op=ALU.add),
          reads=[bg[1]], writes=[b_sss])
    fw.op(S, lambda: act.activation(out=sss[:, 4:8], in_=sss[:, 0:4], func=AF.Ln, scale=1.0 / 128, bias=EPS),
          reads=[b_sss], writes=[b_sss])
    fw.op(S, lambda: act.activation(out=sss[:, 0:4], in_=sss[:, 4:8], func=AF.Exp, scale=-0.5), reads=[b_sss], writes=[b_sss])
    fw.op(V, lambda: vec.tensor_tensor(out=g2.rearrange("p (a d) -> p a d", d=128), in0=ot.rearrange("p (a d) -> p a d", d=128),
                                       in1=sss[:, 0:4].unsqueeze(2).broadcast_to([64, 4, 128]), op=ALU.mult),
          reads=[b_ot, b_sss], writes=[bg[1]])
    fw.op(V, lambda: vec.tensor_tensor(out=ozs[:, 0:512], in0=g2, in1=g1, op=ALU.mult), reads=[bg[0], bg[1]], writes=[b_ozs])
    fw.dma(P, ozs_scr[:, :], ozs, reads=[b_ozs], writes=[b_ozscr])

    fw.barrier()
    ar2 = Arena(ar.regions)
    wgs, b_wgs = ar2.alloc("wgs", [128, 8, 512], BF16)
    wglu, b_wglu = ar2.alloc("wglu", [128, 4, 512], BF16)
    wout, b_wout = ar2.alloc("wout", [128, 8, 1024], BF16)
    bglu, b_bglu = ar2.alloc("bglu", [128, 512], F32)
    idxb, b_idxb = ar2.alloc("idxb", [128, 64], I32)
    fw.dma(SP, bglu, bglu_d.partition_broadcast(128), writes=[b_bglu])
    fw.dma(SP, idxb, idxb_d[:, :], writes=[b_idxb])
    fw.dma(SP, wsf, wgs_d[:, :, :], reads=[b_wab], writes=[b_wst])
    for kt in range(8):
        fw.op(V if kt % 2 else P, (lambda kt=kt: (vec if kt % 2 else pool).tensor_scalar(
            out=wgs[:, kt, :], in0=wsf[:, kt, :], scalar1=ngt[:, kt:kt + 1], scalar2=None, op0=ALU.mult)),
            reads=[b_wst, b_ng], writes=[b_wgs])
    fw.dma(SP, wsf[:, 0:4, :], wglu_d[:, :, :], writes=[b_wst])
    fw.op(V, lambda: vec.tensor_copy(out=wglu, in_=wsf[:, 0:4, :]), reads=[b_wst], writes=[b_wglu])
    for hf in range(2):
        fw.dma(SP, wsf, wout_d[:, hf, :, :], writes=[b_wst])
        for kt in range(8):
            fw.op(V if kt % 2 else P, (lambda kt=kt, hf=hf: (vec if kt % 2 else pool).tensor_copy(
                out=wout[:, kt, hf * 512:(hf + 1) * 512], in_=wsf[:, kt, :])), reads=[b_wst], writes=[b_wout])
    Bx = [ar2.alloc("Bx%d" % i, [128, DM], F32) for i in range(3)]
    Bgst = [ar2.alloc("Bg%d" % i, [128, 4, 256], F32) for i in range(2)]
    Boz = [ar2.alloc("Boz%d" % i, [128, DM], F32) for i in range(2)]
    Bxn, b_Bxn = ar2.alloc("Bxn", [128, DM], BF16)
    BxT, b_BxT = ar2.alloc("BxT", [128, 8, 128], BF16)
    Bss, b_Bss = ar2.alloc("Bss", [128, 4], F32)
    Bsg, b_Bsg = ar2.alloc("Bsg", [128, 512], F32)
    Bzb, b_Bzb = ar2.alloc("Bzb", [128, 512], BF16)
    BzT, b_BzT = ar2.alloc("BzT", [128, 4, 128], BF16)
    Bpr, b_Bpr = ar2.alloc("Bpr", [128, 512], F32)
    Bobs = [ar2.alloc("Bob%d" % i, [128, DM], BF16) for i in range(2)]
    BoT, b_BoT = ar2.alloc("BoT", [128, 8, 128], BF16)
    By, b_By = ar2.alloc("By", [128, DM], F32)
    fw.op(P, lambda: pool.memset(Boz[0][0], 0.0), writes=[Boz[0][1]])
    def phb(t):
        fw.stage = 1
        rows = slice(t * 128, (t + 1) * 128)
        xt, b_xt = Bx[t % 3]
        Bob, b_Bob = Bobs[t % 2]
        gst, b_gst = Bgst[t % 2]
        ozt, b_oz = Boz[t % 2]
        pT = banks[0][:, :].bitcast(BF16); b_pT = bbuf[0]
        pG = banks[1 if t % 2 == 0 else 7]; b_pG = bbuf[1 if t % 2 == 0 else 7]
        pL = banks[2]; b_pL = bbuf[2]
        pO = banks[3][:, :].bitcast(BF16); b_pO = bbuf[3]
        pZ = banks[6][:, :].bitcast(BF16); b_pZ = bbuf[6]
        pY = (banks[4], banks[5]); b_pY = (bbuf[4], bbuf[5])
        fw.dma(SP, xt, xq[rows, :], writes=[b_xt])
        if t < 16:
            for r in range(4):
                fw.idma(gst[:, r, :], agout, idxb[:, t * 4 + r:t * 4 + r + 1], reads=[b_idxb, b_agout], writes=[b_gst])
            fw.op(P, lambda: pool.tensor_copy(out=ozt[:, 0:512].rearrange("p (h e) -> p h e", h=4), in_=gst[:, :, 0:128]),
                  reads=[b_gst], writes=[b_oz])
            fw.op(V, lambda: vec.tensor_copy(out=ozt[:, 512:1024].rearrange("p (h e) -> p h e", h=4), in_=gst[:, :, 128:256]),
                  reads=[b_gst], writes=[b_oz])
        else:
            fw.dma(SP, ozt[0:64, :], ozs_scr[:, :], reads=[b_ozscr], writes=[b_oz])
        fw.op(S, lambda: act.activation(out=junk[:], in_=xt, func=AF.Square, accum_out=Bss[:, 0:1]),
              reads=[b_xt], writes=[b_junk, b_Bss])
        fw.op(S, lambda: act.activation(out=Bss[:, 1:2], in_=Bss[:, 0:1], func=AF.Ln, scale=1.0 / DM, bias=EPS),
              reads=[b_Bss], writes=[b_Bss])
        fw.op(S, lambda: act.activation(out=Bss[:, 2:3], in_=Bss[:, 1:2], func=AF.Exp, scale=-0.5), reads=[b_Bss], writes=[b_Bss])
        fw.op(S, lambda: act.activation(out=Bxn, in_=xt, func=AF.Copy, scale=Bss[:, 2:3]), reads=[b_xt, b_Bss], writes=[b_Bxn])
        for kt in range(8):
            fw.op(T, (lambda kt=kt: pe.transpose(out=pT[:, kt * 128:(kt + 1) * 128], in_=Bxn[:, kt * 128:(kt + 1) * 128],
                                                 identity=ident[:])), reads=[b_Bxn, b_ident], writes=[b_pT], sig=(kt == 7))
        fw.op(V, lambda: vec.tensor_copy(out=BxT.rearrange("p a b -> p (a b)"), in_=pT[:, :]), reads=[b_pT], writes=[b_BxT])
        for kt in range(8):
            fw.op(T, (lambda kt=kt: pe.matmul(pG[:, :], lhsT=BxT[:, kt, :], rhs=wgs[:, kt, :], start=(kt == 0), stop=(kt == 7))),
                  reads=[b_BxT, b_wgs], writes=[b_pG], sig=(kt == 7))
        fw.stage = 2
        fw.op(S, lambda: act.activation(out=Bsg, in_=pG[:, :], func=AF.Exp, scale=-1.0), reads=[b_pG], writes=[b_Bsg])
        fw.op(S, lambda: act.activation(out=Bsg, in_=Bsg, func=AF.Ln, bias=1.0), reads=[b_Bsg], writes=[b_Bsg])
        fw.op(S, lambda: act.activation(out=Bsg, in_=Bsg, func=AF.Exp, scale=-1.0), reads=[b_Bsg], writes=[b_Bsg])
        fw.op(V, lambda: vec.tensor_tensor(out=Bsg, in0=pG[:, :], in1=Bsg, op=ALU.mult), reads=[b_pG, b_Bsg], writes=[b_Bsg])
        fw.op(P, lambda: pool.tensor_copy(out=Bzb, in_=ozt[:, 512:1024]), reads=[b_oz], writes=[b_Bzb])
        for kt in range(4):
            fw.op(T, (lambda kt=kt: pe.transpose(out=pZ[:, kt * 128:(kt + 1) * 128], in_=Bzb[:, kt * 128:(kt + 1) * 128],
                                                 identity=ident[:])), reads=[b_Bzb, b_ident], writes=[b_pZ], sig=(kt == 3))
        fw.op(V, lambda: vec.tensor_copy(out=BzT.rearrange("p a b -> p (a b)"), in_=pZ[:, 0:512]), reads=[b_pZ], writes=[b_BzT])
        for kt in range(4):
            fw.op(T, (lambda kt=kt: pe.matmul(pL[:, :], lhsT=BzT[:, kt, :], rhs=wglu[:, kt, :], start=(kt == 0), stop=(kt == 3))),
                  reads=[b_BzT, b_wglu], writes=[b_pL], sig=(kt == 3))
        fw.op(V, lambda: vec.tensor_tensor(out=Bpr, in0=pL[:, :], in1=bglu, op=ALU.add), reads=[b_pL, b_bglu], writes=[b_Bpr])
        fw.op(S, lambda: act.activation(out=Bpr, in_=Bpr, func=AF.Exp, scale=-1.0), reads=[b_Bpr], writes=[b_Bpr])
        fw.op(S, lambda: act.activation(out=Bpr, in_=Bpr, func=AF.Ln, bias=1.0), reads=[b_Bpr], writes=[b_Bpr])
        fw.op(S, lambda: act.activation(out=Bpr, in_=Bpr, func=AF.Exp, scale=-1.0), reads=[b_Bpr], writes=[b_Bpr])
        fw.op(P, lambda: pool.tensor_tensor(out=Bpr, in0=Bpr, in1=ozt[:, 512:1024], op=ALU.mult), reads=[b_Bpr, b_oz], writes=[b_Bpr])
        fw.op(P, lambda: pool.tensor_tensor(out=Bob[:, 512:1024], in0=Bpr, in1=Bsg, op=ALU.mult), reads=[b_Bpr, b_Bsg], writes=[b_Bob])
        fw.op(P, lambda: pool.tensor_copy(out=Bob[:, 0:512], in_=ozt[:, 0:512]), reads=[b_oz], writes=[b_Bob])
        fw.stage = 3
        for kt in range(8):
            fw.op(T, (lambda kt=kt: pe.transpose(out=pO[:, kt * 128:(kt + 1) * 128], in_=Bob[:, kt * 128:(kt + 1) * 128],
                                                 identity=ident[:])), reads=[b_Bob, b_ident], writes=[b_pO], sig=(kt == 7))
        fw.op(V, lambda: vec.tensor_copy(out=BoT.rearrange("p a b -> p (a b)"), in_=pO[:, :]), reads=[b_pO], writes=[b_BoT])
        for nb in range(2):
            for kt in range(8):
                fw.op(T, (lambda kt=kt, nb=nb: pe.matmul(pY[nb][:, :], lhsT=BoT[:, kt, :], rhs=wout[:, kt, nb * 512:(nb + 1) * 512],
                                                         start=(kt == 0), stop=(kt == 7))),
                      reads=[b_BoT, b_wout], writes=[b_pY[nb]], sig=(kt == 7))
            fw.op(V, (lambda nb=nb: vec.tensor_tensor(out=By[:, nb * 512:(nb + 1) * 512], in0=pY[nb][:, :],
                                                      in1=xt[:, nb * 512:(nb + 1) * 512], op=ALU.add)),
                  reads=[b_pY[nb], b_xt], writes=[b_By])
        fw.dma(P, yq[rows, :], By, reads=[b_By], writes=[b_yq])
    fw.only = 1
    phb(0)
    phb(1)
    fw.only = 2
    phb(0)
    for t in range(NTB):
        if t + 2 < NTB:
            fw.only = 1
            phb(t + 2)
        if t + 1 < NTB:
            fw.only = 2
            phb(t + 1)
        fw.only = 3
        phb(t)
    fw.only = None
    fw.barrier()
    return nc


NTB = 17


def build_b():
    nc = bass.Bass("TRN2", target_bir_lowering=False)
    g = B()
    g.nc = nc
    g.uid = 0
    fw = FW(nc)
    V, S, P, T, SP = fw.dve, fw.act, fw.pool, fw.pe, fw.sp
    vec, act, pool, pe = nc.vector, nc.scalar, nc.gpsimd, nc.tensor

    def din(name, shape, dt=F32):
        return nc.dram_tensor(name, list(shape), dt, kind="ExternalInput").ap()
    xq = din("xq", [NTB * 128, DM])
    oz = din("oz", [NTB * 128, DM])
    wgs_d = din("wgs", [128, 8, 512])
    wglu_d = din("wglu", [128, 4, 512])
    wout_d = din("wout", [128, 8, 1024])
    bglu_d = din("bglu", [512])
    ng = din("ng", [128, 8])
    ident_d = din("ident", [128, 128])
    yq = nc.dram_tensor("yq", [NTB * 128, DM], F32, kind="ExternalOutput").ap()
    b_yq = Buf("yq")
    banks = [nc.alloc_psum_tensor("bank%d" % i, [128, 512], F32) for i in range(8)]
    bbuf = [Buf("bank%d" % i) for i in range(8)]

    ident_f, b_identf = _sb(g, "identf", [128, 128], F32)
    ident, b_ident = _sb(g, "ident", [128, 128], BF16)
    fw.dma(SP, ident_f[:], ident_d[:, :], writes=[b_identf])
    fw.op(V, lambda: vec.tensor_copy(out=ident[:], in_=ident_f[:]), reads=[b_identf], writes=[b_ident])
    ngt, b_ng = _sb(g, "ng", [128, 8], F32)
    fw.dma(SP, ngt[:], ng[:, :], writes=[b_ng])
    bglu, b_bglu = _sb(g, "bglu", [128, 512], F32)
    fw.dma(SP, bglu[:], bglu_d.partition_broadcast(128), writes=[b_bglu])
    wst, b_wst = _sb(g, "wst", [128, 8, 1024], F32)
    wgs, b_wgs = _sb(g, "wgs", [128, 8, 512], BF16)
    wglu, b_wglu = _sb(g, "wglu", [128, 4, 512], BF16)
    wout, b_wout = _sb(g, "wout", [128, 8, 1024], BF16)
    fw.dma(SP, wst[:, :, 0:512], wgs_d[:, :, :], writes=[b_wst])
    for kt in range(8):
        fw.op(V, (lambda kt=kt: vec.tensor_scalar(out=wgs[:, kt, :], in0=wst[:, kt, 0:512], scalar1=ngt[:, kt:kt + 1],
                                                  scalar2=None, op0=ALU.mult)), reads=[b_wst, b_ng], writes=[b_wgs])
    fw.dma(SP, wst[:, 0:4, 0:512], wglu_d[:, :, :], reads=[], writes=[b_wst])
    fw.op(V, lambda: vec.tensor_copy(out=wglu[:], in_=wst[:, 0:4, 0:512]), reads=[b_wst], writes=[b_wglu])
    fw.dma(SP, wst[:], wout_d[:, :, :], writes=[b_wst])
    for kt in range(8):
        fw.op(V if kt % 2 else P, (lambda kt=kt: (vec if kt % 2 else pool).tensor_copy(out=wout[:, kt, :], in_=wst[:, kt, :])),
              reads=[b_wst], writes=[b_wout])

    xb = _sb(g, "xt", [128, DM], F32, 2)
    ozb = _sb(g, "ozt", [128, DM], F32, 2)
    xnb = _sb(g, "xn", [128, DM], BF16, 2)
    xTb = _sb(g, "xT", [128, 8, 128], BF16, 2)
    junk, b_junk = _sb(g, "junk", [128, DM], BF16)
    ssb = _sb(g, "ss", [128, 4], F32, 2)
    sgb = _sb(g, "sg", [128, 512], F32, 2)
    zbb = _sb(g, "zb", [128, 512], BF16, 2)
    zTb = _sb(g, "zT", [128, 4, 128], BF16, 2)
    prb = _sb(g, "pr", [128, 512], F32, 2)
    obb = _sb(g, "ob", [128, DM], BF16, 2)
    oTb = _sb(g, "oT", [128, 8, 128], BF16, 2)
    yb = _sb(g, "y", [128, DM], F32, 2)
    for t in range(NTB):
        rows = slice(t * 128, (t + 1) * 128)
        xt, b_xt = xb[t % 2]
        ozt, b_oz = ozb[t % 2]
        xn, b_xn = xnb[t % 2]
        xT, b_xT = xTb[t % 2]
        ss, b_ss = ssb[t % 2]
        sg, b_sg = sgb[t % 2]
        zb, b_zb = zbb[t % 2]
        zT, b_zT = zTb[t % 2]
        pr, b_pr = prb[t % 2]
        ob, b_ob = obb[t % 2]
        oT, b_oT = oTb[t % 2]
        y, b_y = yb[t % 2]
        pT = banks[0][:, :].bitcast(BF16); b_pT = bbuf[0]
        pG = banks[1 if t % 2 == 0 else 7]; b_pG = bbuf[1 if t % 2 == 0 else 7]
        pL = banks[2]; b_pL = bbuf[2]
        pO = banks[3][:, :].bitcast(BF16); b_pO = bbuf[3]
        pZ = banks[6][:, :].bitcast(BF16); b_pZ = bbuf[6]
        pY = (banks[4], banks[5]); b_pY = (bbuf[4], bbuf[5])
        fw.dma(SP, xt[:], xq[rows, :], writes=[b_xt])
        fw.dma(SP, ozt[:], oz[rows, :], writes=[b_oz])
        fw.op(S, lambda: act.activation(out=junk[:], in_=xt[:], func=AF.Square, accum_out=ss[:, 0:1]),
              reads=[b_xt], writes=[b_junk, b_ss])
        fw.op(S, lambda: act.activation(out=ss[:, 1:2], in_=ss[:, 0:1], func=AF.Ln, scale=1.0 / DM, bias=EPS),
              reads=[b_ss], writes=[b_ss])
        fw.op(S, lambda: act.activation(out=ss[:, 2:3], in_=ss[:, 1:2], func=AF.Exp, scale=-0.5), reads=[b_ss], writes=[b_ss])
        fw.op(S, lambda: act.activation(out=xn[:], in_=xt[:], func=AF.Copy, scale=ss[:, 2:3]), reads=[b_xt, b_ss], writes=[b_xn])
        for kt in range(8):
            fw.op(T, (lambda kt=kt: pe.transpose(out=pT[:, kt * 128:(kt + 1) * 128], in_=xn[:, kt * 128:(kt + 1) * 128],
                                                 identity=ident[:])), reads=[b_xn, b_ident], writes=[b_pT], sig=(kt == 7))
        fw.op(V, lambda: vec.tensor_copy(out=xT[:].rearrange("p a b -> p (a b)"), in_=pT[:, :]), reads=[b_pT], writes=[b_xT])
        for kt in range(8):
            fw.op(T, (lambda kt=kt: pe.matmul(pG[:, :], lhsT=xT[:, kt, :], rhs=wgs[:, kt, :], start=(kt == 0), stop=(kt == 7))),
                  reads=[b_xT, b_wgs], writes=[b_pG], sig=(kt == 7))
        fw.op(S, lambda: act.activation(out=sg[:], in_=pG[:, :], func=AF.Exp, scale=-1.0), reads=[b_pG], writes=[b_sg])
        fw.op(V, lambda: vec.tensor_scalar(out=sg[:], in0=sg[:], scalar1=1.0, scalar2=None, op0=ALU.add), reads=[b_sg], writes=[b_sg])
        fw.op(V, lambda: vec.reciprocal(out=sg[:], in_=sg[:]), reads=[b_sg], writes=[b_sg])
        fw.op(V, lambda: vec.tensor_tensor(out=sg[:], in0=pG[:, :], in1=sg[:], op=ALU.mult), reads=[b_pG, b_sg], writes=[b_sg])
        fw.op(P, lambda: pool.tensor_copy(out=zb[:], in_=ozt[:, 512:1024]), reads=[b_oz], writes=[b_zb])
        for kt in range(4):
            fw.op(T, (lambda kt=kt: pe.transpose(out=pZ[:, kt * 128:(kt + 1) * 128], in_=zb[:, kt * 128:(kt + 1) * 128],
                                                 identity=ident[:])), reads=[b_zb, b_ident], writes=[b_pZ], sig=(kt == 3))
        fw.op(V, lambda: vec.tensor_copy(out=zT[:].rearrange("p a b -> p (a b)"), in_=pZ[:, 0:512]), reads=[b_pZ], writes=[b_zT])
        for kt in range(4):
            fw.op(T, (lambda kt=kt: pe.matmul(pL[:, :], lhsT=zT[:, kt, :], rhs=wglu[:, kt, :], start=(kt == 0), stop=(kt == 3))),
                  reads=[b_zT, b_wglu], writes=[b_pL], sig=(kt == 3))
        fw.op(V, lambda: vec.tensor_tensor(out=pr[:], in0=pL[:, :], in1=bglu[:], op=ALU.add), reads=[b_pL, b_bglu], writes=[b_pr])
        fw.op(S, lambda: act.activation(out=pr[:], in_=pr[:], func=AF.Exp, scale=-1.0), reads=[b_pr], writes=[b_pr])
        fw.op(V, lambda: vec.tensor_scalar(out=pr[:], in0=pr[:], scalar1=1.0, scalar2=None, op0=ALU.add), reads=[b_pr], writes=[b_pr])
        fw.op(V, lambda: vec.reciprocal(out=pr[:], in_=pr[:]), reads=[b_pr], writes=[b_pr])
        fw.op(P, lambda: pool.tensor_tensor(out=pr[:], in0=pr[:], in1=ozt[:, 512:1024], op=ALU.mult), reads=[b_pr, b_oz], writes=[b_pr])
        fw.op(P, lambda: pool.tensor_tensor(out=ob[:, 512:1024], in0=pr[:], in1=sg[:], op=ALU.mult), reads=[b_pr, b_sg], writes=[b_ob])
        fw.op(P, lambda: pool.tensor_copy(out=ob[:, 0:512], in_=ozt[:, 0:512]), reads=[b_oz], writes=[b_ob])
        for kt in range(8):
            fw.op(T, (lambda kt=kt: pe.transpose(out=pO[:, kt * 128:(kt + 1) * 128], in_=ob[:, kt * 128:(kt + 1) * 128],
                                                 identity=ident[:])), reads=[b_ob, b_ident], writes=[b_pO], sig=(kt == 7))
        fw.op(V, lambda: vec.tensor_copy(out=oT[:].rearrange("p a b -> p (a b)"), in_=pO[:, :]), reads=[b_pO], writes=[b_oT])
        for nb in range(2):
            for kt in range(8):
                fw.op(T, (lambda kt=kt, nb=nb: pe.matmul(pY[nb][:, :], lhsT=oT[:, kt, :], rhs=wout[:, kt, nb * 512:(nb + 1) * 512],
                                                         start=(kt == 0), stop=(kt == 7))),
                      reads=[b_oT, b_wout], writes=[b_pY[nb]], sig=(kt == 7))
            fw.op(V, (lambda nb=nb: vec.tensor_tensor(out=y[:, nb * 512:(nb + 1) * 512], in0=pY[nb][:, :],
                                                      in1=xt[:, nb * 512:(nb + 1) * 512], op=ALU.add)),
                  reads=[b_pY[nb], b_xt], writes=[b_y])
        fw.dma(P, yq[rows, :], y[:], reads=[b_y], writes=[b_yq])
    fw.barrier()
    return nc


def _onehot():
    rel = np.arange(640) - 255
    n = np.maximum(rel, 0)
    nf = np.maximum(n, 1).astype(np.float32)
    large = 16 + (np.log(nf / np.float32(16)) / np.float32(np.log(128 / 16)) * np.float32(16)).astype(np.int32)
    large = np.minimum(large, 31)
    bucket = np.where(n < 16, n, large)
    oh = np.zeros((33, 640), np.float32)
    for m in range(640):
        if rel[m] < 0:
            oh[32, m] = 1.0
        else:
            oh[bucket[m], m] = 1.0
    return oh


def _prep_core(c, inp):
    b, h = c // 4, c % 4
    w_in = inp["w_in"][0]
    cols = np.concatenate([np.arange(h * 128, (h + 1) * 128) + off for off in (0, 512, 1024, 1536, 2048)])
    wa = w_in[:, cols]
    wa = np.ascontiguousarray(wa.reshape(8, 128, 640).transpose(1, 0, 2))
    ng = np.ascontiguousarray(inp["norm_g"][0].reshape(8, 128).T)
    qg, kg = inp["q_norm_g"][0], inp["k_norm_g"][0]
    gqk = np.concatenate([qg, qg, kg, kg]).astype(np.float32)
    lamv = np.concatenate([inp["lambda_q1"][0], inp["lambda_k1"][0], inp["lambda_q2"][0], inp["lambda_k2"][0]]).astype(np.float32)
    rb = inp["rel_bias"][:, h].astype(np.float32)
    rbx = np.ascontiguousarray(np.repeat(np.concatenate([rb, np.array([-30000.0], np.float32)]).reshape(33, 1), 128, axis=1))
    gs_ = np.arange(8 * h, 8 * h + 8)
    def pairlay(a):
        return np.ascontiguousarray(a.reshape(4, 2, 64).transpose(1, 2, 0).reshape(128, 4))
    are = pairlay(inp["ssm_a_re"][0][gs_]); aim = pairlay(inp["ssm_a_im"][0][gs_])
    ldt = pairlay(np.repeat(inp["ssm_log_dt"][0][gs_][:, None], 64, axis=1))
    s5a = np.concatenate([are, aim, ldt], axis=1).astype(np.float32)
    def blay(bm):
        return bm.reshape(4, 2, 64, 16).transpose(1, 2, 0, 3).reshape(128, 4, 16)
    s5b = np.ascontiguousarray(np.stack([blay(inp["ssm_b_re"][0][gs_]), blay(inp["ssm_b_im"][0][gs_])], axis=1)).astype(np.float32)
    def clay(cm):
        o = np.zeros((2, 64, 4, 2, 16), np.float32)
        cm4 = cm.reshape(4, 2, 16, 64)
        for gi in range(2):
            o[gi, :, :, gi, :] = cm4[:, gi].transpose(2, 0, 1)
        return o.reshape(128, 4, 32)
    s5c = np.ascontiguousarray(np.stack([clay(inp["ssm_c_re"][0][gs_]), clay(inp["ssm_c_im"][0][gs_])], axis=1))
    w_in0 = inp["w_in"][0]
    def klay2(w):
        return w.reshape(8, 128, w.shape[1]).transpose(1, 0, 2)
    wsm = np.ascontiguousarray(np.stack([klay2(w_in0[:, i * 512:(i + 1) * 512]) for i in range(5)], axis=1)).astype(np.float32)
    A_re, A_im, LDT = inp["ssm_a_re"][0], inp["ssm_a_im"][0], inp["ssm_log_dt"][0]
    def pl16(a):
        return a.reshape(16, 2, 64).transpose(1, 2, 0).reshape(128, 16)
    s5a16 = np.ascontiguousarray(np.concatenate([pl16(A_re), pl16(A_im), pl16(np.repeat(LDT[:, None], 64, axis=1))], axis=1)).astype(np.float32)
    def bl16(bm):
        return bm.reshape(16, 2, 64, 16).transpose(1, 2, 0, 3).reshape(128, 16, 16)
    s5b16 = np.ascontiguousarray(np.stack([bl16(inp["ssm_b_re"][0]), bl16(inp["ssm_b_im"][0])], axis=1)).astype(np.float32)
    def cl16(cm):
        o = np.zeros((2, 64, 16, 2, 16), np.float32)
        cm4 = cm.reshape(16, 2, 16, 64)
        for gi in range(2):
            o[gi, :, :, gi, :] = cm4[:, gi].transpose(2, 0, 1)
        return o.reshape(128, 16, 32)
    s5c16 = np.ascontiguousarray(np.stack([cl16(inp["ssm_c_re"][0]), cl16(inp["ssm_c_im"][0])], axis=1))
    def hl(st):
        return st.reshape(16, 16, 2, 64).transpose(2, 3, 1, 0).reshape(128, 16, 16)
    hst = np.ascontiguousarray(np.stack([hl(inp["state_ssm_re"][0][16 * c:16 * c + 16]),
                                         hl(inp["state_ssm_im"][0][16 * c:16 * c + 16])], axis=1)).astype(np.float32)
    RB = inp["rel_bias"].astype(np.float32)
    def bkt(n):
        n = np.asarray(n)
        nf = np.maximum(n, 1).astype(np.float32)
        large = 16 + (np.log(nf / np.float32(16)) / np.float32(np.log(128 / 16)) * np.float32(16)).astype(np.int32)
        return np.where(n < 16, n, np.minimum(large, 31))
    oh15 = np.zeros((32, 4, 128), np.float32)
    for qi in range(4):
        bk_ = bkt(128 + qi - np.arange(128))
        oh15[bk_, qi, np.arange(128)] = 1.0
    ohn = np.zeros((33, 4, 4), np.float32)
    for qi in range(4):
        for kj in range(4):
            if kj <= qi:
                ohn[qi - kj, qi, kj] = 1.0
            else:
                ohn[32, qi, kj] = 1.0
    rb33 = np.concatenate([RB, np.full((1, 4), -30000.0, np.float32)], axis=0)
    dm = np.zeros((8, 8), np.float32)
    for qi in range(4):
        dm[qi, qi] = 1.0
        dm[4 + qi, 4 + qi] = 1.0
    jq = c % 4
    xq = np.zeros((NTB * 128, DM), np.float32)
    xq[:2048] = inp["x_prompt"][b, 2048 * jq:2048 * (jq + 1)]
    xq[2048:2048 + 64] = inp["x_sample"].reshape(512, DM)[64 * c:64 * (c + 1)]
    idxb = np.zeros((128, 64), np.int32)
    for t in range(16):
        for r in range(4):
            gtok = 2048 * jq + 128 * t + np.arange(128)
            idxb[:, t * 4 + r] = (gtok // 1024) * 4096 + r * 1024 + (gtok % 1024)
    wo = inp["w_out"][0]
    return {
        "xq": xq, "idxb": idxb,
        "wgs": np.ascontiguousarray(klay2(w_in0[:, 2560:3072])).astype(np.float32),
        "wglu": np.ascontiguousarray(inp["w_glu"][0].reshape(4, 128, 512).transpose(1, 0, 2)).astype(np.float32),
        "wout": np.ascontiguousarray(np.stack([klay2(wo[:, 0:512]), klay2(wo[:, 512:1024])], axis=1)).astype(np.float32),
        "bglu": inp["b_glu"][0].astype(np.float32),
        "xs": np.ascontiguousarray(inp["x_sample"].reshape(512, DM)[64 * c:64 * (c + 1)]),
        "wsm": wsm, "g16": np.concatenate([np.tile(inp["q_norm_g"][0], 8), np.tile(inp["k_norm_g"][0], 8)]).astype(np.float32),
        "s5a16": s5a16, "s5b16": s5b16, "s5c16": s5c16, "hst": hst, "dsk16": inp["ssm_d"][0].astype(np.float32),
        "pt": np.ascontiguousarray(inp["page_table"][16 * c:16 * c + 16].reshape(256)).astype(np.int32),
        "pcol": np.arange(128, dtype=np.float32).reshape(128, 1),
        "rbfar": np.repeat(RB[31], 8).astype(np.float32), "oh15": oh15, "ohn": ohn, "rb33": rb33, "dm": dm,
        "sg4": np.tile(inp["subln_g"][0], 4).astype(np.float32),
        "ck": inp["cache_k"][0].reshape(-1, 512), "cv": inp["cache_v"][0].reshape(-1, 512),
        "s5a": s5a, "s5b": s5b, "s5c": s5c, "dsk": np.ascontiguousarray(inp["ssm_d"][0][128 * h:128 * h + 128]),
        "lamv": lamv, "sg": inp["subln_g"][0].astype(np.float32), "rb31": rb[31:32].copy(),
        "rbx": rbx, "oh": _onehot(),
        "xp": np.ascontiguousarray(inp["x_prompt"][b]),
        "wa": wa, "ng": ng, "ident": np.eye(128, dtype=np.float32), "gqk": gqk,
    }


_NC = None
_NCB = None
_OZS = None


def kernel(**inp):
    global _NC
    inp = {k: np.asarray(v) for k, v in inp.items()}
    if _NC is None:
        _NC = build(int(inp["cache_k"].shape[1]))
    nc = _NC
    in_maps = [_prep_core(c, inp) for c in range(8)]
    res = run_bass_kernel_spmd(nc, in_maps, core_ids=list(range(8)))
    R = res.results
    B_, DB, DS_ = 2, 128, 4
    y_prompt = np.zeros((B_, SEQ, DM), np.float32)
    y_sample = np.zeros((DB, DS_, DM), np.float32)
    k_prompt = np.zeros((1, B_, SEQ, 4, 128), np.float32)
    v_prompt = np.zeros((1, B_, SEQ, 4, 128), np.float32)
    k_sample = np.zeros((1, DB, DS_, 4, 128), np.float32)
    v_sample = np.zeros((1, DB, DS_, 4, 128), np.float32)
    srp = np.zeros((1, B_, 32, 64), np.float32)
    sip = np.zeros((1, B_, 32, 64), np.float32)
    srs = np.zeros((1, DB, 32, 64), np.float32)
    sis = np.zeros((1, DB, 32, 64), np.float32)
    for c in range(8):
        b, h = c // 4, c % 4
        k_prompt[0, b, :, h, :] = R[c]["kp"]
        v_prompt[0, b, :, h, :] = R[c]["vp"]
        k_sample[0, 16 * c:16 * (c + 1)] = R[c]["ks"].reshape(16, 4, 4, 128)
        v_sample[0, 16 * c:16 * (c + 1)] = R[c]["vs"].reshape(16, 4, 4, 128)
        st = R[c]["sss"].reshape(2, 64, 2, 16, 16)
        for ri, dst in ((0, srs), (1, sis)):
            dst[0, 16 * c:16 * c + 16] = st[:, :, ri].transpose(3, 2, 0, 1).reshape(16, 32, 64)
        sf = R[c]["sfin"].reshape(2, 64, 2, 4)
        for pr in range(4):
            for gi in range(2):
                srp[0, b, 8 * h + 2 * pr + gi, :] = sf[gi, :, 0, pr]
                sip[0, b, 8 * h + 2 * pr + gi, :] = sf[gi, :, 1, pr]
    for c in range(8):
        b, j = c // 4, c % 4
        yq = R[c]["yq"]
        y_prompt[b, 2048 * j:2048 * (j + 1)] = yq[:2048]
        y_sample.reshape(DB * DS_, DM)[64 * c:64 * (c + 1)] = yq[2048:2048 + 64]
    return (y_prompt, y_sample, k_prompt, v_prompt, k_sample, v_sample, srp, sip, srs, sis)
```
